# Optimizing a Trainium2 kernel written in Bass

```python
import math
import jax, jax.numpy as jnp
from jax import lax
import numpy as np

D_MODEL = 1024
BATCH = 32
SEQ = 256
DEPTH = 4
DEC_BATCH = 8
DEC_SEQ = 1024
PAST_LEN = 512

GRID_W = 64
N_PAIRS = DEPTH // 2
EPS = 1e-6
A_WIDTH = D_MODEL // 2
A_HEAD = 64
A_HEADS = A_WIDTH // A_HEAD
LORA_W = 64
LORA_A = 64
GN_EPS = 64e-5
B_WIDTH = D_MODEL // 2
CONV_W = 3
C_HEAD = 64
C_Q_HEADS = D_MODEL // C_HEAD
C_KV_HEADS = 4
C_GROUP = C_Q_HEADS // C_KV_HEADS
C_WIDTH = C_Q_HEADS * C_HEAD
KV_WIDTH = C_KV_HEADS * C_HEAD
Q_BLOCK = 128
ROPE_THETA = 10000.0
ROPE_HALF = C_HEAD // 2
A_SHIFT = 3 * A_WIDTH + 2 * LORA_W + 2 * LORA_A
EVEN_IN = A_SHIFT + A_WIDTH + 4 * B_WIDTH
ODD_IN = C_WIDTH + 2 * KV_WIDTH + C_WIDTH

kernel_name = 'hybrid_rwkv7_shortconv_gqa_diffusion_step'

f32 = jnp.float32


def rmsnorm(x, g):
    xf = x.astype(f32)
    y = xf * lax.rsqrt(jnp.mean(xf * xf, axis=-1, keepdims=True) + EPS)
    return (y * g.astype(f32)).astype(x.dtype)


def adaln(cond, w, b):
    return jax.nn.silu(cond) @ w + b


def modulate(x, g, mod):
    shift, scale, gate = jnp.split(mod, 3, axis=-1)
    h = rmsnorm(x, g) * (1 + scale[:, None]) + shift[:, None]
    return h, gate[:, None]


def _neighbours(f):
    zero = jnp.zeros_like(f[:, :1])
    prev = jnp.concatenate([zero, f[:, :-1]], axis=1)
    nxt = jnp.concatenate([f[:, 1:], zero], axis=1)
    return prev, nxt


def centred_shift(f, mu):
    prev, nxt = _neighbours(f)
    return f + mu * (0.5 * (prev + nxt) - f)


def centred_conv3(u, w, b):
    prev, nxt = _neighbours(u)
    return prev * w[0] + u * w[1] + nxt * w[2] + b


def rwkv_scan(S0, r, w, k, v, kk, a, reverse):
    def step(S, inp):
        r_t, w_t, k_t, v_t, kk_t, a_t = inp
        sk = jnp.einsum('bhvk,bhk->bhv', S, kk_t)
        S = (S * w_t[:, :, None, :] - sk[..., None] * (kk_t * a_t)[:, :, None, :]
             + v_t[..., None] * k_t[:, :, None, :])
        return S, jnp.einsum('bhvk,bhk->bhv', S, r_t)
    xs = tuple(jnp.moveaxis(t, 1, 0) for t in (r, w, k, v, kk, a))
    S, ys = lax.scan(step, S0, xs, reverse=reverse)
    return S, jnp.moveaxis(ys, 0, 1)


def rwkv_conv_mixer(h, S0, w_in, mu, lw2, w0, la2, a0, k_k, k_a, r_k, lnx_g, lnx_b,
                    conv_w, conv_b, w_out):
    B, L, _ = h.shape
    proj = h @ w_in
    feat = centred_shift(proj[..., :A_SHIFT], mu)
    r = feat[..., :A_WIDTH]
    k = feat[..., A_WIDTH:2 * A_WIDTH]
    v = feat[..., 2 * A_WIDTH:3 * A_WIDTH]
    o = 3 * A_WIDTH
    lw = feat[..., o:o + 2 * LORA_W].reshape(B, L, 2, LORA_W)
    o += 2 * LORA_W
    la = feat[..., o:o + 2 * LORA_A].reshape(B, L, 2, LORA_A)
    o = A_SHIFT
    g_a = proj[..., o:o + A_WIDTH]
    o += A_WIDTH
    bg = proj[..., o:o + B_WIDTH]
    cg = proj[..., o + B_WIDTH:o + 2 * B_WIDTH]
    xin = proj[..., o + 2 * B_WIDTH:o + 3 * B_WIDTH]
    g_b = proj[..., o + 3 * B_WIDTH:o + 4 * B_WIDTH]

    w_log = -jax.nn.softplus(-(w0 + jnp.einsum('bldr,drc->bldc', jnp.tanh(lw), lw2))) - 0.5
    decay = jnp.exp(-jnp.exp(w_log.astype(f32)))
    a = jax.nn.sigmoid(a0 + jnp.einsum('bldr,drc->bldc', la, la2))
    k_dir = k[:, :, None, :] * (1 + (a - 1) * k_a)

    heads = lambda t: t.reshape(t.shape[:-1] + (A_HEADS, A_HEAD)).astype(f32)
    r_h, v_h = heads(r), heads(v)
    kk = heads(k * k_k)
    kk = kk / jnp.maximum(jnp.sqrt(jnp.sum(kk * kk, axis=-1, keepdims=True)), 1e-12)
    w_h, a_h, k_h = heads(decay), heads(a), heads(k_dir)
    S0 = S0.astype(f32)
    S_f, y_f = rwkv_scan(S0[:, 0], r_h, w_h[:, :, 0], k_h[:, :, 0], v_h, kk, a_h[:, :, 0], False)
    S_b, y_b = rwkv_scan(S0[:, 1], r_h, w_h[:, :, 1], k_h[:, :, 1], v_h, kk, a_h[:, :, 1], True)
    y = y_f + y_b
    mean = jnp.mean(y, axis=-1, keepdims=True)
    var = jnp.mean(jnp.square(y - mean), axis=-1, keepdims=True)
    y = ((y - mean) * lax.rsqrt(var + GN_EPS)).reshape(B, L, A_WIDTH) * lnx_g + lnx_b
    bonus = jnp.sum(r_h * jnp.mean(k_h, axis=2) * r_k, axis=-1, keepdims=True) * v_h
    out_a = (y + bonus.reshape(B, L, A_WIDTH)).astype(h.dtype) * jax.nn.silu(g_a)

    out_b = bg * centred_conv3(cg * xin, conv_w, conv_b) * jax.nn.silu(g_b)
    out = jnp.concatenate([out_a, out_b], axis=-1) @ w_out
    return out, jnp.stack([S_f, S_b], axis=1)


def gqa_project(h, w_in, q_g, k_g):
    B, L, _ = h.shape
    proj = h @ w_in
    q = proj[..., :C_WIDTH].reshape(B, L, C_Q_HEADS, C_HEAD)
    k = proj[..., C_WIDTH:C_WIDTH + KV_WIDTH].reshape(B, L, C_KV_HEADS, C_HEAD)
    v = proj[..., C_WIDTH + KV_WIDTH:C_WIDTH + 2 * KV_WIDTH].reshape(B, L, C_KV_HEADS, C_HEAD)
    g = proj[..., C_WIDTH + 2 * KV_WIDTH:]
    return rmsnorm(q, q_g), rmsnorm(k, k_g), v, g


def axial_rope_tables(L):
    rows = L // GRID_W
    row = jnp.repeat(jnp.arange(rows, dtype=f32), GRID_W)
    col = (jnp.arange(L) % GRID_W).astype(f32)
    inv = ROPE_THETA ** (-jnp.arange(0, ROPE_HALF, 2, dtype=f32) / ROPE_HALF)
    ang_r = row[:, None, None] * inv
    ang_c = col[:, None, None] * inv
    return jnp.cos(ang_r), jnp.sin(ang_r), jnp.cos(ang_c), jnp.sin(ang_c)


def _rope_half(x, cos, sin):
    x1, x2 = jnp.split(x, 2, axis=-1)
    return jnp.concatenate([x1 * cos - x2 * sin, x2 * cos + x1 * sin], axis=-1)


def apply_axial_rope(x, tables):
    cos_r, sin_r, cos_c, sin_c = tables
    xf = x.astype(f32)
    out = jnp.concatenate([_rope_half(xf[..., :ROPE_HALF], cos_r, sin_r),
                           _rope_half(xf[..., ROPE_HALF:], cos_c, sin_c)], axis=-1)
    return out.astype(x.dtype)


def block_attention(q, k, v):
    B, L = q.shape[:2]
    nb = L // Q_BLOCK
    qb = q.reshape(B, nb, Q_BLOCK, C_KV_HEADS, C_GROUP, C_HEAD).swapaxes(0, 1)
    scale = C_HEAD ** -0.5

    def one(qblk):
        s = jnp.einsum('bqhgd,bhkd->bhgqk', qblk, k).astype(f32) * scale
        p = jax.nn.softmax(s, axis=-1).astype(v.dtype)
        return jnp.einsum('bhgqk,bhkd->bqhgd', p, v)

    out = lax.map(one, qb)
    return out.swapaxes(0, 1).reshape(B, L, C_WIDTH)


def setup_inputs(seed: int = 0) -> dict:
    key = jax.random.key(seed)
    ks = iter(jax.random.split(key, 40))
    nrm = lambda shape, s=1.0: s * jax.random.normal(next(ks), shape, f32)
    P = N_PAIRS
    return {
        'x_prompt': nrm((BATCH, SEQ, D_MODEL)),
        'x_sample': nrm((DEC_BATCH, DEC_SEQ, D_MODEL)),
        'c': nrm((DEC_BATCH, D_MODEL)),
        'state_rwkv': nrm((DEC_BATCH, P, 2, A_HEADS, A_HEAD, A_HEAD), 0.3),
        'cache_k': nrm((DEC_BATCH, P, C_KV_HEADS, PAST_LEN, C_HEAD)),
        'cache_v': nrm((DEC_BATCH, P, C_KV_HEADS, PAST_LEN, C_HEAD)),
        'c_ctx': nrm((D_MODEL,)),
        'w_ada': nrm((DEPTH, D_MODEL, 3 * D_MODEL), 0.5 * D_MODEL ** -0.5),
        'b_ada': nrm((DEPTH, 3 * D_MODEL), 0.02),
        'norm_g': 1.0 + nrm((DEPTH, D_MODEL), 0.02),
        'final_g': 1.0 + nrm((D_MODEL,), 0.02),
        'w_in_e': nrm((P, D_MODEL, EVEN_IN), D_MODEL ** -0.5),
        'mu_shift': jax.random.uniform(next(ks), (P, A_SHIFT), f32),
        'lora_w2': nrm((P, 2, LORA_W, A_WIDTH), 0.1 * LORA_W ** -0.5),
        'w0': jax.random.uniform(next(ks), (P, 2, A_WIDTH), f32, -6.0, 1.0),
        'lora_a2': nrm((P, 2, LORA_A, A_WIDTH), 0.1 * LORA_A ** -0.5),
        'a0': nrm((P, 2, A_WIDTH), 0.1),
        'k_k': 0.85 + nrm((P, A_WIDTH), 0.02),
        'k_a': 1.0 + nrm((P, A_WIDTH), 0.02),
        'r_k': nrm((P, A_HEADS, A_HEAD), 0.1),
        'lnx_g': 1.0 + nrm((P, A_WIDTH), 0.02),
        'lnx_b': nrm((P, A_WIDTH), 0.01),
        'conv_w': nrm((P, CONV_W, B_WIDTH), CONV_W ** -0.5),
        'conv_b': nrm((P, B_WIDTH), 0.01),
        'w_out_e': nrm((P, A_WIDTH + B_WIDTH, D_MODEL), (A_WIDTH + B_WIDTH) ** -0.5),
        'w_in_o': nrm((P, D_MODEL, ODD_IN), D_MODEL ** -0.5),
        'q_norm_g': 1.0 + nrm((P, C_HEAD), 0.02),
        'k_norm_g': 1.0 + nrm((P, C_HEAD), 0.02),
        'w_out_o': nrm((P, C_WIDTH, D_MODEL), C_WIDTH ** -0.5),
    }


def reference(x_prompt, x_sample, c, state_rwkv, cache_k, cache_v, c_ctx, w_ada, b_ada,
              norm_g, final_g, w_in_e, mu_shift, lora_w2, w0, lora_a2, a0, k_k, k_a, r_k,
              lnx_g, lnx_b, conv_w, conv_b, w_out_e, w_in_o, q_norm_g, k_norm_g, w_out_o):
    xp, xs = x_prompt, x_sample
    rope = axial_rope_tables(xs.shape[1])
    zero_state = jnp.zeros((xp.shape[0], 2, A_HEADS, A_HEAD, A_HEAD), f32)
    new_rwkv, new_k, new_v = [], [], []
    for layer in range(DEPTH):
        p = layer // 2
        mod_p = adaln(c_ctx[None, :], w_ada[layer], b_ada[layer])
        mod_s = adaln(c, w_ada[layer], b_ada[layer])
        hp, gate_p = modulate(xp, norm_g[layer], mod_p)
        hs, gate_s = modulate(xs, norm_g[layer], mod_s)
        if layer % 2 == 0:
            ew = (w_in_e[p], mu_shift[p], lora_w2[p], w0[p], lora_a2[p], a0[p], k_k[p],
                  k_a[p], r_k[p], lnx_g[p], lnx_b[p], conv_w[p], conv_b[p], w_out_e[p])
            out_p, st_p = rwkv_conv_mixer(hp, zero_state, *ew)
            out_s, _ = rwkv_conv_mixer(hs, state_rwkv[:, p], *ew)
            new_rwkv.append(st_p)
        else:
            qp, kp, vp, gp = gqa_project(hp, w_in_o[p], q_norm_g[p], k_norm_g[p])
            kp_t, vp_t = kp.transpose(0, 2, 1, 3), vp.transpose(0, 2, 1, 3)
            out_p = (block_attention(qp, kp_t, vp_t) * jax.nn.silu(gp)) @ w_out_o[p]
            qs, ks_, vs, gs = gqa_project(hs, w_in_o[p], q_norm_g[p], k_norm_g[p])
            qs = apply_axial_rope(qs, rope)
            ks_ = apply_axial_rope(ks_, rope)
            keys = jnp.concatenate([cache_k[:, p], ks_.transpose(0, 2, 1, 3)], axis=2)
            vals = jnp.concatenate([cache_v[:, p], vs.transpose(0, 2, 1, 3)], axis=2)
            out_s = (block_attention(qs, keys, vals) * jax.nn.silu(gs)) @ w_out_o[p]
            new_k.append(kp_t)
            new_v.append(vp_t)
        xp = xp + gate_p * out_p
        xs = xs + gate_s * out_s
    y_prompt = rmsnorm(xp, final_g)
    y_sample = rmsnorm(xs, final_g)
    new_state_rwkv = jnp.stack(new_rwkv, axis=1)
    new_cache_k = jnp.stack(new_k, axis=1)
    new_cache_v = jnp.stack(new_v, axis=1)
    return (y_prompt, y_sample, new_state_rwkv, new_cache_k, new_cache_v)
```

```python
import math
import os
import numpy as np
import concourse.bass as bass
import concourse.mybir as mybir
from concourse.bass_utils import run_bass_kernel_spmd

F32 = mybir.dt.float32
BF16 = mybir.dt.bfloat16
ALU = mybir.AluOpType
AF = mybir.ActivationFunctionType

D = 1024
NT = 1024
KC = 8
CH = 64
NCH = NT // CH
DEPTH = 4
EPS = 1e-6
GN_EPS = 64e-5
C0 = math.exp(-0.5)
EVEN_IN = 4352
ODD_IN = 2560
A_SHIFT = 1792

_off = {}
_n = 0


def _reg(name, n):
    global _n
    _off[name] = (_n, n)
    _n += n


for _l in range(DEPTH):
    _reg(f"bada{_l}", 24)
    _reg(f"normg{_l}", 8)
_reg("finalg", 8)
for _p in range(2):
    _reg(f"mu{_p}", 14)
    _reg(f"w0{_p}", 8)
    _reg(f"a0{_p}", 8)
    _reg(f"kk{_p}", 4)
    _reg(f"ka{_p}", 4)
    _reg(f"rk{_p}", 4)
    _reg(f"lng{_p}", 4)
    _reg(f"lnb{_p}", 4)
    _reg(f"cw{_p}", 12)
    _reg(f"cb{_p}", 4)
    _reg(f"qg{_p}", 1)
    _reg(f"kg{_p}", 1)
NPAR = _n
NCF = 128 + 512 + 2048
NCB = 4736


class _Stop(Exception):
    pass


def ck(n):
    if os.environ.get('K_STOP') == str(n):
        raise _Stop()


class Region:
    __slots__ = ("name", "w", "r", "dsem", "dcnt")

    def __init__(self, name):
        self.name = name
        self.w = None
        self.r = {}
        self.dsem = None
        self.dcnt = 0


class Sched:
    def __init__(self, nc, stack):
        self.nc = nc
        self.eng = {"pe": nc.tensor, "act": nc.scalar, "dve": nc.vector, "pool": nc.gpsimd, "sp": nc.sync}
        self.sem = {k: stack.enter_context(nc.semaphore("s_" + k)) for k in self.eng}
        self.cnt = {k: 0 for k in self.eng}
        self.seen = {k: {} for k in self.eng}
        self.stack = stack
        self.dsems = []
        self.ninstr = 0
        self.capture = None

    def record(self, fn):
        assert self.capture is None
        self.capture = []
        fn()
        q = self.capture
        self.capture = None
        return q

    def _emit(self, it):
        if it[0] == "op":
            self.op(*it[1:])
        else:
            self.dma(*it[1:])

    def replay(self, qa, qb=()):
        nb = 0
        for i, it in enumerate(qa):
            self._emit(it)
            want = ((i + 1) * len(qb)) // max(1, len(qa))
            while nb < want:
                self._emit(qb[nb])
                nb += 1
        while nb < len(qb):
            self._emit(qb[nb])
            nb += 1

    def region(self, name, dma=False):
        r = Region(name)
        if dma:
            r.dsem = self.stack.enter_context(self.nc.semaphore("d_" + name))
        return r

    def _wait(self, e, sem, val):
        if e == "pe" and sem is self.sem["pe"]:
            return
        key = sem.name
        for f_, s_ in self.sem.items():
            if s_ is sem:
                assert val <= self.cnt[f_], ("wait on pending (non-incrementing) op", e, f_, val, self.cnt[f_])
        if self.seen[e].get(key, 0) >= val:
            return
        self.seen[e][key] = val
        self.eng[e].wait_ge(sem, val)

    def _deps(self, e, reads, writes):
        for r in reads:
            if r.w is not None:
                self._wait(e, self.sem[r.w[0]], r.w[1])
            if r.dsem is not None and r.dcnt:
                self._wait(e, r.dsem, 16 * r.dcnt)
        for r in writes:
            if r.w is not None:
                self._wait(e, self.sem[r.w[0]], r.w[1])
            for k, c in r.r.items():
                self._wait(e, self.sem[k], c)
            if r.dsem is not None and r.dcnt:
                self._wait(e, r.dsem, 16 * r.dcnt)

    def op(self, e, ins_fn, reads=(), writes=(), inc=True):
        if self.capture is not None:
            self.capture.append(("op", e, ins_fn, tuple(reads), tuple(writes), inc))
            return
        self._deps(e, reads, writes)
        ins = ins_fn(self.eng[e])
        if inc:
            self.cnt[e] += 1
            c = self.cnt[e]
            ins.then_inc(self.sem[e], 1)
        else:
            assert e == "pe"
            c = self.cnt[e] + 1
        self.seen[e][self.sem[e].name] = max(self.seen[e].get(self.sem[e].name, 0), 0)
        for r in reads:
            r.r[e] = c
        for r in writes:
            r.w = (e, c)
            r.r = {}
        self.ninstr += 1

    def dma(self, q, out, in_, sb_region, sb_is_dst, extra_reads=()):
        r = sb_region
        if self.capture is not None:
            self.capture.append(("dma", q, out, in_, sb_region, sb_is_dst, tuple(extra_reads)))
            return
        if sb_is_dst:
            self._deps(q, extra_reads, [r])
        else:
            self._deps(q, [r] + list(extra_reads), [])
        ins = self.eng[q].dma_start(out=out, in_=in_)
        r.dcnt += 1
        ins.then_inc(r.dsem, 16)
        if sb_is_dst:
            r.w = None
            r.r = {}
        self.ninstr += 1

    def barrier(self):
        for e in self.eng:
            for f in self.eng:
                if f != e and self.cnt[f]:
                    self._wait(e, self.sem[f], self.cnt[f])

    def finish(self, regions):
        for r in regions:
            if r.dsem is not None and r.dcnt:
                self._wait("sp", r.dsem, 16 * r.dcnt)
        for f in self.eng:
            if f != "sp" and self.cnt[f]:
                self._wait("sp", self.sem[f], self.cnt[f])


class T:
    def __init__(self, S, stack, name, shape, dt, psum=False, dma=False):
        nc = S.nc
        self.t = stack.enter_context(nc.psum_tensor(name, shape, dt) if psum else nc.sbuf_tensor(name, shape, dt))
        self.r = S.region(name, dma=dma)

    def __getitem__(self, k):
        return self.t[k]


class View:
    def __init__(self, ap, r):
        self.ap = ap
        self.r = r

    def __getitem__(self, k):
        return self.ap[k]


def v3(ap, q):
    return ap.rearrange("p (q s) -> p q s", q=q)


def build_program(dbg=False):
    from contextlib import ExitStack
    nc = bass.Bass("TRN2", target_bir_lowering=False)
    dt = nc.dram_tensor
    xp_d = dt("x_prompt", [4 * 256, D], F32, kind="ExternalInput").ap()
    xs_d = dt("x_sample", [NT, D], F32, kind="ExternalInput").ap()
    cvec_d = dt("cvec", [128, 16], F32, kind="ExternalInput").ap()
    st_d = dt("state", [2, 2, 8, 64, 64], F32, kind="ExternalInput").ap()
    ck_d = dt("cache_k", [2, 4, 512, 64], F32, kind="ExternalInput").ap()
    cv_d = dt("cache_v", [2, 4, 512, 64], F32, kind="ExternalInput").ap()
    par_d = dt("params", [128, NPAR], F32, kind="ExternalInput").ap()
    wada_d = dt("w_ada", [DEPTH, D, 3 * D], F32, kind="ExternalInput").ap()
    wine_d = dt("w_in_e", [2, D, EVEN_IN], F32, kind="ExternalInput").ap()
    woute_d = dt("w_out_e", [2, D, D], F32, kind="ExternalInput").ap()
    wino_d = dt("w_in_o", [2, D, ODD_IN], F32, kind="ExternalInput").ap()
    wouto_d = dt("w_out_o", [2, D, D], F32, kind="ExternalInput").ap()
    lw2_d = dt("lora_w2", [2, 128, 512], F32, kind="ExternalInput").ap()
    la2_d = dt("lora_a2", [2, 128, 512], F32, kind="ExternalInput").ap()
    cf_d = dt("consts_f", [128, NCF], F32, kind="ExternalInput").ap()
    cb_d = dt("consts_b", [128, NCB], F32, kind="ExternalInput").ap()
    yp_d = dt("y_prompt", [4 * 256, D], F32, kind="ExternalOutput").ap()
    ys_d = dt("y_sample", [NT, D], F32, kind="ExternalOutput").ap()
    ns_d = dt("new_state", [4, 2, 2, 8, 64, 64], F32, kind="ExternalOutput").ap()
    nk_d = dt("new_k", [4, 2, 4, 256, 64], F32, kind="ExternalOutput").ap()
    nv_d = dt("new_v", [4, 2, 4, 256, 64], F32, kind="ExternalOutput").ap()
    dbg_d = dt("dbg", [DEPTH, 2, 128, KC * NT], F32, kind="ExternalOutput").ap() if dbg else None

    with ExitStack() as stack:
        S = Sched(nc, stack)

        def mk(name, shape, dtp, psum=False, dma=False):
            return T(S, stack, name, shape, dtp, psum=psum, dma=dma)

        XT = mk("XT", [128, KC, NT], F32, dma=True)
        HT = mk("HT", [128, KC, NT], BF16)
        YT = mk("YT", [128, KC, NT], BF16)
        NSLOT = 2
        WS = [mk(f"WS{i}", [128, KC, 512], BF16, dma=True) for i in range(NSLOT)]
        PAR = mk("PAR", [128, NPAR], F32, dma=True)
        CV = mk("CV", [128, 16], F32, dma=True)
        CF = mk("CF", [128, NCF], F32, dma=True)
        CB = mk("CB", [128, NCB], BF16)
        MOD = mk("MOD", [128, DEPTH * 2 * 24], F32)
        GSC = mk("GSC", [128, 8], F32)
        DER = mk("DER", [128, 64], F32)
        PS = [mk(f"PS{i}", [128, 512], F32, psum=True) for i in range(8)]
        NX = 12
        X = [mk(f"X{i}", [128, NT], F32, dma=True) for i in range(NX)]
        RSTD = X[9]
        NB = 23
        B = [mk(f"B{i}", [128, NT], BF16) for i in range(NB)]
        SMALL = mk("SMALL", [128, 256], F32)
        TST = mk("TST", [128, 512], F32, dma=True)
        allr = []

        ident = CF[:, 0:128]
        IDN8 = CF[:, 128:640]
        COS = CF[:, 640:1664]
        SIN = CF[:, 1664:2688]
        ONES = CB[:, 0:128]
        BD1 = CB[:, 128:256]
        BDM = CB[:, 256:384]
        ROTT = CB[:, 384:512]
        IDB = CB[:, 512:640]
        MSL = CB[:, 640:1152]
        MSU = CB[:, 1152:1664]
        MIL = CB[:, 1664:2176]
        MIU = CB[:, 2176:2688]
        RESF = CB[:, 2688:3712]
        RESB = CB[:, 3712:4736]

        def par(name, j=0, n=1):
            o0, _ = _off[name]
            return PAR[:, o0 + j:o0 + j + n]

        def mm(out_ap, lhsT, rhs, start, stop, rd, wr, tp=None, inc=None):
            if inc is None:
                inc = bool(stop)
            if tp is None:
                S.op("pe", lambda e: e.matmul(out_ap, lhsT=lhsT, rhs=rhs, start=start, stop=stop), rd, wr, inc)
            else:
                S.op("pe", lambda e: e.matmul(out_ap, lhsT=lhsT, rhs=rhs, start=start, stop=stop,
                                              tile_position=tp), rd, wr, inc)

        def tr(out_ap, in_ap, idn, rd, wr, tp=None):
            if tp is None:
                S.op("pe", lambda e: e.transpose(out=out_ap, in_=in_ap, identity=idn), rd, wr)
            else:
                S.op("pe", lambda e: e.transpose(out=out_ap, in_=in_ap, identity=idn, tile_position=tp), rd, wr)

        def act(out_ap, in_ap, func, rd, wr, bias=0.0, scale=1.0):
            S.op("act", lambda e: e.activation(out=out_ap, in_=in_ap, func=func, bias=bias, scale=scale), rd, wr)

        def tt(out_ap, a, b, op, rd, wr, e="dve"):
            S.op(e, lambda en: en.tensor_tensor(out=out_ap, in0=a, in1=b, op=op), rd, wr)

        def ts(out_ap, a, s1, s2, op0, op1, rd, wr, e="dve"):
            if op1 is None:
                S.op(e, lambda en: en.tensor_scalar(out=out_ap, in0=a, scalar1=s1, scalar2=None, op0=op0), rd, wr)
            else:
                S.op(e, lambda en: en.tensor_scalar(out=out_ap, in0=a, scalar1=s1, scalar2=s2, op0=op0, op1=op1),
                     rd, wr)

        def stt(out_ap, a, s, b, op0, op1, rd, wr, e="dve"):
            S.op(e, lambda en: en.scalar_tensor_tensor(out=out_ap, in0=a, scalar=s, in1=b, op0=op0, op1=op1), rd, wr)

        def cp(out_ap, in_ap, rd, wr, e="dve"):
            if e == "act":
                S.op("act", lambda en: en.copy(out=out_ap, in_=in_ap), rd, wr)
            else:
                S.op(e, lambda en: en.tensor_copy(out=out_ap, in_=in_ap), rd, wr)

        def recip(out_ap, in_ap, rd, wr):
            S.op("dve", lambda en: en.reciprocal(out=out_ap, in_=in_ap), rd, wr)

        def rsqrt_from(out_t, out_ap, in_ap, in_regs, scale, bias_ap):
            act(out_ap, in_ap, AF.Sqrt, in_regs, [out_t.r], bias=bias_ap, scale=scale)
            recip(out_ap, out_ap, [out_t.r], [out_t.r])

        S.dma("sp", PAR[:, :], par_d[:, :], PAR.r, True)
        S.dma("sp", CV[:, :], cvec_d[:, :], CV.r, True)
        S.dma("sp", CF[:, :], cf_d[:, :], CF.r, True)
        for i in range(0, NCB, 1024):
            n = min(1024, NCB - i)
            xi = X[(i // 1024) % 4]
            S.dma("sp", xi[:, 0:n], cb_d[:, i:i + n], xi.r, True)
            cp(CB[:, i:i + n], xi[:, 0:n], [xi.r], [CB.r])
        S.op("dve", lambda en: en.memset(SMALL[:, 0:1], EPS), [], [SMALL.r])
        S.op("dve", lambda en: en.memset(SMALL[:, 1:2], GN_EPS), [], [SMALL.r])
        S.op("dve", lambda en: en.memset(SMALL[:, 2:3], D * EPS), [], [SMALL.r])
        EPSC = SMALL[:, 0:1]
        GNEPSC = SMALL[:, 1:2]
        DEPSC = SMALL[:, 2:3]

        wq = []
        wstate = {"issued": 0}

        def wview(wd, c0, n):
            return wd.rearrange("(kc p) f -> p kc f", p=128)[:, :, c0:c0 + n]

        def w_issue_upto(i):
            while wstate["issued"] <= min(i, len(wq) - 1):
                u = wstate["issued"]
                slot = WS[u % NSLOT]
                for (dc, n, src) in wq[u]:
                    S.dma("pool", slot[:, :, dc:dc + n], src, slot.r, True)
                wstate["issued"] += 1

        def w_get(i):
            w_issue_upto(i + NSLOT - 1)
            return WS[i % NSLOT]

        plan = []
        for l in range(DEPTH):
            for u in range(6):
                wq.append([(0, 512, wview(wada_d[l], u * 512, 512))])
                plan.append(("ada", l, u))
        LAYERS = [l for l in range(int(os.environ.get('K_LAYERS', DEPTH)))
                  if not os.environ.get('K_ONLY') or str(l) in os.environ['K_ONLY']]
        for g in range(2):
            for l in LAYERS:
                p = l // 2
                if l % 2 == 0:
                    w = wine_d[p]
                    wq.append([(0, 256, wview(w, 1536, 256))])
                    plan.append(("e_lora", g, l))
                    for hp in range(4):
                        wq.append([(0, 128, wview(w, hp * 128, 128)), (128, 128, wview(w, 512 + hp * 128, 128)),
                                   (256, 128, wview(w, 1024 + hp * 128, 128)),
                                   (384, 128, wview(w, A_SHIFT + hp * 128, 128))])
                        plan.append(("e_hp", g, l, hp))
                    for cc in range(4):
                        base = A_SHIFT + 512
                        wq.append([(j * 128, 128, wview(w, base + j * 512 + cc * 128, 128)) for j in range(4)])
                        plan.append(("e_b", g, l, cc))
                    for u in range(2):
                        wq.append([(0, 512, wview(woute_d[p], u * 512, 512))])
                        plan.append(("out", g, l, u))
                else:
                    w = wino_d[p]
                    wq.append([(j * 128 + h2 * 64, 64, wview(w, 1024 + j * 64, 64)) for j in range(4) for h2 in range(2)])
                    plan.append(("o_k", g, l))
                    wq.append([(0, 256, wview(w, 1280, 256))])
                    plan.append(("o_v", g, l))
                    for qq in range(4):
                        wq.append([(0, 128, wview(w, (2 * qq) * 128, 128)), (128, 128, wview(w, 1536 + (2 * qq) * 128, 128)),
                                   (256, 128, wview(w, (2 * qq + 1) * 128, 128)),
                                   (384, 128, wview(w, 1536 + (2 * qq + 1) * 128, 128))])
                        plan.append(("o_q", g, l, qq))
                    for u in range(2):
                        wq.append([(0, 512, wview(wouto_d[p], u * 512, 512))])
                        plan.append(("out", g, l, u))
        wi = {"i": 0}

        def next_w(kind):
            i = wi["i"]
            assert plan[i][0] == kind, (plan[i], kind)
            wi["i"] += 1
            return w_get(i)

        SC = B[0]
        act(X[0][:, 0:16], CV[:, :], AF.Sigmoid, [CV.r], [X[0].r])
        tt(SC[:, 0:16], X[0][:, 0:16], CV[:, :], ALU.mult, [X[0].r, CV.r], [SC.r])
        scv = SC[:, 0:16].rearrange("p (g k) -> p g k", g=2)
        for l in range(DEPTH):
            pm = PS[l % 2]
            for u in range(6):
                slot = next_w("ada")
                for jj in range(4):
                    j = u * 4 + jj
                    for kc in range(KC):
                        mm(pm[:, 2 * j:2 * j + 2], slot[:, kc, jj * 128:(jj + 1) * 128], scv[:, :, kc],
                           kc == 0, kc == KC - 1, [slot.r, SC.r], [pm.r])
            pv = pm[:, 0:48].rearrange("p (j g) -> p g j", g=2)
            for g in range(2):
                o0 = (l * 2 + g) * 24
                tt(MOD[:, o0:o0 + 24], pv[:, g, :], par(f"bada{l}", 0, 24), ALU.add, [pm.r, PAR.r], [MOD.r])

        def rms_and_modulate(g, l):
            o0 = (l * 2 + g) * 24
            stt(GSC[:, :], MOD[:, o0 + 8:o0 + 16], 1.0, par(f"normg{l}", 0, 8), ALU.add, ALU.mult,
                [MOD.r, PAR.r], [GSC.r])
            ts(GSC[:, :], GSC[:, :], 32.0, None, ALU.mult, None, [GSC.r], [GSC.r])
            sqs = [B[15], B[10]]
            for tb in range(2):
                tsl = slice(tb * 512, (tb + 1) * 512)
                pm = PS[tb]
                for kc in range(KC):
                    sq = sqs[kc % 2]
                    act(sq[:, tsl], XT[:, kc, tsl], AF.Square, [XT.r], [sq.r])
                    mm(pm[:, :], ONES, sq[:, tsl], kc == 0, kc == KC - 1, [CB.r, sq.r], [pm.r], inc=True)
                rsqrt_from(RSTD, RSTD[:, tsl], pm[:, :], [pm.r, SMALL.r], 1.0, DEPSC)
            for kc in range(KC):
                tmp = X[11] if kc % 2 == 0 else X[10]
                tt(tmp[:, :], XT[:, kc, :], RSTD[:, :], ALU.mult, [XT.r, RSTD.r], [tmp.r])
                act(HT[:, kc, :], tmp[:, :], AF.Identity, [tmp.r, GSC.r, MOD.r], [HT.r],
                    bias=MOD[:, o0 + kc:o0 + kc + 1], scale=GSC[:, kc:kc + 1])

        def proj_row(slot, c0, pm_pair):
            for tb in range(2):
                pm = pm_pair[tb]
                for kc in range(KC):
                    mm(pm[:, :], slot[:, kc, c0:c0 + 128], HT[:, kc, tb * 512:(tb + 1) * 512],
                       kc == 0, kc == KC - 1, [slot.r, HT.r], [pm.r])

        def out_proj(g, l):
            o0 = (l * 2 + g) * 24
            k = 0
            for u in range(2):
                slot = next_w("out")
                for dj in range(4):
                    dc = u * 4 + dj
                    for tb in range(2):
                        pm = PS[k % 2]
                        k += 1
                        tsl = slice(tb * 512, (tb + 1) * 512)
                        for fc in range(KC):
                            mm(pm[:, :], slot[:, fc, dj * 128:(dj + 1) * 128], YT[:, fc, tsl],
                               fc == 0, fc == KC - 1, [slot.r, YT.r], [pm.r])
                        otmp = X[10 + (k % 2)]
                        cp(otmp[:, 0:512], pm[:, :], [pm.r], [otmp.r], e="act")
                        stt(XT[:, dc, tsl], otmp[:, 0:512], MOD[:, o0 + 16 + dc:o0 + 17 + dc], XT[:, dc, tsl],
                            ALU.mult, ALU.add, [otmp.r, MOD.r, XT.r], [XT.r])

        def even_layer(g, l):
            p = l // 2
            nseq = 4 if g == 0 else 1
            slen = NT // nseq
            cps = slen // CH
            ck(0)
            mu = par(f"mu{p}", 0, 14)
            OM = DER[:, 0:14]
            HM = DER[:, 14:28]
            ts(OM, mu, -1.0, 1.0, ALU.mult, ALU.add, [PAR.r], [DER.r])
            ts(HM, mu, 0.5, None, ALU.mult, None, [PAR.r], [DER.r])
            OKA = DER[:, 28:32]
            ts(OKA, par(f"ka{p}", 0, 4), -1.0, 1.0, ALU.mult, ALU.add, [PAR.r], [DER.r])
            HRK = DER[:, 32:36]
            ts(HRK, par(f"rk{p}", 0, 4), 0.5, None, ALU.mult, None, [PAR.r], [DER.r])
            LW2 = B[14]
            S.dma("sp", X[0][:, 0:512], lw2_d[p], X[0].r, True)
            S.dma("sp", X[0][:, 512:1024], la2_d[p], X[0].r, True)
            cp(LW2[:, :], X[0][:, :], [X[0].r], [LW2.r])

            def shift_row(pm_pair, row, out_t, out_ap, fin=None, Fr=None):
                Fr = X[10] if Fr is None else Fr
                FE = X[11] if fin is not None else out_t
                FEap = X[11][:, :] if fin is not None else out_ap
                for tb in range(2):
                    tsl = slice(tb * 512, (tb + 1) * 512)
                    cp(Fr[:, tsl], pm_pair[tb][:, :], [pm_pair[tb].r], [Fr.r], e="act")
                    ts(FEap[:, tsl], Fr[:, tsl], OM[:, row:row + 1], None, ALU.mult, None,
                       [Fr.r, DER.r], [FE.r])
                F3 = v3(Fr[:, :], nseq)
                E3 = v3(FEap, nseq)
                stt(E3[:, :, 1:slen], F3[:, :, 0:slen - 1], HM[:, row:row + 1], E3[:, :, 1:slen], ALU.mult, ALU.add,
                    [Fr.r, DER.r, FE.r], [FE.r])
                stt(E3[:, :, 0:slen - 1], F3[:, :, 1:slen], HM[:, row:row + 1], E3[:, :, 0:slen - 1], ALU.mult, ALU.add,
                    [Fr.r, DER.r, FE.r], [FE.r])
                if fin is not None:
                    fin(FE)

            ck('a')
            slot = next_w("e_lora")
            LWT = B[12]
            LAT = B[13]
            proj_row(slot, 0, PS[0:2])
            ck('b')
            shift_row(PS[0:2], 12, None, None,
                      fin=lambda FE: act(LWT[:, :], FE[:, :], AF.Tanh, [FE.r], [LWT.r]))
            proj_row(slot, 128, PS[0:2])
            shift_row(PS[0:2], 13, None, None, fin=lambda FE: cp(LAT[:, :], FE[:, :], [FE.r], [LAT.r]))

            ck(1)
            for hp in range(4):
                slot = next_w("e_hp")
                R, K, V = X[0], X[1], X[2]
                proj_row(slot, 0, PS[0:2])
                proj_row(slot, 128, PS[2:4])
                shift_row(PS[0:2], hp, R, R[:, :], Fr=X[10])
                proj_row(slot, 256, PS[0:2])
                shift_row(PS[2:4], 4 + hp, K, K[:, :], Fr=X[11])
                GA = B[11]
                proj_row(slot, 384, PS[2:4])
                shift_row(PS[0:2], 8 + hp, V, V[:, :], Fr=X[10])
                for tb in range(2):
                    tsl = slice(tb * 512, (tb + 1) * 512)
                    act(X[11][:, tsl], PS[2 + tb][:, :], AF.Sigmoid, [PS[2 + tb].r], [X[11].r])
                    tt(GA[:, tsl], PS[2 + tb][:, :], X[11][:, tsl], ALU.mult, [PS[2 + tb].r, X[11].r], [GA.r])
                ck(2)
                KKN = X[3]
                ts(KKN[:, :], K[:, :], par(f"kk{p}", hp, 1), None, ALU.mult, None, [K.r, PAR.r], [KKN.r])
                sq = B[15]
                act(sq[:, :], KKN[:, :], AF.Square, [KKN.r], [sq.r])
                for tb in range(2):
                    tsl = slice(tb * 512, (tb + 1) * 512)
                    mm(PS[tb][:, :], BD1, sq[:, tsl], True, True, [CB.r, sq.r], [PS[tb].r])
                    act(X[10][:, tsl], PS[tb][:, :], AF.Sqrt, [PS[tb].r], [X[10].r])
                    ts(X[10][:, tsl], X[10][:, tsl], 1e-12, None, ALU.max, None, [X[10].r], [X[10].r])
                recip(X[10][:, :], X[10][:, :], [X[10].r], [X[10].r])
                tt(KKN[:, :], KKN[:, :], X[10][:, :], ALU.mult, [KKN.r, X[10].r], [KKN.r])
                Vb = B[10]
                cp(Vb[:, :], V[:, :], [V.r], [Vb.r])
                Vt = B[9]

                def to_tok(src, dst):
                    for half in range(2):
                        pm = PS[2 + half]
                        for cc in range(8):
                            c = half * 8 + cc
                            for hh in range(2):
                                sl = slice(hh * 64, hh * 64 + 64)
                                mm(pm[sl, cc * 64:(cc + 1) * 64], src[sl, c * 64:(c + 1) * 64],
                                   IDB[sl, hh * 64:hh * 64 + 64], True, True, [src.r, CB.r], [pm.r],
                                   tp=(hh * 64, hh * 64), inc=(cc == 7 and hh == 1))
                        cp(dst[:, half * 512:(half + 1) * 512], pm[:, :], [pm.r], [dst.r], e="act")
                ck(3)
                to_tok(Vb, Vt)
                ck(4)

                KS = X[4]
                YA = X[5]
                S.op("dve", lambda en: en.memset(YA[:, :], 0.0), [], [YA.r])
                SETS = [dict(RT=B[0], KK=B[3], KTt=B[4], BTt=B[5], AkT=B[6], QbT=B[7], QkT=B[8], NIT=B[15]),
                        dict(RT=B[16], KK=B[17], KTt=B[18], BTt=B[19], AkT=B[20], QbT=B[10], QkT=B[21], NIT=B[22])]
                SEQB = [dict(T32=TST, TBF=[mk_tb[0], mk_tb[1]], XS=mk_xs[0], NU=mk_xs[1], TW=X[10], PTS=X[11],
                             banks=(PS[2], PS[3], PS[5], PS[6])),
                        dict(T32=TST1, TBF=[mk_tb[2], mk_tb[3]], XS=mk_xs[2], NU=mk_xs[3], TW=X[8], PTS=X[9],
                             banks=(PS[4], PS[7], PS[0], PS[1]))]

                def prep_local(d):
                    st = SETS[d]
                    SG, L, E1, E2, A_, TMP = X[6], X[7], X[8], X[9], X[10], X[11]
                    dsl = slice(d * 64, d * 64 + 64)
                    for tb in range(2):
                        tsl = slice(tb * 512, (tb + 1) * 512)
                        mm(PS[tb][:, :], LW2[dsl, hp * 128:(hp + 1) * 128], LWT[dsl, tsl], True, True,
                           [LW2.r, LWT.r], [PS[tb].r], tp=(d * 64, 0))
                        act(SG[:, tsl], PS[tb][:, :], AF.Sigmoid, [PS[tb].r, PAR.r], [SG.r],
                            bias=par(f"w0{p}", d * 4 + hp, 1))
                    if d == 0:
                        S.op("dve", lambda en: en.tensor_tensor_scan(out=L[:, :], data0=RESF, data1=SG[:, :],
                                                                     initial=0.0, op0=ALU.mult, op1=ALU.add),
                             [CF.r, SG.r], [L.r])
                    else:
                        S.op("dve", lambda en: en.tensor_tensor_scan(out=L[:, ::-1], data0=RESB[:, ::-1],
                                                                     data1=SG[:, ::-1], initial=0.0,
                                                                     op0=ALU.mult, op1=ALU.add),
                             [CF.r, SG.r], [L.r])
                    for tb in range(2):
                        tsl = slice(tb * 512, (tb + 1) * 512)
                        mm(PS[tb][:, :], LW2[dsl, 512 + hp * 128:512 + (hp + 1) * 128], LAT[dsl, tsl], True, True,
                           [LW2.r, LAT.r], [PS[tb].r], tp=(d * 64, 0))
                        act(A_[:, tsl], PS[tb][:, :], AF.Sigmoid, [PS[tb].r, PAR.r], [A_.r],
                            bias=par(f"a0{p}", d * 4 + hp, 1))
                    RTd, KTd, BTd, KKTd = st["RT"], B[1], B[2], st["KK"]
                    act(E1[:, :], L[:, :], AF.Exp, [L.r], [E1.r], scale=-C0)
                    tt(RTd[:, :], R[:, :], E1[:, :], ALU.mult, [R.r, E1.r], [RTd.r])
                    WC = SMALL[:, 16 + d * 16:32 + d * 16]
                    e3 = E1[:, :].rearrange("p (c t) -> p c t", t=CH)
                    cp(WC, e3[:, :, CH - 1] if d == 0 else e3[:, :, 0], [E1.r], [SMALL.r])
                    act(E2[:, :], L[:, :], AF.Exp, [L.r], [E2.r], scale=C0)
                    ts(TMP[:, :], A_[:, :], par(f"ka{p}", hp, 1), OKA[:, hp:hp + 1], ALU.mult, ALU.add,
                       [A_.r, PAR.r, DER.r], [TMP.r])
                    tt(TMP[:, :], TMP[:, :], K[:, :], ALU.mult, [TMP.r, K.r], [TMP.r])
                    if d == 0:
                        cp(KS[:, :], TMP[:, :], [TMP.r], [KS.r])
                    else:
                        tt(KS[:, :], KS[:, :], TMP[:, :], ALU.add, [KS.r, TMP.r], [KS.r])
                    tt(KTd[:, :], TMP[:, :], E2[:, :], ALU.mult, [TMP.r, E2.r], [KTd.r])
                    tt(TMP[:, :], A_[:, :], KKN[:, :], ALU.mult, [A_.r, KKN.r], [TMP.r])
                    stt(BTd[:, :], TMP[:, :], -1.0, E2[:, :], ALU.mult, ALU.mult, [TMP.r, E2.r], [BTd.r])
                    tt(TMP[:, :], L[:, :], SG[:, :], ALU.subtract, [L.r, SG.r], [TMP.r])
                    act(TMP[:, :], TMP[:, :], AF.Exp, [TMP.r], [TMP.r], scale=-C0)
                    tt(KKTd[:, :], KKN[:, :], TMP[:, :], ALU.mult, [KKN.r, TMP.r], [KKTd.r])
                    KTt, BTt = st["KTt"], st["BTt"]
                    to_tok(KTd, KTt)
                    to_tok(BTd, BTt)
                    AkT, QbT, QkT, NIT = st["AkT"], st["QbT"], st["QkT"], st["NIT"]
                    mA, mB, mI = (MSL, MSU, MIU) if d == 0 else (MSU, MSL, MIL)
                    if g == 0:
                        A0, B0, N0 = X[6], X[7], X[8]
                        A1, B1 = X[9], X[10]
                    elif d == 1:
                        vws = []
                        for xt_ in (X[6], X[7], X[8]):
                            ab = xt_[:, :].bitcast(BF16)
                            vws.append(View(ab[:, 0:1024], xt_.r))
                            vws.append(View(ab[:, 1024:2048], xt_.r))
                        A0, B0, N0, A1, B1 = vws[0], vws[2], vws[4], vws[1], vws[3]
                    else:
                        A0, B0, N0 = B[16], B[17], B[18]
                        A1, B1 = B[19], B[20]

                    def scores(lh, rh, pm, half):
                        for cc in range(8):
                            c = half * 8 + cc
                            for hh in range(2):
                                sl = slice(hh * 64, hh * 64 + 64)
                                mm(pm[sl, cc * 64:(cc + 1) * 64], lh[sl, c * 64:(c + 1) * 64],
                                   rh[sl, c * 64:(c + 1) * 64], True, True, [lh.r, rh.r], [pm.r],
                                   tp=(hh * 64, hh * 64), inc=(cc == 7 and hh == 1))
                    for half in range(2):
                        hs = slice(half * 512, (half + 1) * 512)
                        scores(KKTd, BTd, PS[2], half)
                        tt(A0[:, hs], PS[2][:, :], mA, ALU.mult, [PS[2].r, CF.r], [A0.r])
                        scores(BTd, KKTd, PS[3], half)
                        tt(B0[:, hs], PS[3][:, :], mB, ALU.mult, [PS[3].r, CF.r], [B0.r])
                        tt(N0[:, hs], IDN8, B0[:, hs], ALU.add, [CF.r, B0.r], [N0.r])
                        scores(KTd, KKTd, PS[4], half)
                        tt(AkT[:, hs], PS[4][:, :], mB, ALU.mult, [PS[4].r, CF.r], [AkT.r])
                        scores(BTd, RTd, PS[5], half)
                        tt(QbT[:, hs], PS[5][:, :], mI, ALU.mult, [PS[5].r, CF.r], [QbT.r])
                        scores(KTd, RTd, PS[6], half)
                        tt(QkT[:, hs], PS[6][:, :], mI, ALU.mult, [PS[6].r, CF.r], [QkT.r])
                    Ap, Bp, An, Bn = A0, B0, A1, B1
                    for lev in range(1, 6):
                        for half in range(2):
                            hs = slice(half * 512, (half + 1) * 512)
                            pa = PS[2] if half == 0 else PS[5]
                            pb = PS[3] if half == 0 else PS[6]
                            pn = PS[4]
                            for cc in range(8):
                                c = half * 8 + cc
                                cs = slice(c * 64, (c + 1) * 64)
                                for hh in range(2):
                                    sl = slice(hh * 64, hh * 64 + 64)
                                    mm(pa[sl, cc * 64:(cc + 1) * 64], Bp[sl, cs], Ap[sl, cs], True, True,
                                       [Ap.r, Bp.r], [pa.r], tp=(hh * 64, hh * 64), inc=(cc == 7 and hh == 1))
                            cp(An[:, hs], pa[:, :], [pa.r], [An.r], e="act")
                            if lev < 5:
                                for cc in range(8):
                                    c = half * 8 + cc
                                    cs = slice(c * 64, (c + 1) * 64)
                                    for hh in range(2):
                                        sl = slice(hh * 64, hh * 64 + 64)
                                        mm(pb[sl, cc * 64:(cc + 1) * 64], Ap[sl, cs], Bp[sl, cs], True, True,
                                           [Ap.r, Bp.r], [pb.r], tp=(hh * 64, hh * 64), inc=(cc == 7 and hh == 1))
                                cp(Bn[:, hs], pb[:, :], [pb.r], [Bn.r], e="act")
                            for cc in range(8):
                                c = half * 8 + cc
                                cs = slice(c * 64, (c + 1) * 64)
                                for hh in range(2):
                                    sl = slice(hh * 64, hh * 64 + 64)
                                    mm(pn[sl, cc * 64:(cc + 1) * 64], An[sl, cs], N0[sl, cs], True, True,
                                       [An.r, N0.r], [pn.r], tp=(hh * 64, hh * 64), inc=(cc == 7 and hh == 1))
                            tt(N0[:, hs], pn[:, :], N0[:, hs], ALU.add, [pn.r, N0.r], [N0.r])
                        Ap, Bp, An, Bn = An, Bn, Ap, Bp
                    cp(NIT[:, :], N0[:, :], [N0.r], [NIT.r])

                def init_state(d):
                    sb_ = SEQB[d]
                    T32 = sb_["T32"]
                    if g == 0:
                        S.op("dve", lambda en: en.memset(T32[:, 0:nseq * 64], 0.0), [], [T32.r])
                    else:
                        for hh in range(2):
                            S.dma("sp", TST[hh * 64:hh * 64 + 64, 256:320], st_d[p, d, hp * 2 + hh], TST.r, True)
                        for hh in range(2):
                            sl = slice(hh * 64, hh * 64 + 64)
                            mm(PS[4][sl, 0:64], TST[sl, 256:320], CF[sl, hh * 64:hh * 64 + 64], True, True,
                               [TST.r, CF.r], [PS[4].r], tp=(hh * 64, hh * 64))
                        cp(T32[:, 0:64], PS[4][:, 0:64], [PS[4].r], [T32.r])
                    cp(sb_["TBF"][0][:, 0:nseq * 64], T32[:, 0:nseq * 64], [T32.r], [sb_["TBF"][0].r])

                def seq(d):
                    st, sb_ = SETS[d], SEQB[d]
                    RTd, KKTd, KTt, BTt = st["RT"], st["KK"], st["KTt"], st["BTt"]
                    AkT, QbT, QkT, NIT = st["AkT"], st["QbT"], st["QkT"], st["NIT"]
                    T32, TBF, XS, NU, TW, PTS = sb_["T32"], sb_["TBF"], sb_["XS"], sb_["NU"], sb_["TW"], sb_["PTS"]
                    px, pu, pt, py = sb_["banks"]
                    WC = SMALL[:, 16 + d * 16:32 + d * 16]
                    ya3 = YA[:, :].rearrange("p (q j t) -> p q j t", q=nseq, j=cps)
                    for j in range(cps):
                        cj = j if d == 0 else cps - 1 - j
                        tcur, tnxt = TBF[j % 2], TBF[(j + 1) % 2]
                        for q in range(nseq):
                            c = q * cps + cj
                            cs = slice(c * 64, (c + 1) * 64)
                            qs = slice(q * 64, (q + 1) * 64)
                            for hh in range(2):
                                sl = slice(hh * 64, hh * 64 + 64)
                                tp = (hh * 64, hh * 64)
                                mm(px[sl, qs], KKTd[sl, cs], tcur[sl, qs], True, False, [KKTd.r, tcur.r], [px.r], tp=tp)
                                mm(px[sl, qs], AkT[sl, cs], Vt[sl, cs], False, True, [AkT.r, Vt.r], [px.r], tp=tp,
                                   inc=(q == nseq - 1 and hh == 1))
                        cp(XS[:, 0:nseq * 64], px[:, 0:nseq * 64], [px.r], [XS.r])
                        for q in range(nseq):
                            c = q * cps + cj
                            cs = slice(c * 64, (c + 1) * 64)
                            qs = slice(q * 64, (q + 1) * 64)
                            for hh in range(2):
                                sl = slice(hh * 64, hh * 64 + 64)
                                mm(pu[sl, qs], NIT[sl, cs], XS[sl, qs], True, True, [NIT.r, XS.r], [pu.r],
                                   tp=(hh * 64, hh * 64), inc=(q == nseq - 1 and hh == 1))
                        cp(NU[:, 0:nseq * 64], pu[:, 0:nseq * 64], [pu.r], [NU.r])
                        for q in range(nseq):
                            c = q * cps + cj
                            cs = slice(c * 64, (c + 1) * 64)
                            qs = slice(q * 64, (q + 1) * 64)
                            for hh in range(2):
                                sl = slice(hh * 64, hh * 64 + 64)
                                tp = (hh * 64, hh * 64)
                                mm(pt[sl, qs], BTt[sl, cs], NU[sl, qs], True, False, [BTt.r, NU.r], [pt.r], tp=tp)
                                mm(pt[sl, qs], KTt[sl, cs], Vt[sl, cs], False, True, [KTt.r, Vt.r], [pt.r], tp=tp,
                                   inc=False)
                                mm(py[sl, qs], tcur[sl, qs], RTd[sl, cs], True, False, [tcur.r, RTd.r], [py.r], tp=tp)
                                mm(py[sl, qs], NU[sl, qs], QbT[sl, cs], False, False, [NU.r, QbT.r], [py.r], tp=tp)
                                mm(py[sl, qs], Vt[sl, cs], QkT[sl, cs], False, True, [Vt.r, QkT.r], [py.r], tp=tp,
                                   inc=(q == nseq - 1 and hh == 1))
                        pyv = py[:, 0:nseq * 64].rearrange("p (q t) -> p q t", q=nseq)
                        tt(ya3[:, :, cj, :], pyv, ya3[:, :, cj, :], ALU.add, [py.r, YA.r], [YA.r])
                        cp(PTS[:, 0:nseq * 64], pt[:, 0:nseq * 64], [pt.r], [PTS.r])
                        for q in range(nseq):
                            c = q * cps + cj
                            qs = slice(q * 64, (q + 1) * 64)
                            ts(TW[:, qs], T32[:, qs], WC[:, c:c + 1], None, ALU.mult, None, [T32.r, SMALL.r], [TW.r])
                            stt(tnxt[:, qs], PTS[:, qs], WC[:, c:c + 1], TW[:, qs], ALU.mult, ALU.add,
                                [PTS.r, SMALL.r, TW.r], [tnxt.r])
                            stt(T32[:, qs], PTS[:, qs], WC[:, c:c + 1], TW[:, qs], ALU.mult, ALU.add,
                                [PTS.r, SMALL.r, TW.r], [T32.r])

                for d in range(2):
                    prep_local(d)
                    init_state(d)
                qf = S.record(lambda: seq(0))
                qb_ = S.record(lambda: seq(1))
                S.replay(qf, qb_)
                if g == 0:
                    for d in range(2):
                        T32 = SEQB[d]["T32"]
                        for q in range(nseq):
                            for hh in range(2):
                                sl = slice(hh * 64, hh * 64 + 64)
                                mm(PS[4][sl, q * 64:(q + 1) * 64], T32[sl, q * 64:(q + 1) * 64],
                                   CF[sl, hh * 64:hh * 64 + 64], True, True, [T32.r, CF.r], [PS[4].r],
                                   tp=(hh * 64, hh * 64))
                        cp(TST[:, 256:512], PS[4][:, 0:256], [PS[4].r], [TST.r])
                        for q in range(nseq):
                            for hh in range(2):
                                S.dma("sp", ns_d[q, p, d, hp * 2 + hh],
                                      TST[hh * 64:hh * 64 + 64, 256 + q * 64:256 + (q + 1) * 64], TST.r, False)
                ck(8)
                Yb = B[0]
                cp(Yb[:, :], YA[:, :], [YA.r], [Yb.r], e="act")
                DD, RS, BON = X[6], X[7], X[8]
                sq = B[1]
                for tb in range(2):
                    tsl = slice(tb * 512, (tb + 1) * 512)
                    mm(PS[tb][:, :], BDM, Yb[:, tsl], True, True, [CB.r, Yb.r], [PS[tb].r])
                    tt(DD[:, tsl], YA[:, tsl], PS[tb][:, :], ALU.subtract, [YA.r, PS[tb].r], [DD.r])
                act(sq[:, :], DD[:, :], AF.Square, [DD.r], [sq.r])
                for tb in range(2):
                    tsl = slice(tb * 512, (tb + 1) * 512)
                    mm(PS[tb][:, :], BDM, sq[:, tsl], True, True, [CB.r, sq.r], [PS[tb].r])
                    rsqrt_from(RS, RS[:, tsl], PS[tb][:, :], [PS[tb].r, SMALL.r], 1.0, GNEPSC)
                tt(DD[:, :], DD[:, :], RS[:, :], ALU.mult, [DD.r, RS.r], [DD.r])
                ts(DD[:, :], DD[:, :], par(f"lng{p}", hp, 1), par(f"lnb{p}", hp, 1), ALU.mult, ALU.add,
                   [DD.r, PAR.r], [DD.r])
                rkb = B[2]
                stt(rkb[:, :], KS[:, :], HRK[:, hp:hp + 1], R[:, :], ALU.mult, ALU.mult, [KS.r, DER.r, R.r], [rkb.r])
                for tb in range(2):
                    tsl = slice(tb * 512, (tb + 1) * 512)
                    mm(PS[tb][:, :], BD1, rkb[:, tsl], True, True, [CB.r, rkb.r], [PS[tb].r])
                    tt(BON[:, tsl], PS[tb][:, :], V[:, tsl], ALU.mult, [PS[tb].r, V.r], [BON.r])
                tt(DD[:, :], DD[:, :], BON[:, :], ALU.add, [DD.r, BON.r], [DD.r])
                tt(YT[:, hp, :], DD[:, :], GA[:, :], ALU.mult, [DD.r, GA.r], [YT.r])

            ck(9)
            for cc in range(4):
                slot = next_w("e_b")
                BG, CG, U_, CO, SGm = X[0], X[1], X[2], X[3], X[4]
                proj_row(slot, 0, PS[0:2])
                proj_row(slot, 128, PS[2:4])
                for tb in range(2):
                    cp(BG[:, tb * 512:(tb + 1) * 512], PS[tb][:, :], [PS[tb].r], [BG.r], e="act")
                proj_row(slot, 256, PS[0:2])
                for tb in range(2):
                    cp(CG[:, tb * 512:(tb + 1) * 512], PS[2 + tb][:, :], [PS[2 + tb].r], [CG.r], e="act")
                proj_row(slot, 384, PS[2:4])
                for tb in range(2):
                    tsl = slice(tb * 512, (tb + 1) * 512)
                    tt(U_[:, tsl], PS[tb][:, :], CG[:, tsl], ALU.mult, [PS[tb].r, CG.r], [U_.r])
                cw = lambda tap: par(f"cw{p}", tap * 4 + cc, 1)
                ts(CO[:, :], U_[:, :], cw(1), par(f"cb{p}", cc, 1), ALU.mult, ALU.add, [U_.r, PAR.r], [CO.r])
                U3 = v3(U_[:, :], nseq)
                C3 = v3(CO[:, :], nseq)
                stt(C3[:, :, 1:slen], U3[:, :, 0:slen - 1], cw(0), C3[:, :, 1:slen], ALU.mult, ALU.add,
                    [U_.r, PAR.r, CO.r], [CO.r])
                stt(C3[:, :, 0:slen - 1], U3[:, :, 1:slen], cw(2), C3[:, :, 0:slen - 1], ALU.mult, ALU.add,
                    [U_.r, PAR.r, CO.r], [CO.r])
                tt(CO[:, :], CO[:, :], BG[:, :], ALU.mult, [CO.r, BG.r], [CO.r])
                for tb in range(2):
                    tsl = slice(tb * 512, (tb + 1) * 512)
                    act(SGm[:, tsl], PS[2 + tb][:, :], AF.Sigmoid, [PS[2 + tb].r], [SGm.r])
                    tt(SGm[:, tsl], PS[2 + tb][:, :], SGm[:, tsl], ALU.mult, [PS[2 + tb].r, SGm.r], [SGm.r])
                tt(YT[:, 4 + cc, :], CO[:, :], SGm[:, :], ALU.mult, [CO.r, SGm.r], [YT.r])

        def odd_layer(g, l):
            p = l // 2
            nkeys = 256 if g == 0 else 1536
            nkb = nkeys // 128
            KTa = [B[0], B[1], B[2], B[3]] if g == 0 else None
            if g == 0:
                ktile = [(B[j], 0) for j in range(4)]
            else:
                ktile = [(B[2 * j], B[2 * j + 1]) for j in range(4)]
            Vtok = B[8]
            Vtok2 = B[9]
            Vc = B[10]
            slot = next_w("o_k")

            def qk_norm_row(pm_pair, gname, out_t, out_ap, rope):
                Q = X[4]
                sq = B[15]
                RS = X[5]
                for tb in range(2):
                    tsl = slice(tb * 512, (tb + 1) * 512)
                    cp(Q[:, tsl], pm_pair[tb][:, :], [pm_pair[tb].r], [Q.r], e="act")
                    act(sq[:, tsl], pm_pair[tb][:, :], AF.Square, [pm_pair[tb].r], [sq.r])
                for tb in range(2):
                    tsl = slice(tb * 512, (tb + 1) * 512)
                    mm(pm_pair[tb][:, :], BDM, sq[:, tsl], True, True, [CB.r, sq.r], [pm_pair[tb].r])
                    rsqrt_from(RS, RS[:, tsl], pm_pair[tb][:, :], [pm_pair[tb].r, SMALL.r], 1.0, EPSC)
                if not rope:
                    stt(out_ap, Q[:, :], par(gname, 0, 1), RS[:, :], ALU.mult, ALU.mult, [Q.r, PAR.r, RS.r], [out_t.r])
                    return
                QN, QNb, T1 = X[6], B[15], X[7]
                stt(QN[:, :], Q[:, :], par(gname, 0, 1), RS[:, :], ALU.mult, ALU.mult, [Q.r, PAR.r, RS.r], [QN.r])
                cp(QNb[:, :], QN[:, :], [QN.r], [QNb.r], e="act")
                tt(T1[:, :], QN[:, :], COS, ALU.mult, [QN.r, CF.r], [T1.r])
                for tb in range(2):
                    tsl = slice(tb * 512, (tb + 1) * 512)
                    mm(pm_pair[tb][:, :], ROTT, QNb[:, tsl], True, True, [CB.r, QNb.r], [pm_pair[tb].r])
                    tt(QN[:, tsl], pm_pair[tb][:, :], SIN[:, tsl],
                       ALU.mult, [pm_pair[tb].r, CF.r], [QN.r])
                tt(out_ap, T1[:, :], QN[:, :], ALU.add, [T1.r, QN.r], [out_t.r])

            KN32 = X[8]
            for j in range(4):
                proj_row(slot, j * 128, PS[0:2])
                kt = ktile[j][0]
                if g == 0:
                    qk_norm_row(PS[0:2], f"kg{p}", KN32, KN32[:, :], False)
                    cp(kt[:, :], KN32[:, :], [KN32.r], [kt.r], e="act")
                    for blk in range(8):
                        tr(PS[2 + blk // 4][:, (blk % 4) * 128:(blk % 4 + 1) * 128],
                           KN32[:, blk * 128:(blk + 1) * 128], ident, [KN32.r, CF.r], [PS[2 + blk // 4].r])
                    OK_ = X[9]
                    for hf in range(2):
                        cp(OK_[:, hf * 512:(hf + 1) * 512], PS[2 + hf][:, :], [PS[2 + hf].r], [OK_.r])
                    for q in range(4):
                        src = OK_[:, q * 256:(q + 1) * 256].rearrange("p (b c) -> p b c", b=2)[:, :, 0:64]
                        dst = nk_d[q, p, j].rearrange("(b t) c -> t b c", b=2)
                        S.dma("sp", dst, src, OK_.r, False)
                else:
                    qk_norm_row(PS[0:2], f"kg{p}", kt, kt[:, :], True)
                    kc_t = ktile[j][1]
                    st = X[9]
                    for h2 in range(2):
                        S.dma("sp", st[:, 0:512].rearrange("p (b c) -> p b c", b=4)[:, :, h2 * 64:(h2 + 1) * 64],
                              ck_d[p, j].rearrange("(b t) c -> t b c", b=4), st.r, True)
                    for blk in range(4):
                        tr(PS[2][:, blk * 128:(blk + 1) * 128], st[:, blk * 128:(blk + 1) * 128], ident,
                           [st.r, CF.r], [PS[2].r])
                    cp(kc_t[:, 0:512], PS[2][:, :], [PS[2].r], [kc_t.r], e="act")
            ck('o2')
            slot = next_w("o_v")
            VO = X[10]
            for blk in range(8):
                pm = PS[blk % 2]
                for kc in range(KC):
                    mm(pm[:, 0:256], HT[:, kc, blk * 128:(blk + 1) * 128], slot[:, kc, 0:256], kc == 0, kc == KC - 1,
                       [HT.r, slot.r], [pm.r])
                vdst = (Vtok if blk < 4 else Vtok2)
                if g == 1:
                    cp(vdst[:, (blk % 4) * 256:(blk % 4 + 1) * 256], pm[:, 0:256], [pm.r], [vdst.r], e="act")
                if g == 0:
                    cp(VO[:, (blk % 4) * 256:(blk % 4 + 1) * 256], pm[:, 0:256], [pm.r], [VO.r])
                    cp(vdst[:, (blk % 4) * 256:(blk % 4 + 1) * 256], VO[:, (blk % 4) * 256:(blk % 4 + 1) * 256],
                       [VO.r], [vdst.r], e="act")
                    if blk % 4 == 3:
                        for qq in range(2):
                            q = (blk // 4) * 2 + qq
                            src = VO[:, qq * 512:(qq + 1) * 512].rearrange("p (b h c) -> p b h c", b=2, h=4)
                            for hh in range(4):
                                dst = nv_d[q, p, hh].rearrange("(b t) c -> t b c", b=2)
                                S.dma("sp", dst, src[:, :, hh, :], VO.r, False)
            if g == 1:
                st = X[9]
                for hh in range(4):
                    S.dma("sp", st[:, :].rearrange("p (b h c) -> p b h c", b=4, h=4)[:, :, hh, :],
                          cv_d[p, hh].rearrange("(b t) c -> t b c", b=4), st.r, True)
                cp(Vc[:, :], st[:, :], [st.r], [Vc.r])

            ck('o3')

            def vblock(kb, kvh, seq=None):
                if g == 0:
                    blk = seq * 2 + kb
                    tl = Vtok if blk < 4 else Vtok2
                    return tl[:, (blk % 4) * 256 + kvh * 64:(blk % 4) * 256 + kvh * 64 + 64], tl.r
                if kb < 4:
                    return Vc[:, kb * 256 + kvh * 64:kb * 256 + kvh * 64 + 64], Vc.r
                blk = kb - 4
                tl = Vtok if blk < 4 else Vtok2
                return tl[:, (blk % 4) * 256 + kvh * 64:(blk % 4) * 256 + kvh * 64 + 64], tl.r

            def kblock(kb, kvh, sl, seq=None):
                if g == 0:
                    t_ = ktile[kvh][0]
                    c0 = seq * 256 + kb * 128
                    return t_[sl, c0:c0 + 128], t_.r
                if kb < 4:
                    t_ = ktile[kvh][1]
                    return t_[sl, kb * 128:(kb + 1) * 128], t_.r
                t_ = ktile[kvh][0]
                return t_[sl, (kb - 4) * 128:(kb - 3) * 128], t_.r

            QT = B[11]
            GSs = [B[12], B[20]]
            QZs = [[B[16], B[17]], [B[18], B[19]]]
            PTs = [B[13], B[14]]
            sbank = [PS[2], PS[3], PS[4]]
            po, pd = PS[5], PS[6]
            RD, OA = X[8], X[9]
            slot_h = [None]
            nexp = [0]
            nblk = [0]

            def prep(qr, buf):
                if qr % 2 == 0:
                    slot_h[0] = next_w("o_q")
                slot = slot_h[0]
                r2 = qr % 2
                QZ, GS = QZs[buf], GSs[buf]
                proj_row(slot, r2 * 256, PS[0:2])
                qk_norm_row(PS[0:2], f"qg{p}", QT, QT[:, :], g == 1)
                for hh in range(2):
                    sl = slice(hh * 64, hh * 64 + 64)
                    so = slice((1 - hh) * 64, (1 - hh) * 64 + 64)
                    S.op("dve", lambda en, t_=QZ[hh], so_=so: en.memset(t_[so_, :], 0.0), [], [QZ[hh].r])
                    cp(QZ[hh][sl, :], QT[sl, :], [QT.r, QZ[hh].r], [QZ[hh].r])
                proj_row(slot, r2 * 256 + 128, PS[0:2])
                for tb in range(2):
                    tsl = slice(tb * 512, (tb + 1) * 512)
                    act(X[5][:, tsl], PS[tb][:, :], AF.Sigmoid, [PS[tb].r], [X[5].r])
                    tt(GS[:, tsl], PS[tb][:, :], X[5][:, tsl], ALU.mult, [PS[tb].r, X[5].r], [GS.r])

            def attn(qr, buf):
                kvh = qr // 2
                QZ, GS = QZs[buf], GSs[buf]
                qblocks = [(q, q * 256, 256) for q in range(4)] if g == 0 else [(None, 0, 512), (None, 512, 512)]
                nk = nkb if g == 1 else 2
                units = [(bi, seq, q0, qn, hh, kb) for bi, (seq, q0, qn) in enumerate(qblocks)
                         for hh in range(2) for kb in range(nk)]
                ids = {}

                def emit_score(u):
                    bi, seq, q0, qn, hh, kb = u
                    i = nexp[0]
                    nexp[0] += 1
                    ids[u] = i
                    sb = sbank[i % 3]
                    kap, kreg = kblock(kb, kvh, slice(0, 128), seq)
                    mm(sb[:, 0:qn], kap, QZ[hh][:, q0:q0 + qn], True, True, [kreg, QZ[hh].r], [sb.r])

                def emit_rest(u):
                    bi, seq, q0, qn, hh, kb = u
                    i = ids[u]
                    sb = sbank[i % 3]
                    pt_ = PTs[i % 2]
                    po = PS[5] if (nblk[0] + bi) % 2 == 0 else PS[7]
                    sl = slice(hh * 64, hh * 64 + 64)
                    act(pt_[:, 0:qn], sb[:, 0:qn], AF.Exp, [sb.r], [pt_.r], scale=0.125)
                    vap, vreg = vblock(kb, kvh, seq)
                    last = kb == nk - 1
                    mm(po[sl, 0:qn], vap, pt_[:, 0:qn], kb == 0, last, [vreg, pt_.r], [po.r], tp=(0, hh * 64))
                    mm(pd[sl, 0:qn], ONES[:, 0:64], pt_[:, 0:qn], kb == 0, last, [CB.r, pt_.r], [pd.r],
                       tp=(0, hh * 64), inc=True)
                    if last and hh == 1:
                        recip(RD[:, 0:qn], pd[:, 0:qn], [pd.r], [RD.r])
                        tt(OA[:, 0:qn], po[:, 0:qn], RD[:, 0:qn], ALU.mult, [po.r, RD.r], [OA.r])
                        tt(YT[:, qr, q0:q0 + qn], OA[:, 0:qn], GS[:, q0:q0 + qn], ALU.mult, [OA.r, GS.r], [YT.r])

                emit_score(units[0])
                if len(units) > 1:
                    emit_score(units[1])
                for k, u in enumerate(units):
                    if k + 2 < len(units):
                        emit_score(units[k + 2])
                    emit_rest(u)
                nblk[0] += len(qblocks)

            S.replay(S.record(lambda: prep(0, 0)))
            for qr in range(8):
                qa = S.record(lambda: attn(qr, qr % 2))
                qb = S.record(lambda: prep(qr + 1, (qr + 1) % 2)) if qr < 7 else []
                S.replay(qa, qb)

        mk_tb = [mk(f"TBF{i}", [128, 256], BF16) for i in range(4)]
        mk_xs = [mk(f"XSb{i}", [128, 256], BF16) for i in range(4)]
        TST1 = mk("TST1", [128, 256], F32)

        for g in range(2):
            xin = xp_d if g == 0 else xs_d
            yout = yp_d if g == 0 else ys_d
            for rnd in range(2):
                for i in range(4):
                    tt_ = rnd * 4 + i
                    S.dma("sp", X[i][:, :], xin[tt_ * 128:(tt_ + 1) * 128, :], X[i].r, True)
                for kc in range(KC):
                    pm = PS[kc % 2]
                    for i in range(4):
                        tr(pm[:, i * 128:(i + 1) * 128], X[i][:, kc * 128:(kc + 1) * 128], ident, [X[i].r, CF.r], [pm.r])
                    cp(XT[:, kc, rnd * 512:(rnd + 1) * 512], pm[:, :], [pm.r], [XT.r], e="act" if kc % 2 else "dve")
            try:
              for l in LAYERS:
                S.barrier()
                rms_and_modulate(g, l)
                if l % 2 == 0:
                    even_layer(g, l)
                else:
                    odd_layer(g, l)
                out_proj(g, l)
                if dbg:
                    S.dma("sp", dbg_d[l, g].rearrange("p (k t) -> p k t", k=KC), XT[:, :, :], XT.r, False)
            except _Stop:
                wi['i'] = len(plan)
                LAYERS = []
            S.barrier()
            sq = B[15]
            for tb in range(2):
                tsl = slice(tb * 512, (tb + 1) * 512)
                pm = PS[tb]
                for kc in range(KC):
                    act(sq[:, tsl], XT[:, kc, tsl], AF.Square, [XT.r], [sq.r])
                    mm(pm[:, :], ONES, sq[:, tsl], kc == 0, kc == KC - 1, [CB.r, sq.r], [pm.r], inc=True)
                rsqrt_from(RSTD, RSTD[:, tsl], pm[:, :], [pm.r, SMALL.r], 1.0, DEPSC)
            ts(GSC[:, :], par("finalg", 0, 8), 32.0, None, ALU.mult, None, [PAR.r], [GSC.r])
            for kc in range(KC):
                stt(XT[:, kc, :], XT[:, kc, :], GSC[:, kc:kc + 1], RSTD[:, :], ALU.mult, ALU.mult,
                    [XT.r, GSC.r, RSTD.r], [XT.r])
            for rnd in range(2):
                for i in range(4):
                    tt_ = rnd * 4 + i
                    for kc4 in range(2):
                        pm = PS[(i * 2 + kc4) % 2]
                        for k2 in range(4):
                            kc = kc4 * 4 + k2
                            tr(pm[:, k2 * 128:(k2 + 1) * 128], XT[:, kc, tt_ * 128:(tt_ + 1) * 128], ident,
                               [XT.r, CF.r], [pm.r])
                        cp(X[i][:, kc4 * 512:(kc4 + 1) * 512], pm[:, :], [pm.r], [X[i].r], e="act" if kc4 else "dve")
                    S.dma("sp", yout[tt_ * 128:(tt_ + 1) * 128, :], X[i][:, :], X[i].r, False)
            S.barrier()

        S.finish([t_.r for t_ in X] + [TST.r, XT.r] + [w_.r for w_ in WS])
        print("instructions emitted:", S.ninstr, {k: v for k, v in S.cnt.items()})
    return nc


def _consts():
    f32 = np.float32
    p = np.arange(128)
    i64 = p % 64
    ident = np.eye(128, dtype=f32)
    col = np.arange(512)
    j64 = col % 64
    idn8 = (i64[:, None] == j64[None, :]).astype(f32)
    t = np.arange(1024)
    inv = 10000.0 ** (-(np.arange(0, 32, 2).astype(np.float64)) / 32.0)
    within = i64 % 32
    f = within % 16
    half = i64 // 32
    pos = np.where(half[:, None] == 0, (t // 64)[None, :], (t % 64)[None, :]).astype(np.float64)
    ang = pos * inv[f][:, None]
    cos = np.cos(ang).astype(f32)
    sin = np.sin(ang).astype(f32)
    cf = np.concatenate([ident, idn8, cos, sin], axis=1).astype(f32)
    ones = np.ones((128, 128), f32)
    bd1 = ((p[:, None] // 64) == (p[None, :] // 64)).astype(f32)
    bdm = bd1 / 64.0
    rott = np.zeros((128, 128), f32)
    for m in range(128):
        if (m % 32) < 16:
            rott[m + 16, m] = -1.0
        else:
            rott[m - 16, m] = 1.0
    msl = (i64[:, None] > j64[None, :]).astype(f32)
    msu = (i64[:, None] < j64[None, :]).astype(f32)
    mil = (i64[:, None] >= j64[None, :]).astype(f32)
    miu = (i64[:, None] <= j64[None, :]).astype(f32)
    resf = np.broadcast_to((t % 64 != 0).astype(f32)[None, :], (128, 1024))
    resb = np.broadcast_to((t % 64 != 63).astype(f32)[None, :], (128, 1024))
    cb = np.concatenate([ones, bd1, bdm, rott, ident, msl, msu, mil, miu, resf, resb], axis=1).astype(f32)
    assert cf.shape[1] == NCF and cb.shape[1] == NCB
    return np.ascontiguousarray(cf), np.ascontiguousarray(cb)


def _pack_params(inp):
    f32 = np.float32
    P = np.zeros((128, NPAR), f32)

    def put(name, arr):
        o0, n = _off[name]
        assert arr.shape == (128, n), (name, arr.shape, n)
        P[:, o0:o0 + n] = arr

    col = lambda v, n: np.asarray(v, f32).reshape(n, 128).T
    for l in range(DEPTH):
        put(f"bada{l}", col(inp["b_ada"][l], 24))
        put(f"normg{l}", col(inp["norm_g"][l], 8))
    put("finalg", col(inp["final_g"], 8))
    for p in range(2):
        put(f"mu{p}", col(inp["mu_shift"][p], 14))
        put(f"w0{p}", np.asarray(inp["w0"][p], f32).reshape(2, 4, 128).transpose(2, 0, 1).reshape(128, 8))
        put(f"a0{p}", np.asarray(inp["a0"][p], f32).reshape(2, 4, 128).transpose(2, 0, 1).reshape(128, 8))
        put(f"kk{p}", col(inp["k_k"][p], 4))
        put(f"ka{p}", col(inp["k_a"][p], 4))
        put(f"rk{p}", col(np.asarray(inp["r_k"][p]).reshape(512), 4))
        put(f"lng{p}", col(inp["lnx_g"][p], 4))
        put(f"lnb{p}", col(inp["lnx_b"][p], 4))
        put(f"cw{p}", np.asarray(inp["conv_w"][p], f32).reshape(3, 4, 128).transpose(2, 0, 1).reshape(128, 12))
        put(f"cb{p}", col(inp["conv_b"][p], 4))
        put(f"qg{p}", np.tile(np.asarray(inp["q_norm_g"][p], f32), 2).reshape(128, 1))
        put(f"kg{p}", np.tile(np.asarray(inp["k_norm_g"][p], f32), 2).reshape(128, 1))
    return P


_DBG = False


def kernel(**inp):
    f32 = np.float32
    inp = {k: np.asarray(v) for k, v in inp.items()}
    nc = build_program(dbg=_DBG)
    cf, cb = _consts()
    params = _pack_params(inp)
    shared = {
        "params": params, "consts_f": cf, "consts_b": cb,
        "w_ada": np.ascontiguousarray(inp["w_ada"], f32),
        "w_in_e": np.ascontiguousarray(inp["w_in_e"], f32),
        "w_out_e": np.ascontiguousarray(inp["w_out_e"], f32),
        "w_in_o": np.ascontiguousarray(inp["w_in_o"], f32),
        "w_out_o": np.ascontiguousarray(inp["w_out_o"], f32),
        "lora_w2": np.ascontiguousarray(inp["lora_w2"], f32).reshape(2, 128, 512),
        "lora_a2": np.ascontiguousarray(inp["lora_a2"], f32).reshape(2, 128, 512),
    }
    in_maps = []
    for i in range(8):
        cv = np.zeros((128, 16), f32)
        cv[:, 0:8] = inp["c_ctx"].astype(f32).reshape(8, 128).T
        cv[:, 8:16] = inp["c"][i].astype(f32).reshape(8, 128).T
        m = dict(shared)
        m["x_prompt"] = np.ascontiguousarray(inp["x_prompt"][4 * i:4 * i + 4], f32).reshape(1024, D)
        m["x_sample"] = np.ascontiguousarray(inp["x_sample"][i], f32)
        m["cvec"] = cv
        m["state"] = np.ascontiguousarray(inp["state_rwkv"][i], f32)
        m["cache_k"] = np.ascontiguousarray(inp["cache_k"][i], f32)
        m["cache_v"] = np.ascontiguousarray(inp["cache_v"][i], f32)
        in_maps.append(m)
    res = run_bass_kernel_spmd(nc, in_maps, core_ids=list(range(8)))
    R = res.results
    y_prompt = np.concatenate([r["y_prompt"].reshape(4, 256, D) for r in R], axis=0)
    y_sample = np.stack([r["y_sample"] for r in R], axis=0)
    new_state = np.concatenate([r["new_state"] for r in R], axis=0)
    new_k = np.concatenate([r["new_k"] for r in R], axis=0)
    new_v = np.concatenate([r["new_v"] for r in R], axis=0)
    if _DBG:
        kernel.dbg = [r["dbg"] for r in R]
    return (y_prompt.astype(f32), y_sample.astype(f32), new_state.astype(f32), new_k.astype(f32), new_v.astype(f32))
```

```python
import math
import os
import numpy as np
import concourse.bass as bass
import concourse.mybir as mybir
from concourse.bass_utils import run_bass_kernel_spmd

F32 = mybir.dt.float32
BF16 = mybir.dt.bfloat16
ALU = mybir.AluOpType
AF = mybir.ActivationFunctionType

D = 1024
NT = 1024
KC = 8
CH = 64
NCH = NT // CH
DEPTH = 4
EPS = 1e-6
GN_EPS = 64e-5
C0 = math.exp(-0.5)
EVEN_IN = 4352
ODD_IN = 2560
A_SHIFT = 1792

_off = {}
_n = 0


def _reg(name, n):
    global _n
    _off[name] = (_n, n)
    _n += n


for _l in range(DEPTH):
    _reg(f"bada{_l}", 24)
    _reg(f"normg{_l}", 8)
_reg("finalg", 8)
for _p in range(2):
    _reg(f"mu{_p}", 14)
    _reg(f"w0{_p}", 8)
    _reg(f"a0{_p}", 8)
    _reg(f"kk{_p}", 4)
    _reg(f"ka{_p}", 4)
    _reg(f"rk{_p}", 4)
    _reg(f"lng{_p}", 4)
    _reg(f"lnb{_p}", 4)
    _reg(f"cw{_p}", 12)
    _reg(f"cb{_p}", 4)
    _reg(f"qg{_p}", 1)
    _reg(f"kg{_p}", 1)
NPAR = _n
NCF = 128 + 512 + 2048
NCB = 4736


class _Stop(Exception):
    pass


def ck(n):
    if os.environ.get('K_STOP') == str(n):
        raise _Stop()


class Region:
    __slots__ = ("name", "w", "r", "dsem", "dcnt")

    def __init__(self, name):
        self.name = name
        self.w = None
        self.r = {}
        self.dsem = None
        self.dcnt = 0


class Sched:
    def __init__(self, nc, stack):
        self.nc = nc
        self.eng = {"pe": nc.tensor, "act": nc.scalar, "dve": nc.vector, "pool": nc.gpsimd, "sp": nc.sync}
        self.sem = {k: stack.enter_context(nc.semaphore("s_" + k)) for k in self.eng}
        self.cnt = {k: 0 for k in self.eng}
        self.seen = {k: {} for k in self.eng}
        self.stack = stack
        self.dsems = []
        self.ninstr = 0
        self.capture = None

    def record(self, fn):
        assert self.capture is None
        self.capture = []
        fn()
        q = self.capture
        self.capture = None
        return q

    def _emit(self, it):
        if it[0] == "op":
            self.op(*it[1:])
        else:
            self.dma(*it[1:])

    def replay(self, qa, qb=()):
        nb = 0
        for i, it in enumerate(qa):
            self._emit(it)
            want = ((i + 1) * len(qb)) // max(1, len(qa))
            while nb < want:
                self._emit(qb[nb])
                nb += 1
        while nb < len(qb):
            self._emit(qb[nb])
            nb += 1

    def region(self, name, dma=False):
        r = Region(name)
        if dma:
            r.dsem = self.stack.enter_context(self.nc.semaphore("d_" + name))
        return r

    def _wait(self, e, sem, val):
        if e == "pe" and sem is self.sem["pe"]:
            return
        key = sem.name
        for f_, s_ in self.sem.items():
            if s_ is sem:
                assert val <= self.cnt[f_], ("wait on pending (non-incrementing) op", e, f_, val, self.cnt[f_])
        if self.seen[e].get(key, 0) >= val:
            return
        self.seen[e][key] = val
        self.eng[e].wait_ge(sem, val)

    def _deps(self, e, reads, writes):
        for r in reads:
            if r.w is not None:
                self._wait(e, self.sem[r.w[0]], r.w[1])
            if r.dsem is not None and r.dcnt:
                self._wait(e, r.dsem, 16 * r.dcnt)
        for r in writes:
            if r.w is not None:
                self._wait(e, self.sem[r.w[0]], r.w[1])
            for k, c in r.r.items():
                self._wait(e, self.sem[k], c)
            if r.dsem is not None and r.dcnt:
                self._wait(e, r.dsem, 16 * r.dcnt)

    def op(self, e, ins_fn, reads=(), writes=(), inc=True):
        if self.capture is not None:
            self.capture.append(("op", e, ins_fn, tuple(reads), tuple(writes), inc))
            return
        self._deps(e, reads, writes)
        ins = ins_fn(self.eng[e])
        if inc:
            self.cnt[e] += 1
            c = self.cnt[e]
            ins.then_inc(self.sem[e], 1)
        else:
            assert e == "pe"
            c = self.cnt[e] + 1
        self.seen[e][self.sem[e].name] = max(self.seen[e].get(self.sem[e].name, 0), 0)
        for r in reads:
            r.r[e] = c
        for r in writes:
            r.w = (e, c)
            r.r = {}
        self.ninstr += 1

    def dma(self, q, out, in_, sb_region, sb_is_dst, extra_reads=()):
        r = sb_region
        if self.capture is not None:
            self.capture.append(("dma", q, out, in_, sb_region, sb_is_dst, tuple(extra_reads)))
            return
        if sb_is_dst:
            self._deps(q, extra_reads, [r])
        else:
            self._deps(q, [r] + list(extra_reads), [])
        ins = self.eng[q].dma_start(out=out, in_=in_)
        r.dcnt += 1
        ins.then_inc(r.dsem, 16)
        if sb_is_dst:
            r.w = None
            r.r = {}
        self.ninstr += 1

    def barrier(self):
        for e in self.eng:
            for f in self.eng:
                if f != e and self.cnt[f]:
                    self._wait(e, self.sem[f], self.cnt[f])

    def finish(self, regions):
        for r in regions:
            if r.dsem is not None and r.dcnt:
                self._wait("sp", r.dsem, 16 * r.dcnt)
        for f in self.eng:
            if f != "sp" and self.cnt[f]:
                self._wait("sp", self.sem[f], self.cnt[f])


class T:
    def __init__(self, S, stack, name, shape, dt, psum=False, dma=False):
        nc = S.nc
        self.t = stack.enter_context(nc.psum_tensor(name, shape, dt) if psum else nc.sbuf_tensor(name, shape, dt))
        self.r = S.region(name, dma=dma)

    def __getitem__(self, k):
        return self.t[k]


class View:
    def __init__(self, ap, r):
        self.ap = ap
        self.r = r

    def __getitem__(self, k):
        return self.ap[k]


def v3(ap, q):
    return ap.rearrange("p (q s) -> p q s", q=q)


def build_program(dbg=False):
    from contextlib import ExitStack
    nc = bass.Bass("TRN2", target_bir_lowering=False)
    dt = nc.dram_tensor
    xp_d = dt("x_prompt", [4 * 256, D], F32, kind="ExternalInput").ap()
    xs_d = dt("x_sample", [NT, D], F32, kind="ExternalInput").ap()
    cvec_d = dt("cvec", [128, 16], F32, kind="ExternalInput").ap()
    st_d = dt("state", [2, 2, 8, 64, 64], F32, kind="ExternalInput").ap()
    ck_d = dt("cache_k", [2, 4, 512, 64], F32, kind="ExternalInput").ap()
    cv_d = dt("cache_v", [2, 4, 512, 64], F32, kind="ExternalInput").ap()
    par_d = dt("params", [128, NPAR], F32, kind="ExternalInput").ap()
    wada_d = dt("w_ada", [DEPTH, D, 3 * D], F32, kind="ExternalInput").ap()
    wine_d = dt("w_in_e", [2, D, EVEN_IN], F32, kind="ExternalInput").ap()
    woute_d = dt("w_out_e", [2, D, D], F32, kind="ExternalInput").ap()
    wino_d = dt("w_in_o", [2, D, ODD_IN], F32, kind="ExternalInput").ap()
    wouto_d = dt("w_out_o", [2, D, D], F32, kind="ExternalInput").ap()
    lw2_d = dt("lora_w2", [2, 128, 512], F32, kind="ExternalInput").ap()
    la2_d = dt("lora_a2", [2, 128, 512], F32, kind="ExternalInput").ap()
    cf_d = dt("consts_f", [128, NCF], F32, kind="ExternalInput").ap()
    cb_d = dt("consts_b", [128, NCB], F32, kind="ExternalInput").ap()
    yp_d = dt("y_prompt", [4 * 256, D], F32, kind="ExternalOutput").ap()
    ys_d = dt("y_sample", [NT, D], F32, kind="ExternalOutput").ap()
    ns_d = dt("new_state", [4, 2, 2, 8, 64, 64], F32, kind="ExternalOutput").ap()
    nk_d = dt("new_k", [4, 2, 4, 256, 64], F32, kind="ExternalOutput").ap()
    nv_d = dt("new_v", [4, 2, 4, 256, 64], F32, kind="ExternalOutput").ap()
    dbg_d = dt("dbg", [DEPTH, 2, 128, KC * NT], F32, kind="ExternalOutput").ap() if dbg else None

    with ExitStack() as stack:
        S = Sched(nc, stack)

        def mk(name, shape, dtp, psum=False, dma=False):
            return T(S, stack, name, shape, dtp, psum=psum, dma=dma)

        XT = mk("XT", [128, KC, NT], F32, dma=True)
        HT = mk("HT", [128, KC, NT], BF16)
        YT = mk("YT", [128, KC, NT], BF16)
        NSLOT = 2
        WS = [mk(f"WS{i}", [128, KC, 512], BF16, dma=True) for i in range(NSLOT)]
        PAR = mk("PAR", [128, NPAR], F32, dma=True)
        CV = mk("CV", [128, 16], F32, dma=True)
        CF = mk("CF", [128, NCF], F32, dma=True)
        CB = mk("CB", [128, NCB], BF16)
        MOD = mk("MOD", [128, DEPTH * 2 * 24], F32)
        GSC = mk("GSC", [128, 8], F32)
        DER = mk("DER", [128, 64], F32)
        PS = [mk(f"PS{i}", [128, 512], F32, psum=True) for i in range(8)]
        NX = 12
        X = [mk(f"X{i}", [128, NT], F32, dma=True) for i in range(NX)]
        RSTD = X[9]
        NB = 23
        B = [mk(f"B{i}", [128, NT], BF16) for i in range(NB)]
        SMALL = mk("SMALL", [128, 256], F32)
        TST = mk("TST", [128, 512], F32, dma=True)
        allr = []

        ident = CF[:, 0:128]
        IDN8 = CF[:, 128:640]
        COS = CF[:, 640:1664]
        SIN = CF[:, 1664:2688]
        ONES = CB[:, 0:128]
        BD1 = CB[:, 128:256]
        BDM = CB[:, 256:384]
        ROTT = CB[:, 384:512]
        IDB = CB[:, 512:640]
        MSL = CB[:, 640:1152]
        MSU = CB[:, 1152:1664]
        MIL = CB[:, 1664:2176]
        MIU = CB[:, 2176:2688]
        RESF = CB[:, 2688:3712]
        RESB = CB[:, 3712:4736]

        def par(name, j=0, n=1):
            o0, _ = _off[name]
            return PAR[:, o0 + j:o0 + j + n]

        def mm(out_ap, lhsT, rhs, start, stop, rd, wr, tp=None, inc=None):
            if inc is None:
                inc = bool(stop)
            if tp is None:
                S.op("pe", lambda e: e.matmul(out_ap, lhsT=lhsT, rhs=rhs, start=start, stop=stop), rd, wr, inc)
            else:
                S.op("pe", lambda e: e.matmul(out_ap, lhsT=lhsT, rhs=rhs, start=start, stop=stop,
                                              tile_position=tp), rd, wr, inc)

        def tr(out_ap, in_ap, idn, rd, wr, tp=None):
            if tp is None:
                S.op("pe", lambda e: e.transpose(out=out_ap, in_=in_ap, identity=idn), rd, wr)
            else:
                S.op("pe", lambda e: e.transpose(out=out_ap, in_=in_ap, identity=idn, tile_position=tp), rd, wr)

        def act(out_ap, in_ap, func, rd, wr, bias=0.0, scale=1.0):
            S.op("act", lambda e: e.activation(out=out_ap, in_=in_ap, func=func, bias=bias, scale=scale), rd, wr)

        def tt(out_ap, a, b, op, rd, wr, e="dve"):
            S.op(e, lambda en: en.tensor_tensor(out=out_ap, in0=a, in1=b, op=op), rd, wr)

        def ts(out_ap, a, s1, s2, op0, op1, rd, wr, e="dve"):
            if op1 is None:
                S.op(e, lambda en: en.tensor_scalar(out=out_ap, in0=a, scalar1=s1, scalar2=None, op0=op0), rd, wr)
            else:
                S.op(e, lambda en: en.tensor_scalar(out=out_ap, in0=a, scalar1=s1, scalar2=s2, op0=op0, op1=op1),
                     rd, wr)

        def stt(out_ap, a, s, b, op0, op1, rd, wr, e="dve"):
            S.op(e, lambda en: en.scalar_tensor_tensor(out=out_ap, in0=a, scalar=s, in1=b, op0=op0, op1=op1), rd, wr)

        def cp(out_ap, in_ap, rd, wr, e="dve"):
            if e == "act":
                S.op("act", lambda en: en.copy(out=out_ap, in_=in_ap), rd, wr)
            else:
                S.op(e, lambda en: en.tensor_copy(out=out_ap, in_=in_ap), rd, wr)

        def recip(out_ap, in_ap, rd, wr):
            S.op("dve", lambda en: en.reciprocal(out=out_ap, in_=in_ap), rd, wr)

        def rsqrt_from(out_t, out_ap, in_ap, in_regs, scale, bias_ap):
            act(out_ap, in_ap, AF.Sqrt, in_regs, [out_t.r], bias=bias_ap, scale=scale)
            recip(out_ap, out_ap, [out_t.r], [out_t.r])

        S.dma("sp", PAR[:, :], par_d[:, :], PAR.r, True)
        S.dma("sp", CV[:, :], cvec_d[:, :], CV.r, True)
        S.dma("sp", CF[:, :], cf_d[:, :], CF.r, True)
        for i in range(0, NCB, 1024):
            n = min(1024, NCB - i)
            xi = X[(i // 1024) % 4]
            S.dma("sp", xi[:, 0:n], cb_d[:, i:i + n], xi.r, True)
            cp(CB[:, i:i + n], xi[:, 0:n], [xi.r], [CB.r])
        S.op("dve", lambda en: en.memset(SMALL[:, 0:1], EPS), [], [SMALL.r])
        S.op("dve", lambda en: en.memset(SMALL[:, 1:2], GN_EPS), [], [SMALL.r])
        S.op("dve", lambda en: en.memset(SMALL[:, 2:3], D * EPS), [], [SMALL.r])
        EPSC = SMALL[:, 0:1]
        GNEPSC = SMALL[:, 1:2]
        DEPSC = SMALL[:, 2:3]

        wq = []
        wstate = {"issued": 0}

        def wview(wd, c0, n):
            return wd.rearrange("(kc p) f -> p kc f", p=128)[:, :, c0:c0 + n]

        def w_issue_upto(i):
            while wstate["issued"] <= min(i, len(wq) - 1):
                u = wstate["issued"]
                slot = WS[u % NSLOT]
                for (dc, n, src) in wq[u]:
                    S.dma("pool", slot[:, :, dc:dc + n], src, slot.r, True)
                wstate["issued"] += 1

        def w_get(i):
            w_issue_upto(i + NSLOT - 1)
            return WS[i % NSLOT]

        plan = []
        for l in range(DEPTH):
            for u in range(6):
                wq.append([(0, 512, wview(wada_d[l], u * 512, 512))])
                plan.append(("ada", l, u))
        LAYERS = [l for l in range(int(os.environ.get('K_LAYERS', DEPTH)))
                  if not os.environ.get('K_ONLY') or str(l) in os.environ['K_ONLY']]
        for g in range(2):
            for l in LAYERS:
                p = l // 2
                if l % 2 == 0:
                    w = wine_d[p]
                    wq.append([(0, 256, wview(w, 1536, 256))])
                    plan.append(("e_lora", g, l))
                    for hp in range(4):
                        wq.append([(0, 128, wview(w, hp * 128, 128)), (128, 128, wview(w, 512 + hp * 128, 128)),
                                   (256, 128, wview(w, 1024 + hp * 128, 128)),
                                   (384, 128, wview(w, A_SHIFT + hp * 128, 128))])
                        plan.append(("e_hp", g, l, hp))
                    for cc in range(4):
                        base = A_SHIFT + 512
                        wq.append([(j * 128, 128, wview(w, base + j * 512 + cc * 128, 128)) for j in range(4)])
                        plan.append(("e_b", g, l, cc))
                    for u in range(2):
                        wq.append([(0, 512, wview(woute_d[p], u * 512, 512))])
                        plan.append(("out", g, l, u))
                else:
                    w = wino_d[p]
                    wq.append([(j * 128 + h2 * 64, 64, wview(w, 1024 + j * 64, 64)) for j in range(4) for h2 in range(2)])
                    plan.append(("o_k", g, l))
                    wq.append([(0, 256, wview(w, 1280, 256))])
                    plan.append(("o_v", g, l))
                    for qq in range(4):
                        wq.append([(0, 128, wview(w, (2 * qq) * 128, 128)), (128, 128, wview(w, 1536 + (2 * qq) * 128, 128)),
                                   (256, 128, wview(w, (2 * qq + 1) * 128, 128)),
                                   (384, 128, wview(w, 1536 + (2 * qq + 1) * 128, 128))])
                        plan.append(("o_q", g, l, qq))
                    for u in range(2):
                        wq.append([(0, 512, wview(wouto_d[p], u * 512, 512))])
                        plan.append(("out", g, l, u))
        wi = {"i": 0}

        def next_w(kind):
            i = wi["i"]
            assert plan[i][0] == kind, (plan[i], kind)
            wi["i"] += 1
            return w_get(i)

        SC = B[0]
        act(X[0][:, 0:16], CV[:, :], AF.Sigmoid, [CV.r], [X[0].r])
        tt(SC[:, 0:16], X[0][:, 0:16], CV[:, :], ALU.mult, [X[0].r, CV.r], [SC.r])
        scv = SC[:, 0:16].rearrange("p (g k) -> p g k", g=2)
        for l in range(DEPTH):
            pm = PS[l % 2]
            for u in range(6):
                slot = next_w("ada")
                for jj in range(4):
                    j = u * 4 + jj
                    for kc in range(KC):
                        mm(pm[:, 2 * j:2 * j + 2], slot[:, kc, jj * 128:(jj + 1) * 128], scv[:, :, kc],
                           kc == 0, kc == KC - 1, [slot.r, SC.r], [pm.r])
            pv = pm[:, 0:48].rearrange("p (j g) -> p g j", g=2)
            for g in range(2):
                o0 = (l * 2 + g) * 24
                tt(MOD[:, o0:o0 + 24], pv[:, g, :], par(f"bada{l}", 0, 24), ALU.add, [pm.r, PAR.r], [MOD.r])

        def rms_and_modulate(g, l):
            o0 = (l * 2 + g) * 24
            stt(GSC[:, :], MOD[:, o0 + 8:o0 + 16], 1.0, par(f"normg{l}", 0, 8), ALU.add, ALU.mult,
                [MOD.r, PAR.r], [GSC.r])
            ts(GSC[:, :], GSC[:, :], 32.0, None, ALU.mult, None, [GSC.r], [GSC.r])
            sqs = [B[15], B[10]]
            for tb in range(2):
                tsl = slice(tb * 512, (tb + 1) * 512)
                pm = PS[tb]
                for kc in range(KC):
                    sq = sqs[kc % 2]
                    act(sq[:, tsl], XT[:, kc, tsl], AF.Square, [XT.r], [sq.r])
                    mm(pm[:, :], ONES, sq[:, tsl], kc == 0, kc == KC - 1, [CB.r, sq.r], [pm.r], inc=True)
                rsqrt_from(RSTD, RSTD[:, tsl], pm[:, :], [pm.r, SMALL.r], 1.0, DEPSC)
            for kc in range(KC):
                tmp = X[11] if kc % 2 == 0 else X[10]
                tt(tmp[:, :], XT[:, kc, :], RSTD[:, :], ALU.mult, [XT.r, RSTD.r], [tmp.r])
                act(HT[:, kc, :], tmp[:, :], AF.Identity, [tmp.r, GSC.r, MOD.r], [HT.r],
                    bias=MOD[:, o0 + kc:o0 + kc + 1], scale=GSC[:, kc:kc + 1])

        def proj_row(slot, c0, pm_pair):
            for tb in range(2):
                pm = pm_pair[tb]
                for kc in range(KC):
                    mm(pm[:, :], slot[:, kc, c0:c0 + 128], HT[:, kc, tb * 512:(tb + 1) * 512],
                       kc == 0, kc == KC - 1, [slot.r, HT.r], [pm.r])

        def out_proj(g, l):
            o0 = (l * 2 + g) * 24
            k = 0
            for u in range(2):
                slot = next_w("out")
                for dj in range(4):
                    dc = u * 4 + dj
                    for tb in range(2):
                        pm = PS[k % 2]
                        k += 1
                        tsl = slice(tb * 512, (tb + 1) * 512)
                        for fc in range(KC):
                            mm(pm[:, :], slot[:, fc, dj * 128:(dj + 1) * 128], YT[:, fc, tsl],
                               fc == 0, fc == KC - 1, [slot.r, YT.r], [pm.r])
                        otmp = X[10 + (k % 2)]
                        cp(otmp[:, 0:512], pm[:, :], [pm.r], [otmp.r], e="act")
                        stt(XT[:, dc, tsl], otmp[:, 0:512], MOD[:, o0 + 16 + dc:o0 + 17 + dc], XT[:, dc, tsl],
                            ALU.mult, ALU.add, [otmp.r, MOD.r, XT.r], [XT.r])

        def even_layer(g, l):
            p = l // 2
            nseq = 4 if g == 0 else 1
            slen = NT // nseq
            cps = slen // CH
            ck(0)
            mu = par(f"mu{p}", 0, 14)
            OM = DER[:, 0:14]
            HM = DER[:, 14:28]
            ts(OM, mu, -1.0, 1.0, ALU.mult, ALU.add, [PAR.r], [DER.r])
            ts(HM, mu, 0.5, None, ALU.mult, None, [PAR.r], [DER.r])
            OKA = DER[:, 28:32]
            ts(OKA, par(f"ka{p}", 0, 4), -1.0, 1.0, ALU.mult, ALU.add, [PAR.r], [DER.r])
            HRK = DER[:, 32:36]
            ts(HRK, par(f"rk{p}", 0, 4), 0.5, None, ALU.mult, None, [PAR.r], [DER.r])
            LW2 = B[14]
            S.dma("sp", X[0][:, 0:512], lw2_d[p], X[0].r, True)
            S.dma("sp", X[0][:, 512:1024], la2_d[p], X[0].r, True)
            cp(LW2[:, :], X[0][:, :], [X[0].r], [LW2.r])

            def shift_row(pm_pair, row, out_t, out_ap, fin=None, Fr=None):
                Fr = X[10] if Fr is None else Fr
                FE = X[11] if fin is not None else out_t
                FEap = X[11][:, :] if fin is not None else out_ap
                for tb in range(2):
                    tsl = slice(tb * 512, (tb + 1) * 512)
                    cp(Fr[:, tsl], pm_pair[tb][:, :], [pm_pair[tb].r], [Fr.r], e="act")
                    ts(FEap[:, tsl], Fr[:, tsl], OM[:, row:row + 1], None, ALU.mult, None,
                       [Fr.r, DER.r], [FE.r])
                F3 = v3(Fr[:, :], nseq)
                E3 = v3(FEap, nseq)
                stt(E3[:, :, 1:slen], F3[:, :, 0:slen - 1], HM[:, row:row + 1], E3[:, :, 1:slen], ALU.mult, ALU.add,
                    [Fr.r, DER.r, FE.r], [FE.r])
                stt(E3[:, :, 0:slen - 1], F3[:, :, 1:slen], HM[:, row:row + 1], E3[:, :, 0:slen - 1], ALU.mult, ALU.add,
                    [Fr.r, DER.r, FE.r], [FE.r])
                if fin is not None:
                    fin(FE)

            ck('a')
            slot = next_w("e_lora")
            LWT = B[12]
            LAT = B[13]
            proj_row(slot, 0, PS[0:2])
            ck('b')
            shift_row(PS[0:2], 12, None, None,
                      fin=lambda FE: act(LWT[:, :], FE[:, :], AF.Tanh, [FE.r], [LWT.r]))
            proj_row(slot, 128, PS[0:2])
            shift_row(PS[0:2], 13, None, None, fin=lambda FE: cp(LAT[:, :], FE[:, :], [FE.r], [LAT.r]))

            ck(1)
            for hp in range(4):
                slot = next_w("e_hp")
                R, K, V = X[0], X[1], X[2]
                proj_row(slot, 0, PS[0:2])
                proj_row(slot, 128, PS[2:4])
                shift_row(PS[0:2], hp, R, R[:, :], Fr=X[10])
                proj_row(slot, 256, PS[0:2])
                shift_row(PS[2:4], 4 + hp, K, K[:, :], Fr=X[11])
                GA = B[11]
                proj_row(slot, 384, PS[2:4])
                shift_row(PS[0:2], 8 + hp, V, V[:, :], Fr=X[10])
                for tb in range(2):
                    tsl = slice(tb * 512, (tb + 1) * 512)
                    act(X[11][:, tsl], PS[2 + tb][:, :], AF.Sigmoid, [PS[2 + tb].r], [X[11].r])
                    tt(GA[:, tsl], PS[2 + tb][:, :], X[11][:, tsl], ALU.mult, [PS[2 + tb].r, X[11].r], [GA.r])
                ck(2)
                KKN = X[3]
                ts(KKN[:, :], K[:, :], par(f"kk{p}", hp, 1), None, ALU.mult, None, [K.r, PAR.r], [KKN.r])
                sq = B[15]
                act(sq[:, :], KKN[:, :], AF.Square, [KKN.r], [sq.r])
                for tb in range(2):
                    tsl = slice(tb * 512, (tb + 1) * 512)
                    mm(PS[tb][:, :], BD1, sq[:, tsl], True, True, [CB.r, sq.r], [PS[tb].r])
                    act(X[10][:, tsl], PS[tb][:, :], AF.Sqrt, [PS[tb].r], [X[10].r])
                    ts(X[10][:, tsl], X[10][:, tsl], 1e-12, None, ALU.max, None, [X[10].r], [X[10].r])
                recip(X[10][:, :], X[10][:, :], [X[10].r], [X[10].r])
                tt(KKN[:, :], KKN[:, :], X[10][:, :], ALU.mult, [KKN.r, X[10].r], [KKN.r])
                Vb = B[10]
                cp(Vb[:, :], V[:, :], [V.r], [Vb.r])
                Vt = B[9]

                def to_tok(src, dst):
                    for half in range(2):
                        pm = PS[2 + half]
                        for cc in range(8):
                            c = half * 8 + cc
                            for hh in range(2):
                                sl = slice(hh * 64, hh * 64 + 64)
                                mm(pm[sl, cc * 64:(cc + 1) * 64], src[sl, c * 64:(c + 1) * 64],
                                   IDB[sl, hh * 64:hh * 64 + 64], True, True, [src.r, CB.r], [pm.r],
                                   tp=(hh * 64, hh * 64), inc=(cc == 7 and hh == 1))
                        cp(dst[:, half * 512:(half + 1) * 512], pm[:, :], [pm.r], [dst.r], e="act")
                ck(3)
                to_tok(Vb, Vt)
                ck(4)

                KS = X[4]
                YA = X[5]
                S.op("dve", lambda en: en.memset(YA[:, :], 0.0), [], [YA.r])
                SETS = [dict(RT=B[0], KK=B[3], KTt=B[4], BTt=B[5], AkT=B[6], QbT=B[7], QkT=B[8], NIT=B[15]),
                        dict(RT=B[16], KK=B[17], KTt=B[18], BTt=B[19], AkT=B[20], QbT=B[10], QkT=B[21], NIT=B[22])]
                SEQB = [dict(T32=TST, TBF=[mk_tb[0], mk_tb[1]], XS=mk_xs[0], NU=mk_xs[1], TW=X[10], PTS=X[11],
                             banks=(PS[2], PS[3], PS[5], PS[6])),
                        dict(T32=TST1, TBF=[mk_tb[2], mk_tb[3]], XS=mk_xs[2], NU=mk_xs[3], TW=X[8], PTS=X[9],
                             banks=(PS[4], PS[7], PS[0], PS[1]))]

                def prep_local(d):
                    st = SETS[d]
                    SG, L, E1, E2, A_, TMP = X[6], X[7], X[8], X[9], X[10], X[11]
                    dsl = slice(d * 64, d * 64 + 64)
                    for tb in range(2):
                        tsl = slice(tb * 512, (tb + 1) * 512)
                        mm(PS[tb][:, :], LW2[dsl, hp * 128:(hp + 1) * 128], LWT[dsl, tsl], True, True,
                           [LW2.r, LWT.r], [PS[tb].r], tp=(d * 64, 0))
                        act(SG[:, tsl], PS[tb][:, :], AF.Sigmoid, [PS[tb].r, PAR.r], [SG.r],
                            bias=par(f"w0{p}", d * 4 + hp, 1))
                    if d == 0:
                        S.op("dve", lambda en: en.tensor_tensor_scan(out=L[:, :], data0=RESF, data1=SG[:, :],
                                                                     initial=0.0, op0=ALU.mult, op1=ALU.add),
                             [CF.r, SG.r], [L.r])
                    else:
                        S.op("dve", lambda en: en.tensor_tensor_scan(out=L[:, ::-1], data0=RESB[:, ::-1],
                                                                     data1=SG[:, ::-1], initial=0.0,
                                                                     op0=ALU.mult, op1=ALU.add),
                             [CF.r, SG.r], [L.r])
                    for tb in range(2):
                        tsl = slice(tb * 512, (tb + 1) * 512)
                        mm(PS[tb][:, :], LW2[dsl, 512 + hp * 128:512 + (hp + 1) * 128], LAT[dsl, tsl], True, True,
                           [LW2.r, LAT.r], [PS[tb].r], tp=(d * 64, 0))
                        act(A_[:, tsl], PS[tb][:, :], AF.Sigmoid, [PS[tb].r, PAR.r], [A_.r],
                            bias=par(f"a0{p}", d * 4 + hp, 1))
                    RTd, KTd, BTd, KKTd = st["RT"], B[1], B[2], st["KK"]
                    act(E1[:, :], L[:, :], AF.Exp, [L.r], [E1.r], scale=-C0)
                    tt(RTd[:, :], R[:, :], E1[:, :], ALU.mult, [R.r, E1.r], [RTd.r])
                    WC = SMALL[:, 16 + d * 16:32 + d * 16]
                    e3 = E1[:, :].rearrange("p (c t) -> p c t", t=CH)
                    cp(WC, e3[:, :, CH - 1] if d == 0 else e3[:, :, 0], [E1.r], [SMALL.r])
                    act(E2[:, :], L[:, :], AF.Exp, [L.r], [E2.r], scale=C0)
                    ts(TMP[:, :], A_[:, :], par(f"ka{p}", hp, 1), OKA[:, hp:hp + 1], ALU.mult, ALU.add,
                       [A_.r, PAR.r, DER.r], [TMP.r])
                    tt(TMP[:, :], TMP[:, :], K[:, :], ALU.mult, [TMP.r, K.r], [TMP.r])
                    if d == 0:
                        cp(KS[:, :], TMP[:, :], [TMP.r], [KS.r])
                    else:
                        tt(KS[:, :], KS[:, :], TMP[:, :], ALU.add, [KS.r, TMP.r], [KS.r])
                    tt(KTd[:, :], TMP[:, :], E2[:, :], ALU.mult, [TMP.r, E2.r], [KTd.r])
                    tt(TMP[:, :], A_[:, :], KKN[:, :], ALU.mult, [A_.r, KKN.r], [TMP.r])
                    stt(BTd[:, :], TMP[:, :], -1.0, E2[:, :], ALU.mult, ALU.mult, [TMP.r, E2.r], [BTd.r])
                    tt(TMP[:, :], L[:, :], SG[:, :], ALU.subtract, [L.r, SG.r], [TMP.r])
                    act(TMP[:, :], TMP[:, :], AF.Exp, [TMP.r], [TMP.r], scale=-C0)
                    tt(KKTd[:, :], KKN[:, :], TMP[:, :], ALU.mult, [KKN.r, TMP.r], [KKTd.r])
                    KTt, BTt = st["KTt"], st["BTt"]
                    to_tok(KTd, KTt)
                    to_tok(BTd, BTt)
                    AkT, QbT, QkT, NIT = st["AkT"], st["QbT"], st["QkT"], st["NIT"]
                    mA, mB, mI = (MSL, MSU, MIU) if d == 0 else (MSU, MSL, MIL)
                    if g == 0:
                        A0, B0, N0 = X[6], X[7], X[8]
                        A1, B1 = X[9], X[10]
                    elif d == 1:
                        vws = []
                        for xt_ in (X[6], X[7], X[8]):
                            ab = xt_[:, :].bitcast(BF16)
                            vws.append(View(ab[:, 0:1024], xt_.r))
                            vws.append(View(ab[:, 1024:2048], xt_.r))
                        A0, B0, N0, A1, B1 = vws[0], vws[2], vws[4], vws[1], vws[3]
                    else:
                        A0, B0, N0 = B[16], B[17], B[18]
                        A1, B1 = B[19], B[20]

                    def scores(lh, rh, pm, half):
                        for cc in range(8):
                            c = half * 8 + cc
                            for hh in range(2):
                                sl = slice(hh * 64, hh * 64 + 64)
                                mm(pm[sl, cc * 64:(cc + 1) * 64], lh[sl, c * 64:(c + 1) * 64],
                                   rh[sl, c * 64:(c + 1) * 64], True, True, [lh.r, rh.r], [pm.r],
                                   tp=(hh * 64, hh * 64), inc=(cc == 7 and hh == 1))
                    for half in range(2):
                        hs = slice(half * 512, (half + 1) * 512)
                        scores(KKTd, BTd, PS[2], half)
                        tt(A0[:, hs], PS[2][:, :], mA, ALU.mult, [PS[2].r, CF.r], [A0.r])
                        scores(BTd, KKTd, PS[3], half)
                        tt(B0[:, hs], PS[3][:, :], mB, ALU.mult, [PS[3].r, CF.r], [B0.r])
                        tt(N0[:, hs], IDN8, B0[:, hs], ALU.add, [CF.r, B0.r], [N0.r])
                        scores(KTd, KKTd, PS[4], half)
                        tt(AkT[:, hs], PS[4][:, :], mB, ALU.mult, [PS[4].r, CF.r], [AkT.r])
                        scores(BTd, RTd, PS[5], half)
                        tt(QbT[:, hs], PS[5][:, :], mI, ALU.mult, [PS[5].r, CF.r], [QbT.r])
                        scores(KTd, RTd, PS[6], half)
                        tt(QkT[:, hs], PS[6][:, :], mI, ALU.mult, [PS[6].r, CF.r], [QkT.r])
                    Ap, Bp, An, Bn = A0, B0, A1, B1
                    for lev in range(1, 6):
                        for half in range(2):
                            hs = slice(half * 512, (half + 1) * 512)
                            pa = PS[2] if half == 0 else PS[5]
                            pb = PS[3] if half == 0 else PS[6]
                            pn = PS[4]
                            for cc in range(8):
                                c = half * 8 + cc
                                cs = slice(c * 64, (c + 1) * 64)
                                for hh in range(2):
                                    sl = slice(hh * 64, hh * 64 + 64)
                                    mm(pa[sl, cc * 64:(cc + 1) * 64], Bp[sl, cs], Ap[sl, cs], True, True,
                                       [Ap.r, Bp.r], [pa.r], tp=(hh * 64, hh * 64), inc=(cc == 7 and hh == 1))
                            cp(An[:, hs], pa[:, :], [pa.r], [An.r], e="act")
                            if lev < 5:
                                for cc in range(8):
                                    c = half * 8 + cc
                                    cs = slice(c * 64, (c + 1) * 64)
                                    for hh in range(2):
                                        sl = slice(hh * 64, hh * 64 + 64)
                                        mm(pb[sl, cc * 64:(cc + 1) * 64], Ap[sl, cs], Bp[sl, cs], True, True,
                                           [Ap.r, Bp.r], [pb.r], tp=(hh * 64, hh * 64), inc=(cc == 7 and hh == 1))
                                cp(Bn[:, hs], pb[:, :], [pb.r], [Bn.r], e="act")
                            for cc in range(8):
                                c = half * 8 + cc
                                cs = slice(c * 64, (c + 1) * 64)
                                for hh in range(2):
                                    sl = slice(hh * 64, hh * 64 + 64)
                                    mm(pn[sl, cc * 64:(cc + 1) * 64], An[sl, cs], N0[sl, cs], True, True,
                                       [An.r, N0.r], [pn.r], tp=(hh * 64, hh * 64), inc=(cc == 7 and hh == 1))
                            tt(N0[:, hs], pn[:, :], N0[:, hs], ALU.add, [pn.r, N0.r], [N0.r])
                        Ap, Bp, An, Bn = An, Bn, Ap, Bp
                    cp(NIT[:, :], N0[:, :], [N0.r], [NIT.r])

                def init_state(d):
                    sb_ = SEQB[d]
                    T32 = sb_["T32"]
                    if g == 0:
                        S.op("dve", lambda en: en.memset(T32[:, 0:nseq * 64], 0.0), [], [T32.r])
                    else:
                        for hh in range(2):
                            S.dma("sp", TST[hh * 64:hh * 64 + 64, 256:320], st_d[p, d, hp * 2 + hh], TST.r, True)
                        for hh in range(2):
                            sl = slice(hh * 64, hh * 64 + 64)
                            mm(PS[4][sl, 0:64], TST[sl, 256:320], CF[sl, hh * 64:hh * 64 + 64], True, True,
                               [TST.r, CF.r], [PS[4].r], tp=(hh * 64, hh * 64))
                        cp(T32[:, 0:64], PS[4][:, 0:64], [PS[4].r], [T32.r])
                    cp(sb_["TBF"][0][:, 0:nseq * 64], T32[:, 0:nseq * 64], [T32.r], [sb_["TBF"][0].r])

                def seq(d):
                    st, sb_ = SETS[d], SEQB[d]
                    RTd, KKTd, KTt, BTt = st["RT"], st["KK"], st["KTt"], st["BTt"]
                    AkT, QbT, QkT, NIT = st["AkT"], st["QbT"], st["QkT"], st["NIT"]
                    T32, TBF, XS, NU, TW, PTS = sb_["T32"], sb_["TBF"], sb_["XS"], sb_["NU"], sb_["TW"], sb_["PTS"]
                    px, pu, pt, py = sb_["banks"]
                    WC = SMALL[:, 16 + d * 16:32 + d * 16]
                    ya3 = YA[:, :].rearrange("p (q j t) -> p q j t", q=nseq, j=cps)
                    for j in range(cps):
                        cj = j if d == 0 else cps - 1 - j
                        tcur, tnxt = TBF[j % 2], TBF[(j + 1) % 2]
                        for q in range(nseq):
                            c = q * cps + cj
                            cs = slice(c * 64, (c + 1) * 64)
                            qs = slice(q * 64, (q + 1) * 64)
                            for hh in range(2):
                                sl = slice(hh * 64, hh * 64 + 64)
                                tp = (hh * 64, hh * 64)
                                mm(px[sl, qs], KKTd[sl, cs], tcur[sl, qs], True, False, [KKTd.r, tcur.r], [px.r], tp=tp)
                                mm(px[sl, qs], AkT[sl, cs], Vt[sl, cs], False, True, [AkT.r, Vt.r], [px.r], tp=tp,
                                   inc=(q == nseq - 1 and hh == 1))
                        cp(XS[:, 0:nseq * 64], px[:, 0:nseq * 64], [px.r], [XS.r])
                        for q in range(nseq):
                            c = q * cps + cj
                            cs = slice(c * 64, (c + 1) * 64)
                            qs = slice(q * 64, (q + 1) * 64)
                            for hh in range(2):
                                sl = slice(hh * 64, hh * 64 + 64)
                                mm(pu[sl, qs], NIT[sl, cs], XS[sl, qs], True, True, [NIT.r, XS.r], [pu.r],
                                   tp=(hh * 64, hh * 64), inc=(q == nseq - 1 and hh == 1))
                        cp(NU[:, 0:nseq * 64], pu[:, 0:nseq * 64], [pu.r], [NU.r])
                        for q in range(nseq):
                            c = q * cps + cj
                            cs = slice(c * 64, (c + 1) * 64)
                            qs = slice(q * 64, (q + 1) * 64)
                            for hh in range(2):
                                sl = slice(hh * 64, hh * 64 + 64)
                                tp = (hh * 64, hh * 64)
                                mm(pt[sl, qs], BTt[sl, cs], NU[sl, qs], True, False, [BTt.r, NU.r], [pt.r], tp=tp)
                                mm(pt[sl, qs], KTt[sl, cs], Vt[sl, cs], False, True, [KTt.r, Vt.r], [pt.r], tp=tp,
                                   inc=False)
                                mm(py[sl, qs], tcur[sl, qs], RTd[sl, cs], True, False, [tcur.r, RTd.r], [py.r], tp=tp)
                                mm(py[sl, qs], NU[sl, qs], QbT[sl, cs], False, False, [NU.r, QbT.r], [py.r], tp=tp)
                                mm(py[sl, qs], Vt[sl, cs], QkT[sl, cs], False, True, [Vt.r, QkT.r], [py.r], tp=tp,
                                   inc=(q == nseq - 1 and hh == 1))
                        pyv = py[:, 0:nseq * 64].rearrange("p (q t) -> p q t", q=nseq)
                        tt(ya3[:, :, cj, :], pyv, ya3[:, :, cj, :], ALU.add, [py.r, YA.r], [YA.r])
                        cp(PTS[:, 0:nseq * 64], pt[:, 0:nseq * 64], [pt.r], [PTS.r])
                        for q in range(nseq):
                            c = q * cps + cj
                            qs = slice(q * 64, (q + 1) * 64)
                            ts(TW[:, qs], T32[:, qs], WC[:, c:c + 1], None, ALU.mult, None, [T32.r, SMALL.r], [TW.r])
                            stt(tnxt[:, qs], PTS[:, qs], WC[:, c:c + 1], TW[:, qs], ALU.mult, ALU.add,
                                [PTS.r, SMALL.r, TW.r], [tnxt.r])
                            stt(T32[:, qs], PTS[:, qs], WC[:, c:c + 1], TW[:, qs], ALU.mult, ALU.add,
                                [PTS.r, SMALL.r, TW.r], [T32.r])

                for d in range(2):
                    prep_local(d)
                    init_state(d)
                qf = S.record(lambda: seq(0))
                qb_ = S.record(lambda: seq(1))
                S.replay(qf, qb_)
                if g == 0:
                    for d in range(2):
                        T32 = SEQB[d]["T32"]
                        for q in range(nseq):
                            for hh in range(2):
                                sl = slice(hh * 64, hh * 64 + 64)
                                mm(PS[4][sl, q * 64:(q + 1) * 64], T32[sl, q * 64:(q + 1) * 64],
                                   CF[sl, hh * 64:hh * 64 + 64], True, True, [T32.r, CF.r], [PS[4].r],
                                   tp=(hh * 64, hh * 64))
                        cp(TST[:, 256:512], PS[4][:, 0:256], [PS[4].r], [TST.r])
                        for q in range(nseq):
                            for hh in range(2):
                                S.dma("sp", ns_d[q, p, d, hp * 2 + hh],
                                      TST[hh * 64:hh * 64 + 64, 256 + q * 64:256 + (q + 1) * 64], TST.r, False)
                ck(8)
                Yb = B[0]
                cp(Yb[:, :], YA[:, :], [YA.r], [Yb.r], e="act")
                DD, RS, BON = X[6], X[7], X[8]
                sq = B[1]
                for tb in range(2):
                    tsl = slice(tb * 512, (tb + 1) * 512)
                    mm(PS[tb][:, :], BDM, Yb[:, tsl], True, True, [CB.r, Yb.r], [PS[tb].r])
                    tt(DD[:, tsl], YA[:, tsl], PS[tb][:, :], ALU.subtract, [YA.r, PS[tb].r], [DD.r])
                act(sq[:, :], DD[:, :], AF.Square, [DD.r], [sq.r])
                for tb in range(2):
                    tsl = slice(tb * 512, (tb + 1) * 512)
                    mm(PS[tb][:, :], BDM, sq[:, tsl], True, True, [CB.r, sq.r], [PS[tb].r])
                    rsqrt_from(RS, RS[:, tsl], PS[tb][:, :], [PS[tb].r, SMALL.r], 1.0, GNEPSC)
                tt(DD[:, :], DD[:, :], RS[:, :], ALU.mult, [DD.r, RS.r], [DD.r])
                ts(DD[:, :], DD[:, :], par(f"lng{p}", hp, 1), par(f"lnb{p}", hp, 1), ALU.mult, ALU.add,
                   [DD.r, PAR.r], [DD.r])
                rkb = B[2]
                stt(rkb[:, :], KS[:, :], HRK[:, hp:hp + 1], R[:, :], ALU.mult, ALU.mult, [KS.r, DER.r, R.r], [rkb.r])
                for tb in range(2):
                    tsl = slice(tb * 512, (tb + 1) * 512)
                    mm(PS[tb][:, :], BD1, rkb[:, tsl], True, True, [CB.r, rkb.r], [PS[tb].r])
                    tt(BON[:, tsl], PS[tb][:, :], V[:, tsl], ALU.mult, [PS[tb].r, V.r], [BON.r])
                tt(DD[:, :], DD[:, :], BON[:, :], ALU.add, [DD.r, BON.r], [DD.r])
                tt(YT[:, hp, :], DD[:, :], GA[:, :], ALU.mult, [DD.r, GA.r], [YT.r])

            ck(9)
            for cc in range(4):
                slot = next_w("e_b")
                BG, CG, U_, CO, SGm = X[0], X[1], X[2], X[3], X[4]
                proj_row(slot, 0, PS[0:2])
                proj_row(slot, 128, PS[2:4])
                for tb in range(2):
                    cp(BG[:, tb * 512:(tb + 1) * 512], PS[tb][:, :], [PS[tb].r], [BG.r], e="act")
                proj_row(slot, 256, PS[0:2])
                for tb in range(2):
                    cp(CG[:, tb * 512:(tb + 1) * 512], PS[2 + tb][:, :], [PS[2 + tb].r], [CG.r], e="act")
                proj_row(slot, 384, PS[2:4])
                for tb in range(2):
                    tsl = slice(tb * 512, (tb + 1) * 512)
                    tt(U_[:, tsl], PS[tb][:, :], CG[:, tsl], ALU.mult, [PS[tb].r, CG.r], [U_.r])
                cw = lambda tap: par(f"cw{p}", tap * 4 + cc, 1)
                ts(CO[:, :], U_[:, :], cw(1), par(f"cb{p}", cc, 1), ALU.mult, ALU.add, [U_.r, PAR.r], [CO.r])
                U3 = v3(U_[:, :], nseq)
                C3 = v3(CO[:, :], nseq)
                stt(C3[:, :, 1:slen], U3[:, :, 0:slen - 1], cw(0), C3[:, :, 1:slen], ALU.mult, ALU.add,
                    [U_.r, PAR.r, CO.r], [CO.r])
                stt(C3[:, :, 0:slen - 1], U3[:, :, 1:slen], cw(2), C3[:, :, 0:slen - 1], ALU.mult, ALU.add,
                    [U_.r, PAR.r, CO.r], [CO.r])
                tt(CO[:, :], CO[:, :], BG[:, :], ALU.mult, [CO.r, BG.r], [CO.r])
                for tb in range(2):
                    tsl = slice(tb * 512, (tb + 1) * 512)
                    act(SGm[:, tsl], PS[2 + tb][:, :], AF.Sigmoid, [PS[2 + tb].r], [SGm.r])
                    tt(SGm[:, tsl], PS[2 + tb][:, :], SGm[:, tsl], ALU.mult, [PS[2 + tb].r, SGm.r], [SGm.r])
                tt(YT[:, 4 + cc, :], CO[:, :], SGm[:, :], ALU.mult, [CO.r, SGm.r], [YT.r])

        def odd_layer(g, l):
            p = l // 2
            nkeys = 256 if g == 0 else 1536
            nkb = nkeys // 128
            KTa = [B[0], B[1], B[2], B[3]] if g == 0 else None
            if g == 0:
                ktile = [(B[j], 0) for j in range(4)]
            else:
                ktile = [(B[2 * j], B[2 * j + 1]) for j in range(4)]
            Vtok = B[8]
            Vtok2 = B[9]
            Vc = B[10]
            slot = next_w("o_k")

            def qk_norm_row(pm_pair, gname, out_t, out_ap, rope):
                Q = X[4]
                sq = B[15]
                RS = X[5]
                for tb in range(2):
                    tsl = slice(tb * 512, (tb + 1) * 512)
                    cp(Q[:, tsl], pm_pair[tb][:, :], [pm_pair[tb].r], [Q.r], e="act")
                    act(sq[:, tsl], pm_pair[tb][:, :], AF.Square, [pm_pair[tb].r], [sq.r])
                for tb in range(2):
                    tsl = slice(tb * 512, (tb + 1) * 512)
                    mm(pm_pair[tb][:, :], BDM, sq[:, tsl], True, True, [CB.r, sq.r], [pm_pair[tb].r])
                    rsqrt_from(RS, RS[:, tsl], pm_pair[tb][:, :], [pm_pair[tb].r, SMALL.r], 1.0, EPSC)
                if not rope:
                    stt(out_ap, Q[:, :], par(gname, 0, 1), RS[:, :], ALU.mult, ALU.mult, [Q.r, PAR.r, RS.r], [out_t.r])
                    return
                QN, QNb, T1 = X[6], B[15], X[7]
                stt(QN[:, :], Q[:, :], par(gname, 0, 1), RS[:, :], ALU.mult, ALU.mult, [Q.r, PAR.r, RS.r], [QN.r])
                cp(QNb[:, :], QN[:, :], [QN.r], [QNb.r], e="act")
                tt(T1[:, :], QN[:, :], COS, ALU.mult, [QN.r, CF.r], [T1.r])
                for tb in range(2):
                    tsl = slice(tb * 512, (tb + 1) * 512)
                    mm(pm_pair[tb][:, :], ROTT, QNb[:, tsl], True, True, [CB.r, QNb.r], [pm_pair[tb].r])
                    tt(QN[:, tsl], pm_pair[tb][:, :], SIN[:, tsl],
                       ALU.mult, [pm_pair[tb].r, CF.r], [QN.r])
                tt(out_ap, T1[:, :], QN[:, :], ALU.add, [T1.r, QN.r], [out_t.r])

            KN32 = X[8]
            for j in range(4):
                kpp = PS[0:2] if j % 2 == 0 else PS[6:8]
                proj_row(slot, j * 128, kpp)
                kt = ktile[j][0]
                if g == 0:
                    qk_norm_row(kpp, f"kg{p}", KN32, KN32[:, :], False)
                    cp(kt[:, :], KN32[:, :], [KN32.r], [kt.r], e="act")
                    for blk in range(8):
                        tr(PS[2 + blk // 4][:, (blk % 4) * 128:(blk % 4 + 1) * 128],
                           KN32[:, blk * 128:(blk + 1) * 128], ident, [KN32.r, CF.r], [PS[2 + blk // 4].r])
                    OK_ = X[9]
                    for hf in range(2):
                        cp(OK_[:, hf * 512:(hf + 1) * 512], PS[2 + hf][:, :], [PS[2 + hf].r], [OK_.r])
                    for q in range(4):
                        src = OK_[:, q * 256:(q + 1) * 256].rearrange("p (b c) -> p b c", b=2)[:, :, 0:64]
                        dst = nk_d[q, p, j].rearrange("(b t) c -> t b c", b=2)
                        S.dma("sp", dst, src, OK_.r, False)
                else:
                    qk_norm_row(kpp, f"kg{p}", kt, kt[:, :], True)
                    kc_t = ktile[j][1]
                    st = X[9]
                    for h2 in range(2):
                        S.dma("sp", st[:, 0:512].rearrange("p (b c) -> p b c", b=4)[:, :, h2 * 64:(h2 + 1) * 64],
                              ck_d[p, j].rearrange("(b t) c -> t b c", b=4), st.r, True)
                    for blk in range(4):
                        tr(PS[2][:, blk * 128:(blk + 1) * 128], st[:, blk * 128:(blk + 1) * 128], ident,
                           [st.r, CF.r], [PS[2].r])
                    cp(kc_t[:, 0:512], PS[2][:, :], [PS[2].r], [kc_t.r], e="act")
            ck('o2')
            slot = next_w("o_v")
            VO = X[10]
            for blk in range(8):
                pm = PS[blk % 2]
                for kc in range(KC):
                    mm(pm[:, 0:256], HT[:, kc, blk * 128:(blk + 1) * 128], slot[:, kc, 0:256], kc == 0, kc == KC - 1,
                       [HT.r, slot.r], [pm.r])
                vdst = (Vtok if blk < 4 else Vtok2)
                if g == 1:
                    cp(vdst[:, (blk % 4) * 256:(blk % 4 + 1) * 256], pm[:, 0:256], [pm.r], [vdst.r], e="act")
                if g == 0:
                    cp(VO[:, (blk % 4) * 256:(blk % 4 + 1) * 256], pm[:, 0:256], [pm.r], [VO.r])
                    cp(vdst[:, (blk % 4) * 256:(blk % 4 + 1) * 256], VO[:, (blk % 4) * 256:(blk % 4 + 1) * 256],
                       [VO.r], [vdst.r], e="act")
                    if blk % 4 == 3:
                        for qq in range(2):
                            q = (blk // 4) * 2 + qq
                            src = VO[:, qq * 512:(qq + 1) * 512].rearrange("p (b h c) -> p b h c", b=2, h=4)
                            for hh in range(4):
                                dst = nv_d[q, p, hh].rearrange("(b t) c -> t b c", b=2)
                                S.dma("sp", dst, src[:, :, hh, :], VO.r, False)
            if g == 1:
                st = X[9]
                for hh in range(4):
                    S.dma("sp", st[:, :].rearrange("p (b h c) -> p b h c", b=4, h=4)[:, :, hh, :],
                          cv_d[p, hh].rearrange("(b t) c -> t b c", b=4), st.r, True)
                cp(Vc[:, :], st[:, :], [st.r], [Vc.r])

            ck('o3')

            def vblock(kb, kvh, seq=None):
                if g == 0:
                    blk = seq * 2 + kb
                    tl = Vtok if blk < 4 else Vtok2
                    return tl[:, (blk % 4) * 256 + kvh * 64:(blk % 4) * 256 + kvh * 64 + 64], tl.r
                if kb < 4:
                    return Vc[:, kb * 256 + kvh * 64:kb * 256 + kvh * 64 + 64], Vc.r
                blk = kb - 4
                tl = Vtok if blk < 4 else Vtok2
                return tl[:, (blk % 4) * 256 + kvh * 64:(blk % 4) * 256 + kvh * 64 + 64], tl.r

            def kblock(kb, kvh, sl, seq=None):
                if g == 0:
                    t_ = ktile[kvh][0]
                    c0 = seq * 256 + kb * 128
                    return t_[sl, c0:c0 + 128], t_.r
                if kb < 4:
                    t_ = ktile[kvh][1]
                    return t_[sl, kb * 128:(kb + 1) * 128], t_.r
                t_ = ktile[kvh][0]
                return t_[sl, (kb - 4) * 128:(kb - 3) * 128], t_.r

            QT = B[11]
            GSs = [B[12], B[20]]
            QZs = [[B[16], B[17]], [B[18], B[19]]]
            PTs = [B[13], B[14]]
            sbank = [PS[2], PS[3], PS[4]]
            po, pd = PS[5], PS[6]
            RD, OA = X[8], X[9]
            slot_h = [None]
            nexp = [0]
            nblk = [0]

            def prep(qr, buf):
                if qr % 2 == 0:
                    slot_h[0] = next_w("o_q")
                slot = slot_h[0]
                r2 = qr % 2
                QZ, GS = QZs[buf], GSs[buf]
                proj_row(slot, r2 * 256, PS[0:2])
                qk_norm_row(PS[0:2], f"qg{p}", QT, QT[:, :], g == 1)
                for hh in range(2):
                    sl = slice(hh * 64, hh * 64 + 64)
                    so = slice((1 - hh) * 64, (1 - hh) * 64 + 64)
                    S.op("dve", lambda en, t_=QZ[hh], so_=so: en.memset(t_[so_, :], 0.0), [], [QZ[hh].r])
                    cp(QZ[hh][sl, :], QT[sl, :], [QT.r, QZ[hh].r], [QZ[hh].r])
                proj_row(slot, r2 * 256 + 128, PS[0:2])
                for tb in range(2):
                    tsl = slice(tb * 512, (tb + 1) * 512)
                    act(X[5][:, tsl], PS[tb][:, :], AF.Sigmoid, [PS[tb].r], [X[5].r])
                    tt(GS[:, tsl], PS[tb][:, :], X[5][:, tsl], ALU.mult, [PS[tb].r, X[5].r], [GS.r])

            def attn(qr, buf):
                kvh = qr // 2
                QZ, GS = QZs[buf], GSs[buf]
                qblocks = [(q, q * 256, 256) for q in range(4)] if g == 0 else [(None, 0, 512), (None, 512, 512)]
                nk = nkb if g == 1 else 2
                units = [(bi, seq, q0, qn, hh, kb) for bi, (seq, q0, qn) in enumerate(qblocks)
                         for hh in range(2) for kb in range(nk)]
                ids = {}

                def emit_score(u):
                    bi, seq, q0, qn, hh, kb = u
                    i = nexp[0]
                    nexp[0] += 1
                    ids[u] = i
                    sb = sbank[i % 3]
                    kap, kreg = kblock(kb, kvh, slice(0, 128), seq)
                    mm(sb[:, 0:qn], kap, QZ[hh][:, q0:q0 + qn], True, True, [kreg, QZ[hh].r], [sb.r])

                def emit_rest(u):
                    bi, seq, q0, qn, hh, kb = u
                    i = ids[u]
                    sb = sbank[i % 3]
                    pt_ = PTs[i % 2]
                    po = PS[5] if (nblk[0] + bi) % 2 == 0 else PS[7]
                    sl = slice(hh * 64, hh * 64 + 64)
                    act(pt_[:, 0:qn], sb[:, 0:qn], AF.Exp, [sb.r], [pt_.r], scale=0.125)
                    vap, vreg = vblock(kb, kvh, seq)
                    last = kb == nk - 1
                    mm(po[sl, 0:qn], vap, pt_[:, 0:qn], kb == 0, last, [vreg, pt_.r], [po.r], tp=(0, hh * 64))
                    mm(pd[sl, 0:qn], ONES[:, 0:64], pt_[:, 0:qn], kb == 0, last, [CB.r, pt_.r], [pd.r],
                       tp=(0, hh * 64), inc=True)
                    if last and hh == 1:
                        recip(RD[:, 0:qn], pd[:, 0:qn], [pd.r], [RD.r])
                        tt(OA[:, 0:qn], po[:, 0:qn], RD[:, 0:qn], ALU.mult, [po.r, RD.r], [OA.r])
                        tt(YT[:, qr, q0:q0 + qn], OA[:, 0:qn], GS[:, q0:q0 + qn], ALU.mult, [OA.r, GS.r], [YT.r])

                emit_score(units[0])
                if len(units) > 1:
                    emit_score(units[1])
                for k, u in enumerate(units):
                    if k + 2 < len(units):
                        emit_score(units[k + 2])
                    emit_rest(u)
                nblk[0] += len(qblocks)

            S.replay(S.record(lambda: prep(0, 0)))
            for qr in range(8):
                qa = S.record(lambda: attn(qr, qr % 2))
                qb = S.record(lambda: prep(qr + 1, (qr + 1) % 2)) if qr < 7 else []
                S.replay(qa, qb)

        mk_tb = [mk(f"TBF{i}", [128, 256], BF16) for i in range(4)]
        mk_xs = [mk(f"XSb{i}", [128, 256], BF16) for i in range(4)]
        TST1 = mk("TST1", [128, 256], F32)

        for g in range(2):
            xin = xp_d if g == 0 else xs_d
            yout = yp_d if g == 0 else ys_d
            for rnd in range(2):
                for i in range(4):
                    tt_ = rnd * 4 + i
                    S.dma("sp", X[i][:, :], xin[tt_ * 128:(tt_ + 1) * 128, :], X[i].r, True)
                for kc in range(KC):
                    pm = PS[kc % 2]
                    for i in range(4):
                        tr(pm[:, i * 128:(i + 1) * 128], X[i][:, kc * 128:(kc + 1) * 128], ident, [X[i].r, CF.r], [pm.r])
                    cp(XT[:, kc, rnd * 512:(rnd + 1) * 512], pm[:, :], [pm.r], [XT.r], e="act" if kc % 2 else "dve")
            try:
              for l in LAYERS:
                S.barrier()
                rms_and_modulate(g, l)
                if l % 2 == 0:
                    even_layer(g, l)
                else:
                    odd_layer(g, l)
                out_proj(g, l)
                if dbg:
                    S.dma("sp", dbg_d[l, g].rearrange("p (k t) -> p k t", k=KC), XT[:, :, :], XT.r, False)
            except _Stop:
                wi['i'] = len(plan)
                LAYERS = []
            S.barrier()
            sq = B[15]
            for tb in range(2):
                tsl = slice(tb * 512, (tb + 1) * 512)
                pm = PS[tb]
                for kc in range(KC):
                    act(sq[:, tsl], XT[:, kc, tsl], AF.Square, [XT.r], [sq.r])
                    mm(pm[:, :], ONES, sq[:, tsl], kc == 0, kc == KC - 1, [CB.r, sq.r], [pm.r], inc=True)
                rsqrt_from(RSTD, RSTD[:, tsl], pm[:, :], [pm.r, SMALL.r], 1.0, DEPSC)
            ts(GSC[:, :], par("finalg", 0, 8), 32.0, None, ALU.mult, None, [PAR.r], [GSC.r])
            for kc in range(KC):
                stt(XT[:, kc, :], XT[:, kc, :], GSC[:, kc:kc + 1], RSTD[:, :], ALU.mult, ALU.mult,
                    [XT.r, GSC.r, RSTD.r], [XT.r])
            for rnd in range(2):
                for i in range(4):
                    tt_ = rnd * 4 + i
                    for kc4 in range(2):
                        pm = PS[(i * 2 + kc4) % 2]
                        for k2 in range(4):
                            kc = kc4 * 4 + k2
                            tr(pm[:, k2 * 128:(k2 + 1) * 128], XT[:, kc, tt_ * 128:(tt_ + 1) * 128], ident,
                               [XT.r, CF.r], [pm.r])
                        cp(X[i][:, kc4 * 512:(kc4 + 1) * 512], pm[:, :], [pm.r], [X[i].r], e="act" if kc4 else "dve")
                    S.dma("sp", yout[tt_ * 128:(tt_ + 1) * 128, :], X[i][:, :], X[i].r, False)
            S.barrier()

        S.finish([t_.r for t_ in X] + [TST.r, XT.r] + [w_.r for w_ in WS])
        print("instructions emitted:", S.ninstr, {k: v for k, v in S.cnt.items()})
    return nc


def _consts():
    f32 = np.float32
    p = np.arange(128)
    i64 = p % 64
    ident = np.eye(128, dtype=f32)
    col = np.arange(512)
    j64 = col % 64
    idn8 = (i64[:, None] == j64[None, :]).astype(f32)
    t = np.arange(1024)
    inv = 10000.0 ** (-(np.arange(0, 32, 2).astype(np.float64)) / 32.0)
    within = i64 % 32
    f = within % 16
    half = i64 // 32
    pos = np.where(half[:, None] == 0, (t // 64)[None, :], (t % 64)[None, :]).astype(np.float64)
    ang = pos * inv[f][:, None]
    cos = np.cos(ang).astype(f32)
    sin = np.sin(ang).astype(f32)
    cf = np.concatenate([ident, idn8, cos, sin], axis=1).astype(f32)
    ones = np.ones((128, 128), f32)
    bd1 = ((p[:, None] // 64) == (p[None, :] // 64)).astype(f32)
    bdm = bd1 / 64.0
    rott = np.zeros((128, 128), f32)
    for m in range(128):
        if (m % 32) < 16:
            rott[m + 16, m] = -1.0
        else:
            rott[m - 16, m] = 1.0
    msl = (i64[:, None] > j64[None, :]).astype(f32)
    msu = (i64[:, None] < j64[None, :]).astype(f32)
    mil = (i64[:, None] >= j64[None, :]).astype(f32)
    miu = (i64[:, None] <= j64[None, :]).astype(f32)
    resf = np.broadcast_to((t % 64 != 0).astype(f32)[None, :], (128, 1024))
    resb = np.broadcast_to((t % 64 != 63).astype(f32)[None, :], (128, 1024))
    cb = np.concatenate([ones, bd1, bdm, rott, ident, msl, msu, mil, miu, resf, resb], axis=1).astype(f32)
    assert cf.shape[1] == NCF and cb.shape[1] == NCB
    return np.ascontiguousarray(cf), np.ascontiguousarray(cb)


def _pack_params(inp):
    f32 = np.float32
    P = np.zeros((128, NPAR), f32)

    def put(name, arr):
        o0, n = _off[name]
        assert arr.shape == (128, n), (name, arr.shape, n)
        P[:, o0:o0 + n] = arr

    col = lambda v, n: np.asarray(v, f32).reshape(n, 128).T
    for l in range(DEPTH):
        put(f"bada{l}", col(inp["b_ada"][l], 24))
        put(f"normg{l}", col(inp["norm_g"][l], 8))
    put("finalg", col(inp["final_g"], 8))
    for p in range(2):
        put(f"mu{p}", col(inp["mu_shift"][p], 14))
        put(f"w0{p}", np.asarray(inp["w0"][p], f32).reshape(2, 4, 128).transpose(2, 0, 1).reshape(128, 8))
        put(f"a0{p}", np.asarray(inp["a0"][p], f32).reshape(2, 4, 128).transpose(2, 0, 1).reshape(128, 8))
        put(f"kk{p}", col(inp["k_k"][p], 4))
        put(f"ka{p}", col(inp["k_a"][p], 4))
        put(f"rk{p}", col(np.asarray(inp["r_k"][p]).reshape(512), 4))
        put(f"lng{p}", col(inp["lnx_g"][p], 4))
        put(f"lnb{p}", col(inp["lnx_b"][p], 4))
        put(f"cw{p}", np.asarray(inp["conv_w"][p], f32).reshape(3, 4, 128).transpose(2, 0, 1).reshape(128, 12))
        put(f"cb{p}", col(inp["conv_b"][p], 4))
        put(f"qg{p}", np.tile(np.asarray(inp["q_norm_g"][p], f32), 2).reshape(128, 1))
        put(f"kg{p}", np.tile(np.asarray(inp["k_norm_g"][p], f32), 2).reshape(128, 1))
    return P


_DBG = False


def kernel(**inp):
    f32 = np.float32
    inp = {k: np.asarray(v) for k, v in inp.items()}
    nc = build_program(dbg=_DBG)
    cf, cb = _consts()
    params = _pack_params(inp)
    shared = {
        "params": params, "consts_f": cf, "consts_b": cb,
        "w_ada": np.ascontiguousarray(inp["w_ada"], f32),
        "w_in_e": np.ascontiguousarray(inp["w_in_e"], f32),
        "w_out_e": np.ascontiguousarray(inp["w_out_e"], f32),
        "w_in_o": np.ascontiguousarray(inp["w_in_o"], f32),
        "w_out_o": np.ascontiguousarray(inp["w_out_o"], f32),
        "lora_w2": np.ascontiguousarray(inp["lora_w2"], f32).reshape(2, 128, 512),
        "lora_a2": np.ascontiguousarray(inp["lora_a2"], f32).reshape(2, 128, 512),
    }
    in_maps = []
    for i in range(8):
        cv = np.zeros((128, 16), f32)
        cv[:, 0:8] = inp["c_ctx"].astype(f32).reshape(8, 128).T
        cv[:, 8:16] = inp["c"][i].astype(f32).reshape(8, 128).T
        m = dict(shared)
        m["x_prompt"] = np.ascontiguousarray(inp["x_prompt"][4 * i:4 * i + 4], f32).reshape(1024, D)
        m["x_sample"] = np.ascontiguousarray(inp["x_sample"][i], f32)
        m["cvec"] = cv
        m["state"] = np.ascontiguousarray(inp["state_rwkv"][i], f32)
        m["cache_k"] = np.ascontiguousarray(inp["cache_k"][i], f32)
        m["cache_v"] = np.ascontiguousarray(inp["cache_v"][i], f32)
        in_maps.append(m)
    res = run_bass_kernel_spmd(nc, in_maps, core_ids=list(range(8)))
    R = res.results
    y_prompt = np.concatenate([r["y_prompt"].reshape(4, 256, D) for r in R], axis=0)
    y_sample = np.stack([r["y_sample"] for r in R], axis=0)
    new_state = np.concatenate([r["new_state"] for r in R], axis=0)
    new_k = np.concatenate([r["new_k"] for r in R], axis=0)
    new_v = np.concatenate([r["new_v"] for r in R], axis=0)
    if _DBG:
        kernel.dbg = [r["dbg"] for r in R]
    return (y_prompt.astype(f32), y_sample.astype(f32), new_state.astype(f32), new_k.astype(f32), new_v.astype(f32))
```

```python
import math
import os
import numpy as np
import concourse.bass as bass
import concourse.mybir as mybir
from concourse.bass_utils import run_bass_kernel_spmd

F32 = mybir.dt.float32
BF16 = mybir.dt.bfloat16
ALU = mybir.AluOpType
AF = mybir.ActivationFunctionType

D = 1024
NT = 1024
KC = 8
CH = 64
NCH = NT // CH
DEPTH = 4
EPS = 1e-6
GN_EPS = 64e-5
C0 = math.exp(-0.5)
EVEN_IN = 4352
ODD_IN = 2560
A_SHIFT = 1792

_off = {}
_n = 0


def _reg(name, n):
    global _n
    _off[name] = (_n, n)
    _n += n


for _l in range(DEPTH):
    _reg(f"bada{_l}", 24)
    _reg(f"normg{_l}", 8)
_reg("finalg", 8)
for _p in range(2):
    _reg(f"mu{_p}", 14)
    _reg(f"w0{_p}", 8)
    _reg(f"a0{_p}", 8)
    _reg(f"kk{_p}", 4)
    _reg(f"ka{_p}", 4)
    _reg(f"rk{_p}", 4)
    _reg(f"lng{_p}", 4)
    _reg(f"lnb{_p}", 4)
    _reg(f"cw{_p}", 12)
    _reg(f"cb{_p}", 4)
    _reg(f"qg{_p}", 1)
    _reg(f"kg{_p}", 1)
NPAR = _n
NCF = 128 + 512 + 2048
NCB = 4736


class _Stop(Exception):
    pass


def ck(n):
    if os.environ.get('K_STOP') == str(n):
        raise _Stop()


class Region:
    __slots__ = ("name", "w", "r", "dsem", "dcnt")

    def __init__(self, name):
        self.name = name
        self.w = None
        self.r = {}
        self.dsem = None
        self.dcnt = 0


class Sched:
    def __init__(self, nc, stack):
        self.nc = nc
        self.eng = {"pe": nc.tensor, "act": nc.scalar, "dve": nc.vector, "pool": nc.gpsimd, "sp": nc.sync}
        self.sem = {k: stack.enter_context(nc.semaphore("s_" + k)) for k in self.eng}
        self.cnt = {k: 0 for k in self.eng}
        self.seen = {k: {} for k in self.eng}
        self.stack = stack
        self.dsems = []
        self.ninstr = 0
        self.capture = None

    def record(self, fn):
        assert self.capture is None
        self.capture = []
        fn()
        q = self.capture
        self.capture = None
        return q

    def _emit(self, it):
        if it[0] == "op":
            self.op(*it[1:])
        else:
            self.dma(*it[1:])

    def replay(self, qa, qb=()):
        nb = 0
        for i, it in enumerate(qa):
            self._emit(it)
            want = ((i + 1) * len(qb)) // max(1, len(qa))
            while nb < want:
                self._emit(qb[nb])
                nb += 1
        while nb < len(qb):
            self._emit(qb[nb])
            nb += 1

    def region(self, name, dma=False):
        r = Region(name)
        if dma:
            r.dsem = self.stack.enter_context(self.nc.semaphore("d_" + name))
        return r

    def _wait(self, e, sem, val):
        if e == "pe" and sem is self.sem["pe"]:
            return
        key = sem.name
        for f_, s_ in self.sem.items():
            if s_ is sem:
                assert val <= self.cnt[f_], ("wait on pending (non-incrementing) op", e, f_, val, self.cnt[f_])
        if self.seen[e].get(key, 0) >= val:
            return
        self.seen[e][key] = val
        self.eng[e].wait_ge(sem, val)

    def _deps(self, e, reads, writes):
        for r in reads:
            if r.w is not None:
                self._wait(e, self.sem[r.w[0]], r.w[1])
            if r.dsem is not None and r.dcnt:
                self._wait(e, r.dsem, 16 * r.dcnt)
        for r in writes:
            if r.w is not None:
                self._wait(e, self.sem[r.w[0]], r.w[1])
            for k, c in r.r.items():
                self._wait(e, self.sem[k], c)
            if r.dsem is not None and r.dcnt:
                self._wait(e, r.dsem, 16 * r.dcnt)

    def op(self, e, ins_fn, reads=(), writes=(), inc=True):
        if self.capture is not None:
            self.capture.append(("op", e, ins_fn, tuple(reads), tuple(writes), inc))
            return
        self._deps(e, reads, writes)
        ins = ins_fn(self.eng[e])
        if inc:
            self.cnt[e] += 1
            c = self.cnt[e]
            ins.then_inc(self.sem[e], 1)
        else:
            assert e == "pe"
            c = self.cnt[e] + 1
        self.seen[e][self.sem[e].name] = max(self.seen[e].get(self.sem[e].name, 0), 0)
        for r in reads:
            r.r[e] = c
        for r in writes:
            r.w = (e, c)
            r.r = {}
        self.ninstr += 1

    def dma(self, q, out, in_, sb_region, sb_is_dst, extra_reads=()):
        r = sb_region
        if self.capture is not None:
            self.capture.append(("dma", q, out, in_, sb_region, sb_is_dst, tuple(extra_reads)))
            return
        if sb_is_dst:
            self._deps(q, extra_reads, [r])
        else:
            self._deps(q, [r] + list(extra_reads), [])
        ins = self.eng[q].dma_start(out=out, in_=in_)
        r.dcnt += 1
        ins.then_inc(r.dsem, 16)
        if sb_is_dst:
            r.w = None
            r.r = {}
        self.ninstr += 1

    def barrier(self):
        for e in self.eng:
            for f in self.eng:
                if f != e and self.cnt[f]:
                    self._wait(e, self.sem[f], self.cnt[f])

    def finish(self, regions):
        for r in regions:
            if r.dsem is not None and r.dcnt:
                self._wait("sp", r.dsem, 16 * r.dcnt)
        for f in self.eng:
            if f != "sp" and self.cnt[f]:
                self._wait("sp", self.sem[f], self.cnt[f])


class T:
    def __init__(self, S, stack, name, shape, dt, psum=False, dma=False):
        nc = S.nc
        self.t = stack.enter_context(nc.psum_tensor(name, shape, dt) if psum else nc.sbuf_tensor(name, shape, dt))
        self.r = S.region(name, dma=dma)

    def __getitem__(self, k):
        return self.t[k]


class View:
    def __init__(self, ap, r):
        self.ap = ap
        self.r = r

    def __getitem__(self, k):
        return self.ap[k]


def v3(ap, q):
    return ap.rearrange("p (q s) -> p q s", q=q)


def build_program(dbg=False):
    from contextlib import ExitStack
    nc = bass.Bass("TRN2", target_bir_lowering=False)
    dt = nc.dram_tensor
    xp_d = dt("x_prompt", [4 * 256, D], F32, kind="ExternalInput").ap()
    xs_d = dt("x_sample", [NT, D], F32, kind="ExternalInput").ap()
    cvec_d = dt("cvec", [128, 16], F32, kind="ExternalInput").ap()
    st_d = dt("state", [2, 2, 8, 64, 64], F32, kind="ExternalInput").ap()
    ck_d = dt("cache_k", [2, 4, 512, 64], F32, kind="ExternalInput").ap()
    cv_d = dt("cache_v", [2, 4, 512, 64], F32, kind="ExternalInput").ap()
    par_d = dt("params", [128, NPAR], F32, kind="ExternalInput").ap()
    wada_d = dt("w_ada", [DEPTH, D, 3 * D], F32, kind="ExternalInput").ap()
    wine_d = dt("w_in_e", [2, D, EVEN_IN], F32, kind="ExternalInput").ap()
    woute_d = dt("w_out_e", [2, D, D], F32, kind="ExternalInput").ap()
    wino_d = dt("w_in_o", [2, D, ODD_IN], F32, kind="ExternalInput").ap()
    wouto_d = dt("w_out_o", [2, D, D], F32, kind="ExternalInput").ap()
    lw2_d = dt("lora_w2", [2, 128, 512], F32, kind="ExternalInput").ap()
    la2_d = dt("lora_a2", [2, 128, 512], F32, kind="ExternalInput").ap()
    cf_d = dt("consts_f", [128, NCF], F32, kind="ExternalInput").ap()
    cb_d = dt("consts_b", [128, NCB], F32, kind="ExternalInput").ap()
    yp_d = dt("y_prompt", [4 * 256, D], F32, kind="ExternalOutput").ap()
    ys_d = dt("y_sample", [NT, D], F32, kind="ExternalOutput").ap()
    ns_d = dt("new_state", [4, 2, 2, 8, 64, 64], F32, kind="ExternalOutput").ap()
    nk_d = dt("new_k", [4, 2, 4, 256, 64], F32, kind="ExternalOutput").ap()
    nv_d = dt("new_v", [4, 2, 4, 256, 64], F32, kind="ExternalOutput").ap()
    dbg_d = dt("dbg", [DEPTH, 2, 128, KC * NT], F32, kind="ExternalOutput").ap() if dbg else None

    with ExitStack() as stack:
        S = Sched(nc, stack)

        def mk(name, shape, dtp, psum=False, dma=False):
            return T(S, stack, name, shape, dtp, psum=psum, dma=dma)

        XT = mk("XT", [128, KC, NT], F32, dma=True)
        HT = mk("HT", [128, KC, NT], BF16)
        YT = mk("YT", [128, KC, NT], BF16)
        NSLOT = 2
        WS = [mk(f"WS{i}", [128, KC, 512], BF16, dma=True) for i in range(NSLOT)]
        PAR = mk("PAR", [128, NPAR], F32, dma=True)
        CV = mk("CV", [128, 16], F32, dma=True)
        CF = mk("CF", [128, NCF], F32, dma=True)
        CB = mk("CB", [128, NCB], BF16)
        MOD = mk("MOD", [128, DEPTH * 2 * 24], F32)
        GSC = mk("GSC", [128, 8], F32)
        DER = mk("DER", [128, 64], F32)
        PS = [mk(f"PS{i}", [128, 512], F32, psum=True) for i in range(8)]
        NX = 12
        X = [mk(f"X{i}", [128, NT], F32, dma=True) for i in range(NX)]
        RSTD = X[9]
        NB = 23
        B = [mk(f"B{i}", [128, NT], BF16) for i in range(NB)]
        SMALL = mk("SMALL", [128, 256], F32)
        TST = mk("TST", [128, 512], F32, dma=True)
        allr = []

        ident = CF[:, 0:128]
        IDN8 = CF[:, 128:640]
        COS = CF[:, 640:1664]
        SIN = CF[:, 1664:2688]
        ONES = CB[:, 0:128]
        BD1 = CB[:, 128:256]
        BDM = CB[:, 256:384]
        ROTT = CB[:, 384:512]
        IDB = CB[:, 512:640]
        MSL = CB[:, 640:1152]
        MSU = CB[:, 1152:1664]
        MIL = CB[:, 1664:2176]
        MIU = CB[:, 2176:2688]
        RESF = CB[:, 2688:3712]
        RESB = CB[:, 3712:4736]

        def par(name, j=0, n=1):
            o0, _ = _off[name]
            return PAR[:, o0 + j:o0 + j + n]

        def mm(out_ap, lhsT, rhs, start, stop, rd, wr, tp=None, inc=None):
            if inc is None:
                inc = bool(stop)
            if tp is None:
                S.op("pe", lambda e: e.matmul(out_ap, lhsT=lhsT, rhs=rhs, start=start, stop=stop), rd, wr, inc)
            else:
                S.op("pe", lambda e: e.matmul(out_ap, lhsT=lhsT, rhs=rhs, start=start, stop=stop,
                                              tile_position=tp), rd, wr, inc)

        def tr(out_ap, in_ap, idn, rd, wr, tp=None):
            if tp is None:
                S.op("pe", lambda e: e.transpose(out=out_ap, in_=in_ap, identity=idn), rd, wr)
            else:
                S.op("pe", lambda e: e.transpose(out=out_ap, in_=in_ap, identity=idn, tile_position=tp), rd, wr)

        def act(out_ap, in_ap, func, rd, wr, bias=0.0, scale=1.0):
            S.op("act", lambda e: e.activation(out=out_ap, in_=in_ap, func=func, bias=bias, scale=scale), rd, wr)

        def tt(out_ap, a, b, op, rd, wr, e="dve"):
            S.op(e, lambda en: en.tensor_tensor(out=out_ap, in0=a, in1=b, op=op), rd, wr)

        def ts(out_ap, a, s1, s2, op0, op1, rd, wr, e="dve"):
            if op1 is None:
                S.op(e, lambda en: en.tensor_scalar(out=out_ap, in0=a, scalar1=s1, scalar2=None, op0=op0), rd, wr)
            else:
                S.op(e, lambda en: en.tensor_scalar(out=out_ap, in0=a, scalar1=s1, scalar2=s2, op0=op0, op1=op1),
                     rd, wr)

        def stt(out_ap, a, s, b, op0, op1, rd, wr, e="dve"):
            S.op(e, lambda en: en.scalar_tensor_tensor(out=out_ap, in0=a, scalar=s, in1=b, op0=op0, op1=op1), rd, wr)

        def cp(out_ap, in_ap, rd, wr, e="dve"):
            if e == "act":
                S.op("act", lambda en: en.copy(out=out_ap, in_=in_ap), rd, wr)
            else:
                S.op(e, lambda en: en.tensor_copy(out=out_ap, in_=in_ap), rd, wr)

        def recip(out_ap, in_ap, rd, wr):
            S.op("dve", lambda en: en.reciprocal(out=out_ap, in_=in_ap), rd, wr)

        def rsqrt_from(out_t, out_ap, in_ap, in_regs, scale, bias_ap):
            act(out_ap, in_ap, AF.Sqrt, in_regs, [out_t.r], bias=bias_ap, scale=scale)
            recip(out_ap, out_ap, [out_t.r], [out_t.r])

        S.dma("sp", PAR[:, :], par_d[:, :], PAR.r, True)
        S.dma("sp", CV[:, :], cvec_d[:, :], CV.r, True)
        S.dma("sp", CF[:, :], cf_d[:, :], CF.r, True)
        for i in range(0, NCB, 1024):
            n = min(1024, NCB - i)
            xi = X[(i // 1024) % 4]
            S.dma("sp", xi[:, 0:n], cb_d[:, i:i + n], xi.r, True)
            cp(CB[:, i:i + n], xi[:, 0:n], [xi.r], [CB.r])
        S.op("dve", lambda en: en.memset(SMALL[:, 0:1], EPS), [], [SMALL.r])
        S.op("dve", lambda en: en.memset(SMALL[:, 1:2], GN_EPS), [], [SMALL.r])
        S.op("dve", lambda en: en.memset(SMALL[:, 2:3], D * EPS), [], [SMALL.r])
        EPSC = SMALL[:, 0:1]
        GNEPSC = SMALL[:, 1:2]
        DEPSC = SMALL[:, 2:3]

        wq = []
        wstate = {"issued": 0}

        def wview(wd, c0, n):
            return wd.rearrange("(kc p) f -> p kc f", p=128)[:, :, c0:c0 + n]

        def w_issue_upto(i):
            while wstate["issued"] <= min(i, len(wq) - 1):
                u = wstate["issued"]
                slot = WS[u % NSLOT]
                for (dc, n, src) in wq[u]:
                    S.dma("pool", slot[:, :, dc:dc + n], src, slot.r, True)
                wstate["issued"] += 1

        def w_get(i):
            w_issue_upto(i + NSLOT - 1)
            return WS[i % NSLOT]

        plan = []
        for l in range(DEPTH):
            for u in range(6):
                wq.append([(0, 512, wview(wada_d[l], u * 512, 512))])
                plan.append(("ada", l, u))
        LAYERS = [l for l in range(int(os.environ.get('K_LAYERS', DEPTH)))
                  if not os.environ.get('K_ONLY') or str(l) in os.environ['K_ONLY']]
        for g in range(2):
            for l in LAYERS:
                p = l // 2
                if l % 2 == 0:
                    w = wine_d[p]
                    wq.append([(0, 256, wview(w, 1536, 256))])
                    plan.append(("e_lora", g, l))
                    for hp in range(4):
                        wq.append([(0, 128, wview(w, hp * 128, 128)), (128, 128, wview(w, 512 + hp * 128, 128)),
                                   (256, 128, wview(w, 1024 + hp * 128, 128)),
                                   (384, 128, wview(w, A_SHIFT + hp * 128, 128))])
                        plan.append(("e_hp", g, l, hp))
                    for cc in range(4):
                        base = A_SHIFT + 512
                        wq.append([(j * 128, 128, wview(w, base + j * 512 + cc * 128, 128)) for j in range(4)])
                        plan.append(("e_b", g, l, cc))
                    for u in range(2):
                        wq.append([(0, 512, wview(woute_d[p], u * 512, 512))])
                        plan.append(("out", g, l, u))
                else:
                    w = wino_d[p]
                    wq.append([(j * 128 + h2 * 64, 64, wview(w, 1024 + j * 64, 64)) for j in range(4) for h2 in range(2)])
                    plan.append(("o_k", g, l))
                    wq.append([(0, 256, wview(w, 1280, 256))])
                    plan.append(("o_v", g, l))
                    for qq in range(4):
                        wq.append([(0, 128, wview(w, (2 * qq) * 128, 128)), (128, 128, wview(w, 1536 + (2 * qq) * 128, 128)),
                                   (256, 128, wview(w, (2 * qq + 1) * 128, 128)),
                                   (384, 128, wview(w, 1536 + (2 * qq + 1) * 128, 128))])
                        plan.append(("o_q", g, l, qq))
                    for u in range(2):
                        wq.append([(0, 512, wview(wouto_d[p], u * 512, 512))])
                        plan.append(("out", g, l, u))
        wi = {"i": 0}

        def next_w(kind):
            i = wi["i"]
            assert plan[i][0] == kind, (plan[i], kind)
            wi["i"] += 1
            return w_get(i)

        SC = B[0]
        act(X[0][:, 0:16], CV[:, :], AF.Sigmoid, [CV.r], [X[0].r])
        tt(SC[:, 0:16], X[0][:, 0:16], CV[:, :], ALU.mult, [X[0].r, CV.r], [SC.r])
        scv = SC[:, 0:16].rearrange("p (g k) -> p g k", g=2)
        for l in range(DEPTH):
            pm = PS[l % 2]
            for u in range(6):
                slot = next_w("ada")
                for jj in range(4):
                    j = u * 4 + jj
                    for kc in range(KC):
                        mm(pm[:, 2 * j:2 * j + 2], slot[:, kc, jj * 128:(jj + 1) * 128], scv[:, :, kc],
                           kc == 0, kc == KC - 1, [slot.r, SC.r], [pm.r])
            pv = pm[:, 0:48].rearrange("p (j g) -> p g j", g=2)
            for g in range(2):
                o0 = (l * 2 + g) * 24
                tt(MOD[:, o0:o0 + 24], pv[:, g, :], par(f"bada{l}", 0, 24), ALU.add, [pm.r, PAR.r], [MOD.r])

        def rms_and_modulate(g, l):
            o0 = (l * 2 + g) * 24
            stt(GSC[:, :], MOD[:, o0 + 8:o0 + 16], 1.0, par(f"normg{l}", 0, 8), ALU.add, ALU.mult,
                [MOD.r, PAR.r], [GSC.r])
            ts(GSC[:, :], GSC[:, :], 32.0, None, ALU.mult, None, [GSC.r], [GSC.r])
            sqs = [B[15], B[10]]
            for tb in range(2):
                tsl = slice(tb * 512, (tb + 1) * 512)
                pm = PS[tb]
                for kc in range(KC):
                    sq = sqs[kc % 2]
                    act(sq[:, tsl], XT[:, kc, tsl], AF.Square, [XT.r], [sq.r])
                    mm(pm[:, :], ONES, sq[:, tsl], kc == 0, kc == KC - 1, [CB.r, sq.r], [pm.r], inc=True)
                rsqrt_from(RSTD, RSTD[:, tsl], pm[:, :], [pm.r, SMALL.r], 1.0, DEPSC)
            for kc in range(KC):
                tmp = X[11] if kc % 2 == 0 else X[10]
                tt(tmp[:, :], XT[:, kc, :], RSTD[:, :], ALU.mult, [XT.r, RSTD.r], [tmp.r])
                act(HT[:, kc, :], tmp[:, :], AF.Identity, [tmp.r, GSC.r, MOD.r], [HT.r],
                    bias=MOD[:, o0 + kc:o0 + kc + 1], scale=GSC[:, kc:kc + 1])

        def proj_row(slot, c0, pm_pair):
            for tb in range(2):
                pm = pm_pair[tb]
                for kc in range(KC):
                    mm(pm[:, :], slot[:, kc, c0:c0 + 128], HT[:, kc, tb * 512:(tb + 1) * 512],
                       kc == 0, kc == KC - 1, [slot.r, HT.r], [pm.r])

        def out_proj(g, l):
            o0 = (l * 2 + g) * 24
            k = 0
            for u in range(2):
                slot = next_w("out")
                for dj in range(4):
                    dc = u * 4 + dj
                    for tb in range(2):
                        pm = PS[k % 2]
                        k += 1
                        tsl = slice(tb * 512, (tb + 1) * 512)
                        for fc in range(KC):
                            mm(pm[:, :], slot[:, fc, dj * 128:(dj + 1) * 128], YT[:, fc, tsl],
                               fc == 0, fc == KC - 1, [slot.r, YT.r], [pm.r])
                        otmp = X[10 + (k % 2)]
                        cp(otmp[:, 0:512], pm[:, :], [pm.r], [otmp.r], e="act")
                        stt(XT[:, dc, tsl], otmp[:, 0:512], MOD[:, o0 + 16 + dc:o0 + 17 + dc], XT[:, dc, tsl],
                            ALU.mult, ALU.add, [otmp.r, MOD.r, XT.r], [XT.r])

        def even_layer(g, l):
            p = l // 2
            nseq = 4 if g == 0 else 1
            slen = NT // nseq
            cps = slen // CH
            ck(0)
            mu = par(f"mu{p}", 0, 14)
            OM = DER[:, 0:14]
            HM = DER[:, 14:28]
            ts(OM, mu, -1.0, 1.0, ALU.mult, ALU.add, [PAR.r], [DER.r])
            ts(HM, mu, 0.5, None, ALU.mult, None, [PAR.r], [DER.r])
            OKA = DER[:, 28:32]
            ts(OKA, par(f"ka{p}", 0, 4), -1.0, 1.0, ALU.mult, ALU.add, [PAR.r], [DER.r])
            HRK = DER[:, 32:36]
            ts(HRK, par(f"rk{p}", 0, 4), 0.5, None, ALU.mult, None, [PAR.r], [DER.r])
            LW2 = B[14]
            S.dma("sp", X[0][:, 0:512], lw2_d[p], X[0].r, True)
            S.dma("sp", X[0][:, 512:1024], la2_d[p], X[0].r, True)
            cp(LW2[:, :], X[0][:, :], [X[0].r], [LW2.r])

            def shift_row(pm_pair, row, out_t, out_ap, fin=None, Fr=None):
                Fr = X[10] if Fr is None else Fr
                FE = X[11] if fin is not None else out_t
                FEap = X[11][:, :] if fin is not None else out_ap
                for tb in range(2):
                    tsl = slice(tb * 512, (tb + 1) * 512)
                    cp(Fr[:, tsl], pm_pair[tb][:, :], [pm_pair[tb].r], [Fr.r], e="act")
                    ts(FEap[:, tsl], Fr[:, tsl], OM[:, row:row + 1], None, ALU.mult, None,
                       [Fr.r, DER.r], [FE.r])
                F3 = v3(Fr[:, :], nseq)
                E3 = v3(FEap, nseq)
                stt(E3[:, :, 1:slen], F3[:, :, 0:slen - 1], HM[:, row:row + 1], E3[:, :, 1:slen], ALU.mult, ALU.add,
                    [Fr.r, DER.r, FE.r], [FE.r])
                stt(E3[:, :, 0:slen - 1], F3[:, :, 1:slen], HM[:, row:row + 1], E3[:, :, 0:slen - 1], ALU.mult, ALU.add,
                    [Fr.r, DER.r, FE.r], [FE.r])
                if fin is not None:
                    fin(FE)

            ck('a')
            slot = next_w("e_lora")
            LWT = B[12]
            LAT = B[13]
            proj_row(slot, 0, PS[0:2])
            ck('b')
            shift_row(PS[0:2], 12, None, None,
                      fin=lambda FE: act(LWT[:, :], FE[:, :], AF.Tanh, [FE.r], [LWT.r]))
            proj_row(slot, 128, PS[0:2])
            shift_row(PS[0:2], 13, None, None, fin=lambda FE: cp(LAT[:, :], FE[:, :], [FE.r], [LAT.r]))

            ck(1)
            for hp in range(4):
                slot = next_w("e_hp")
                R, K, V = X[0], X[1], X[2]
                proj_row(slot, 0, PS[0:2])
                proj_row(slot, 128, PS[2:4])
                shift_row(PS[0:2], hp, R, R[:, :], Fr=X[10])
                proj_row(slot, 256, PS[0:2])
                shift_row(PS[2:4], 4 + hp, K, K[:, :], Fr=X[11])
                GA = B[11]
                proj_row(slot, 384, PS[2:4])
                shift_row(PS[0:2], 8 + hp, V, V[:, :], Fr=X[10])
                for tb in range(2):
                    tsl = slice(tb * 512, (tb + 1) * 512)
                    act(X[11][:, tsl], PS[2 + tb][:, :], AF.Sigmoid, [PS[2 + tb].r], [X[11].r])
                    tt(GA[:, tsl], PS[2 + tb][:, :], X[11][:, tsl], ALU.mult, [PS[2 + tb].r, X[11].r], [GA.r])
                ck(2)
                KKN = X[3]
                ts(KKN[:, :], K[:, :], par(f"kk{p}", hp, 1), None, ALU.mult, None, [K.r, PAR.r], [KKN.r])
                sq = B[15]
                act(sq[:, :], KKN[:, :], AF.Square, [KKN.r], [sq.r])
                for tb in range(2):
                    tsl = slice(tb * 512, (tb + 1) * 512)
                    mm(PS[tb][:, :], BD1, sq[:, tsl], True, True, [CB.r, sq.r], [PS[tb].r])
                    act(X[10][:, tsl], PS[tb][:, :], AF.Sqrt, [PS[tb].r], [X[10].r])
                    ts(X[10][:, tsl], X[10][:, tsl], 1e-12, None, ALU.max, None, [X[10].r], [X[10].r])
                recip(X[10][:, :], X[10][:, :], [X[10].r], [X[10].r])
                tt(KKN[:, :], KKN[:, :], X[10][:, :], ALU.mult, [KKN.r, X[10].r], [KKN.r])
                Vb = B[10]
                cp(Vb[:, :], V[:, :], [V.r], [Vb.r])
                Vt = B[9]

                def to_tok(src, dst):
                    for half in range(2):
                        pm = PS[2 + half]
                        for cc in range(8):
                            c = half * 8 + cc
                            for hh in range(2):
                                sl = slice(hh * 64, hh * 64 + 64)
                                mm(pm[sl, cc * 64:(cc + 1) * 64], src[sl, c * 64:(c + 1) * 64],
                                   IDB[sl, hh * 64:hh * 64 + 64], True, True, [src.r, CB.r], [pm.r],
                                   tp=(hh * 64, hh * 64), inc=(cc == 7 and hh == 1))
                        cp(dst[:, half * 512:(half + 1) * 512], pm[:, :], [pm.r], [dst.r], e="act")
                ck(3)
                to_tok(Vb, Vt)
                ck(4)

                KS = X[4]
                YA = X[5]
                S.op("dve", lambda en: en.memset(YA[:, :], 0.0), [], [YA.r])
                SETS = [dict(RT=B[0], KK=B[3], KTt=B[4], BTt=B[5], AkT=B[6], QbT=B[7], QkT=B[8], NIT=B[15]),
                        dict(RT=B[16], KK=B[17], KTt=B[18], BTt=B[19], AkT=B[20], QbT=B[10], QkT=B[21], NIT=B[22])]
                SEQB = [dict(T32=TST, TBF=[mk_tb[0], mk_tb[1]], XS=mk_xs[0], NU=mk_xs[1], TW=X[10], PTS=X[11],
                             banks=(PS[2], PS[3], PS[5], PS[6])),
                        dict(T32=TST1, TBF=[mk_tb[2], mk_tb[3]], XS=mk_xs[2], NU=mk_xs[3], TW=X[8], PTS=X[9],
                             banks=(PS[4], PS[7], PS[0], PS[1]))]

                def prep_local(d):
                    st = SETS[d]
                    SG, L, E1, E2, A_, TMP = X[6], X[7], X[8], X[9], X[10], X[11]
                    dsl = slice(d * 64, d * 64 + 64)
                    for tb in range(2):
                        tsl = slice(tb * 512, (tb + 1) * 512)
                        mm(PS[tb][:, :], LW2[dsl, hp * 128:(hp + 1) * 128], LWT[dsl, tsl], True, True,
                           [LW2.r, LWT.r], [PS[tb].r], tp=(d * 64, 0))
                        act(SG[:, tsl], PS[tb][:, :], AF.Sigmoid, [PS[tb].r, PAR.r], [SG.r],
                            bias=par(f"w0{p}", d * 4 + hp, 1))
                    if d == 0:
                        S.op("dve", lambda en: en.tensor_tensor_scan(out=L[:, :], data0=RESF, data1=SG[:, :],
                                                                     initial=0.0, op0=ALU.mult, op1=ALU.add),
                             [CF.r, SG.r], [L.r])
                    else:
                        S.op("dve", lambda en: en.tensor_tensor_scan(out=L[:, ::-1], data0=RESB[:, ::-1],
                                                                     data1=SG[:, ::-1], initial=0.0,
                                                                     op0=ALU.mult, op1=ALU.add),
                             [CF.r, SG.r], [L.r])
                    for tb in range(2):
                        tsl = slice(tb * 512, (tb + 1) * 512)
                        mm(PS[tb][:, :], LW2[dsl, 512 + hp * 128:512 + (hp + 1) * 128], LAT[dsl, tsl], True, True,
                           [LW2.r, LAT.r], [PS[tb].r], tp=(d * 64, 0))
                        act(A_[:, tsl], PS[tb][:, :], AF.Sigmoid, [PS[tb].r, PAR.r], [A_.r],
                            bias=par(f"a0{p}", d * 4 + hp, 1))
                    RTd, KTd, BTd, KKTd = st["RT"], B[1], B[2], st["KK"]
                    act(E1[:, :], L[:, :], AF.Exp, [L.r], [E1.r], scale=-C0)
                    tt(RTd[:, :], R[:, :], E1[:, :], ALU.mult, [R.r, E1.r], [RTd.r])
                    WC = SMALL[:, 16 + d * 16:32 + d * 16]
                    e3 = E1[:, :].rearrange("p (c t) -> p c t", t=CH)
                    cp(WC, e3[:, :, CH - 1] if d == 0 else e3[:, :, 0], [E1.r], [SMALL.r])
                    act(E2[:, :], L[:, :], AF.Exp, [L.r], [E2.r], scale=C0)
                    ts(TMP[:, :], A_[:, :], par(f"ka{p}", hp, 1), OKA[:, hp:hp + 1], ALU.mult, ALU.add,
                       [A_.r, PAR.r, DER.r], [TMP.r])
                    tt(TMP[:, :], TMP[:, :], K[:, :], ALU.mult, [TMP.r, K.r], [TMP.r])
                    if d == 0:
                        cp(KS[:, :], TMP[:, :], [TMP.r], [KS.r])
                    else:
                        tt(KS[:, :], KS[:, :], TMP[:, :], ALU.add, [KS.r, TMP.r], [KS.r])
                    tt(KTd[:, :], TMP[:, :], E2[:, :], ALU.mult, [TMP.r, E2.r], [KTd.r])
                    tt(TMP[:, :], A_[:, :], KKN[:, :], ALU.mult, [A_.r, KKN.r], [TMP.r])
                    stt(BTd[:, :], TMP[:, :], -1.0, E2[:, :], ALU.mult, ALU.mult, [TMP.r, E2.r], [BTd.r])
                    tt(TMP[:, :], L[:, :], SG[:, :], ALU.subtract, [L.r, SG.r], [TMP.r])
                    act(TMP[:, :], TMP[:, :], AF.Exp, [TMP.r], [TMP.r], scale=-C0)
                    tt(KKTd[:, :], KKN[:, :], TMP[:, :], ALU.mult, [KKN.r, TMP.r], [KKTd.r])
                    KTt, BTt = st["KTt"], st["BTt"]
                    to_tok(KTd, KTt)
                    to_tok(BTd, BTt)
                    AkT, QbT, QkT, NIT = st["AkT"], st["QbT"], st["QkT"], st["NIT"]
                    mA, mB, mI = (MSL, MSU, MIU) if d == 0 else (MSU, MSL, MIL)
                    if g == 0:
                        A0, B0, N0 = X[6], X[7], X[8]
                        A1, B1 = X[9], X[10]
                    elif d == 1:
                        vws = []
                        for xt_ in (X[6], X[7], X[8]):
                            ab = xt_[:, :].bitcast(BF16)
                            vws.append(View(ab[:, 0:1024], xt_.r))
                            vws.append(View(ab[:, 1024:2048], xt_.r))
                        A0, B0, N0, A1, B1 = vws[0], vws[2], vws[4], vws[1], vws[3]
                    else:
                        A0, B0, N0 = B[16], B[17], B[18]
                        A1, B1 = B[19], B[20]

                    def scores(lh, rh, pm, half):
                        for cc in range(8):
                            c = half * 8 + cc
                            for hh in range(2):
                                sl = slice(hh * 64, hh * 64 + 64)
                                mm(pm[sl, cc * 64:(cc + 1) * 64], lh[sl, c * 64:(c + 1) * 64],
                                   rh[sl, c * 64:(c + 1) * 64], True, True, [lh.r, rh.r], [pm.r],
                                   tp=(hh * 64, hh * 64), inc=(cc == 7 and hh == 1))
                    for half in range(2):
                        hs = slice(half * 512, (half + 1) * 512)
                        scores(KKTd, BTd, PS[2], half)
                        tt(A0[:, hs], PS[2][:, :], mA, ALU.mult, [PS[2].r, CF.r], [A0.r])
                        scores(BTd, KKTd, PS[3], half)
                        tt(B0[:, hs], PS[3][:, :], mB, ALU.mult, [PS[3].r, CF.r], [B0.r])
                        tt(N0[:, hs], IDN8, B0[:, hs], ALU.add, [CF.r, B0.r], [N0.r])
                        scores(KTd, KKTd, PS[4], half)
                        tt(AkT[:, hs], PS[4][:, :], mB, ALU.mult, [PS[4].r, CF.r], [AkT.r])
                        scores(BTd, RTd, PS[5], half)
                        tt(QbT[:, hs], PS[5][:, :], mI, ALU.mult, [PS[5].r, CF.r], [QbT.r])
                        scores(KTd, RTd, PS[6], half)
                        tt(QkT[:, hs], PS[6][:, :], mI, ALU.mult, [PS[6].r, CF.r], [QkT.r])
                    Ap, Bp, An, Bn = A0, B0, A1, B1
                    for lev in range(1, 6):
                        for half in range(2):
                            hs = slice(half * 512, (half + 1) * 512)
                            pa = PS[2] if half == 0 else PS[5]
                            pb = PS[3] if half == 0 else PS[6]
                            pn = PS[4]
                            for cc in range(8):
                                c = half * 8 + cc
                                cs = slice(c * 64, (c + 1) * 64)
                                for hh in range(2):
                                    sl = slice(hh * 64, hh * 64 + 64)
                                    mm(pa[sl, cc * 64:(cc + 1) * 64], Bp[sl, cs], Ap[sl, cs], True, True,
                                       [Ap.r, Bp.r], [pa.r], tp=(hh * 64, hh * 64), inc=(cc == 7 and hh == 1))
                            cp(An[:, hs], pa[:, :], [pa.r], [An.r], e="act")
                            if lev < 5:
                                for cc in range(8):
                                    c = half * 8 + cc
                                    cs = slice(c * 64, (c + 1) * 64)
                                    for hh in range(2):
                                        sl = slice(hh * 64, hh * 64 + 64)
                                        mm(pb[sl, cc * 64:(cc + 1) * 64], Ap[sl, cs], Bp[sl, cs], True, True,
                                           [Ap.r, Bp.r], [pb.r], tp=(hh * 64, hh * 64), inc=(cc == 7 and hh == 1))
                                cp(Bn[:, hs], pb[:, :], [pb.r], [Bn.r], e="act")
                            for cc in range(8):
                                c = half * 8 + cc
                                cs = slice(c * 64, (c + 1) * 64)
                                for hh in range(2):
                                    sl = slice(hh * 64, hh * 64 + 64)
                                    mm(pn[sl, cc * 64:(cc + 1) * 64], An[sl, cs], N0[sl, cs], True, True,
                                       [An.r, N0.r], [pn.r], tp=(hh * 64, hh * 64), inc=(cc == 7 and hh == 1))
                            tt(N0[:, hs], pn[:, :], N0[:, hs], ALU.add, [pn.r, N0.r], [N0.r])
                        Ap, Bp, An, Bn = An, Bn, Ap, Bp
                    cp(NIT[:, :], N0[:, :], [N0.r], [NIT.r])

                def init_state(d):
                    sb_ = SEQB[d]
                    T32 = sb_["T32"]
                    if g == 0:
                        S.op("dve", lambda en: en.memset(T32[:, 0:nseq * 64], 0.0), [], [T32.r])
                    else:
                        for hh in range(2):
                            S.dma("sp", TST[hh * 64:hh * 64 + 64, 256:320], st_d[p, d, hp * 2 + hh], TST.r, True)
                        for hh in range(2):
                            sl = slice(hh * 64, hh * 64 + 64)
                            mm(PS[4][sl, 0:64], TST[sl, 256:320], CF[sl, hh * 64:hh * 64 + 64], True, True,
                               [TST.r, CF.r], [PS[4].r], tp=(hh * 64, hh * 64))
                        cp(T32[:, 0:64], PS[4][:, 0:64], [PS[4].r], [T32.r])
                    cp(sb_["TBF"][0][:, 0:nseq * 64], T32[:, 0:nseq * 64], [T32.r], [sb_["TBF"][0].r])

                def seq(d):
                    st, sb_ = SETS[d], SEQB[d]
                    RTd, KKTd, KTt, BTt = st["RT"], st["KK"], st["KTt"], st["BTt"]
                    AkT, QbT, QkT, NIT = st["AkT"], st["QbT"], st["QkT"], st["NIT"]
                    T32, TBF, XS, NU, TW, PTS = sb_["T32"], sb_["TBF"], sb_["XS"], sb_["NU"], sb_["TW"], sb_["PTS"]
                    px, pu, pt, py = sb_["banks"]
                    WC = SMALL[:, 16 + d * 16:32 + d * 16]
                    ya3 = YA[:, :].rearrange("p (q j t) -> p q j t", q=nseq, j=cps)
                    for j in range(cps):
                        cj = j if d == 0 else cps - 1 - j
                        tcur, tnxt = TBF[j % 2], TBF[(j + 1) % 2]
                        for q in range(nseq):
                            c = q * cps + cj
                            cs = slice(c * 64, (c + 1) * 64)
                            qs = slice(q * 64, (q + 1) * 64)
                            for hh in range(2):
                                sl = slice(hh * 64, hh * 64 + 64)
                                tp = (hh * 64, hh * 64)
                                mm(px[sl, qs], KKTd[sl, cs], tcur[sl, qs], True, False, [KKTd.r, tcur.r], [px.r], tp=tp)
                                mm(px[sl, qs], AkT[sl, cs], Vt[sl, cs], False, True, [AkT.r, Vt.r], [px.r], tp=tp,
                                   inc=(q == nseq - 1 and hh == 1))
                        cp(XS[:, 0:nseq * 64], px[:, 0:nseq * 64], [px.r], [XS.r])
                        for q in range(nseq):
                            c = q * cps + cj
                            cs = slice(c * 64, (c + 1) * 64)
                            qs = slice(q * 64, (q + 1) * 64)
                            for hh in range(2):
                                sl = slice(hh * 64, hh * 64 + 64)
                                mm(pu[sl, qs], NIT[sl, cs], XS[sl, qs], True, True, [NIT.r, XS.r], [pu.r],
                                   tp=(hh * 64, hh * 64), inc=(q == nseq - 1 and hh == 1))
                        cp(NU[:, 0:nseq * 64], pu[:, 0:nseq * 64], [pu.r], [NU.r])
                        for q in range(nseq):
                            c = q * cps + cj
                            cs = slice(c * 64, (c + 1) * 64)
                            qs = slice(q * 64, (q + 1) * 64)
                            for hh in range(2):
                                sl = slice(hh * 64, hh * 64 + 64)
                                tp = (hh * 64, hh * 64)
                                mm(pt[sl, qs], BTt[sl, cs], NU[sl, qs], True, False, [BTt.r, NU.r], [pt.r], tp=tp)
                                mm(pt[sl, qs], KTt[sl, cs], Vt[sl, cs], False, True, [KTt.r, Vt.r], [pt.r], tp=tp,
                                   inc=False)
                                mm(py[sl, qs], tcur[sl, qs], RTd[sl, cs], True, False, [tcur.r, RTd.r], [py.r], tp=tp)
                                mm(py[sl, qs], NU[sl, qs], QbT[sl, cs], False, False, [NU.r, QbT.r], [py.r], tp=tp)
                                mm(py[sl, qs], Vt[sl, cs], QkT[sl, cs], False, True, [Vt.r, QkT.r], [py.r], tp=tp,
                                   inc=(q == nseq - 1 and hh == 1))
                        pyv = py[:, 0:nseq * 64].rearrange("p (q t) -> p q t", q=nseq)
                        tt(ya3[:, :, cj, :], pyv, ya3[:, :, cj, :], ALU.add, [py.r, YA.r], [YA.r])
                        cp(PTS[:, 0:nseq * 64], pt[:, 0:nseq * 64], [pt.r], [PTS.r])
                        for q in range(nseq):
                            c = q * cps + cj
                            qs = slice(q * 64, (q + 1) * 64)
                            ts(TW[:, qs], T32[:, qs], WC[:, c:c + 1], None, ALU.mult, None, [T32.r, SMALL.r], [TW.r])
                            stt(tnxt[:, qs], PTS[:, qs], WC[:, c:c + 1], TW[:, qs], ALU.mult, ALU.add,
                                [PTS.r, SMALL.r, TW.r], [tnxt.r])
                            stt(T32[:, qs], PTS[:, qs], WC[:, c:c + 1], TW[:, qs], ALU.mult, ALU.add,
                                [PTS.r, SMALL.r, TW.r], [T32.r])

                for d in range(2):
                    prep_local(d)
                    init_state(d)
                qf = S.record(lambda: seq(0))
                qb_ = S.record(lambda: seq(1))
                S.replay(qf, qb_)
                if g == 0:
                    for d in range(2):
                        T32 = SEQB[d]["T32"]
                        for q in range(nseq):
                            for hh in range(2):
                                sl = slice(hh * 64, hh * 64 + 64)
                                mm(PS[4][sl, q * 64:(q + 1) * 64], T32[sl, q * 64:(q + 1) * 64],
                                   CF[sl, hh * 64:hh * 64 + 64], True, True, [T32.r, CF.r], [PS[4].r],
                                   tp=(hh * 64, hh * 64))
                        cp(TST[:, 256:512], PS[4][:, 0:256], [PS[4].r], [TST.r])
                        for q in range(nseq):
                            for hh in range(2):
                                S.dma("sp", ns_d[q, p, d, hp * 2 + hh],
                                      TST[hh * 64:hh * 64 + 64, 256 + q * 64:256 + (q + 1) * 64], TST.r, False)
                ck(8)
                Yb = B[0]
                cp(Yb[:, :], YA[:, :], [YA.r], [Yb.r], e="act")
                DD, RS, BON = X[6], X[7], X[8]
                sq = B[1]
                for tb in range(2):
                    tsl = slice(tb * 512, (tb + 1) * 512)
                    mm(PS[tb][:, :], BDM, Yb[:, tsl], True, True, [CB.r, Yb.r], [PS[tb].r])
                    tt(DD[:, tsl], YA[:, tsl], PS[tb][:, :], ALU.subtract, [YA.r, PS[tb].r], [DD.r])
                act(sq[:, :], DD[:, :], AF.Square, [DD.r], [sq.r])
                for tb in range(2):
                    tsl = slice(tb * 512, (tb + 1) * 512)
                    mm(PS[tb][:, :], BDM, sq[:, tsl], True, True, [CB.r, sq.r], [PS[tb].r])
                    rsqrt_from(RS, RS[:, tsl], PS[tb][:, :], [PS[tb].r, SMALL.r], 1.0, GNEPSC)
                tt(DD[:, :], DD[:, :], RS[:, :], ALU.mult, [DD.r, RS.r], [DD.r])
                ts(DD[:, :], DD[:, :], par(f"lng{p}", hp, 1), par(f"lnb{p}", hp, 1), ALU.mult, ALU.add,
                   [DD.r, PAR.r], [DD.r])
                rkb = B[2]
                stt(rkb[:, :], KS[:, :], HRK[:, hp:hp + 1], R[:, :], ALU.mult, ALU.mult, [KS.r, DER.r, R.r], [rkb.r])
                for tb in range(2):
                    tsl = slice(tb * 512, (tb + 1) * 512)
                    mm(PS[tb][:, :], BD1, rkb[:, tsl], True, True, [CB.r, rkb.r], [PS[tb].r])
                    tt(BON[:, tsl], PS[tb][:, :], V[:, tsl], ALU.mult, [PS[tb].r, V.r], [BON.r])
                tt(DD[:, :], DD[:, :], BON[:, :], ALU.add, [DD.r, BON.r], [DD.r])
                tt(YT[:, hp, :], DD[:, :], GA[:, :], ALU.mult, [DD.r, GA.r], [YT.r])

            ck(9)
            for cc in range(4):
                slot = next_w("e_b")
                BG, CG, U_, CO, SGm = X[0], X[1], X[2], X[3], X[4]
                proj_row(slot, 0, PS[0:2])
                proj_row(slot, 128, PS[2:4])
                for tb in range(2):
                    cp(BG[:, tb * 512:(tb + 1) * 512], PS[tb][:, :], [PS[tb].r], [BG.r], e="act")
                proj_row(slot, 256, PS[0:2])
                for tb in range(2):
                    cp(CG[:, tb * 512:(tb + 1) * 512], PS[2 + tb][:, :], [PS[2 + tb].r], [CG.r], e="act")
                proj_row(slot, 384, PS[2:4])
                for tb in range(2):
                    tsl = slice(tb * 512, (tb + 1) * 512)
                    tt(U_[:, tsl], PS[tb][:, :], CG[:, tsl], ALU.mult, [PS[tb].r, CG.r], [U_.r])
                cw = lambda tap: par(f"cw{p}", tap * 4 + cc, 1)
                ts(CO[:, :], U_[:, :], cw(1), par(f"cb{p}", cc, 1), ALU.mult, ALU.add, [U_.r, PAR.r], [CO.r])
                U3 = v3(U_[:, :], nseq)
                C3 = v3(CO[:, :], nseq)
                stt(C3[:, :, 1:slen], U3[:, :, 0:slen - 1], cw(0), C3[:, :, 1:slen], ALU.mult, ALU.add,
                    [U_.r, PAR.r, CO.r], [CO.r])
                stt(C3[:, :, 0:slen - 1], U3[:, :, 1:slen], cw(2), C3[:, :, 0:slen - 1], ALU.mult, ALU.add,
                    [U_.r, PAR.r, CO.r], [CO.r])
                tt(CO[:, :], CO[:, :], BG[:, :], ALU.mult, [CO.r, BG.r], [CO.r])
                for tb in range(2):
                    tsl = slice(tb * 512, (tb + 1) * 512)
                    act(SGm[:, tsl], PS[2 + tb][:, :], AF.Sigmoid, [PS[2 + tb].r], [SGm.r])
                    tt(SGm[:, tsl], PS[2 + tb][:, :], SGm[:, tsl], ALU.mult, [PS[2 + tb].r, SGm.r], [SGm.r])
                tt(YT[:, 4 + cc, :], CO[:, :], SGm[:, :], ALU.mult, [CO.r, SGm.r], [YT.r])

        def odd_layer(g, l):
            p = l // 2
            nkeys = 256 if g == 0 else 1536
            nkb = nkeys // 128
            KTa = [B[0], B[1], B[2], B[3]] if g == 0 else None
            if g == 0:
                ktile = [(B[j], 0) for j in range(4)]
            else:
                ktile = [(B[2 * j], B[2 * j + 1]) for j in range(4)]
            Vtok = B[8]
            Vtok2 = B[9]
            Vc = B[10]
            slot = next_w("o_k")

            def qk_norm_row(pm_pair, gname, out_t, out_ap, rope):
                Q = X[4]
                sq = B[15]
                RS = X[5]
                for tb in range(2):
                    tsl = slice(tb * 512, (tb + 1) * 512)
                    cp(Q[:, tsl], pm_pair[tb][:, :], [pm_pair[tb].r], [Q.r], e="act")
                    act(sq[:, tsl], pm_pair[tb][:, :], AF.Square, [pm_pair[tb].r], [sq.r])
                for tb in range(2):
                    tsl = slice(tb * 512, (tb + 1) * 512)
                    mm(pm_pair[tb][:, :], BDM, sq[:, tsl], True, True, [CB.r, sq.r], [pm_pair[tb].r])
                    rsqrt_from(RS, RS[:, tsl], pm_pair[tb][:, :], [pm_pair[tb].r, SMALL.r], 1.0, EPSC)
                if not rope:
                    stt(out_ap, Q[:, :], par(gname, 0, 1), RS[:, :], ALU.mult, ALU.mult, [Q.r, PAR.r, RS.r], [out_t.r])
                    return
                QN, QNb, T1 = X[6], B[15], X[7]
                stt(QN[:, :], Q[:, :], par(gname, 0, 1), RS[:, :], ALU.mult, ALU.mult, [Q.r, PAR.r, RS.r], [QN.r])
                cp(QNb[:, :], QN[:, :], [QN.r], [QNb.r], e="act")
                tt(T1[:, :], QN[:, :], COS, ALU.mult, [QN.r, CF.r], [T1.r])
                for tb in range(2):
                    tsl = slice(tb * 512, (tb + 1) * 512)
                    mm(pm_pair[tb][:, :], ROTT, QNb[:, tsl], True, True, [CB.r, QNb.r], [pm_pair[tb].r])
                    tt(QN[:, tsl], pm_pair[tb][:, :], SIN[:, tsl],
                       ALU.mult, [pm_pair[tb].r, CF.r], [QN.r])
                tt(out_ap, T1[:, :], QN[:, :], ALU.add, [T1.r, QN.r], [out_t.r])

            KN32 = X[8]
            for j in range(4):
                proj_row(slot, j * 128, PS[0:2])
                kt = ktile[j][0]
                if g == 0:
                    qk_norm_row(PS[0:2], f"kg{p}", KN32, KN32[:, :], False)
                    cp(kt[:, :], KN32[:, :], [KN32.r], [kt.r], e="act")
                    for blk in range(8):
                        tr(PS[2 + blk // 4][:, (blk % 4) * 128:(blk % 4 + 1) * 128],
                           KN32[:, blk * 128:(blk + 1) * 128], ident, [KN32.r, CF.r], [PS[2 + blk // 4].r])
                    OK_ = X[9]
                    for hf in range(2):
                        cp(OK_[:, hf * 512:(hf + 1) * 512], PS[2 + hf][:, :], [PS[2 + hf].r], [OK_.r])
                    for q in range(4):
                        src = OK_[:, q * 256:(q + 1) * 256].rearrange("p (b c) -> p b c", b=2)[:, :, 0:64]
                        dst = nk_d[q, p, j].rearrange("(b t) c -> t b c", b=2)
                        S.dma("sp", dst, src, OK_.r, False)
                else:
                    qk_norm_row(PS[0:2], f"kg{p}", kt, kt[:, :], True)
                    kc_t = ktile[j][1]
                    st = X[9]
                    for h2 in range(2):
                        S.dma("sp", st[:, 0:512].rearrange("p (b c) -> p b c", b=4)[:, :, h2 * 64:(h2 + 1) * 64],
                              ck_d[p, j].rearrange("(b t) c -> t b c", b=4), st.r, True)
                    for blk in range(4):
                        tr(PS[2][:, blk * 128:(blk + 1) * 128], st[:, blk * 128:(blk + 1) * 128], ident,
                           [st.r, CF.r], [PS[2].r])
                    cp(kc_t[:, 0:512], PS[2][:, :], [PS[2].r], [kc_t.r], e="act")
            ck('o2')
            slot = next_w("o_v")
            VO = X[10]
            for blk in range(8):
                pm = PS[blk % 2]
                for kc in range(KC):
                    mm(pm[:, 0:256], HT[:, kc, blk * 128:(blk + 1) * 128], slot[:, kc, 0:256], kc == 0, kc == KC - 1,
                       [HT.r, slot.r], [pm.r])
                vdst = (Vtok if blk < 4 else Vtok2)
                if g == 1:
                    cp(vdst[:, (blk % 4) * 256:(blk % 4 + 1) * 256], pm[:, 0:256], [pm.r], [vdst.r], e="act")
                if g == 0:
                    cp(VO[:, (blk % 4) * 256:(blk % 4 + 1) * 256], pm[:, 0:256], [pm.r], [VO.r])
                    cp(vdst[:, (blk % 4) * 256:(blk % 4 + 1) * 256], VO[:, (blk % 4) * 256:(blk % 4 + 1) * 256],
                       [VO.r], [vdst.r], e="act")
                    if blk % 4 == 3:
                        for qq in range(2):
                            q = (blk // 4) * 2 + qq
                            src = VO[:, qq * 512:(qq + 1) * 512].rearrange("p (b h c) -> p b h c", b=2, h=4)
                            for hh in range(4):
                                dst = nv_d[q, p, hh].rearrange("(b t) c -> t b c", b=2)
                                S.dma("sp", dst, src[:, :, hh, :], VO.r, False)
            if g == 1:
                st = X[9]
                for hh in range(4):
                    S.dma("sp", st[:, :].rearrange("p (b h c) -> p b h c", b=4, h=4)[:, :, hh, :],
                          cv_d[p, hh].rearrange("(b t) c -> t b c", b=4), st.r, True)
                cp(Vc[:, :], st[:, :], [st.r], [Vc.r])

            ck('o3')

            def vblock(kb, kvh, seq=None):
                if g == 0:
                    blk = seq * 2 + kb
                    tl = Vtok if blk < 4 else Vtok2
                    return tl[:, (blk % 4) * 256 + kvh * 64:(blk % 4) * 256 + kvh * 64 + 64], tl.r
                if kb < 4:
                    return Vc[:, kb * 256 + kvh * 64:kb * 256 + kvh * 64 + 64], Vc.r
                blk = kb - 4
                tl = Vtok if blk < 4 else Vtok2
                return tl[:, (blk % 4) * 256 + kvh * 64:(blk % 4) * 256 + kvh * 64 + 64], tl.r

            def kblock(kb, kvh, sl, seq=None):
                if g == 0:
                    t_ = ktile[kvh][0]
                    c0 = seq * 256 + kb * 128
                    return t_[sl, c0:c0 + 128], t_.r
                if kb < 4:
                    t_ = ktile[kvh][1]
                    return t_[sl, kb * 128:(kb + 1) * 128], t_.r
                t_ = ktile[kvh][0]
                return t_[sl, (kb - 4) * 128:(kb - 3) * 128], t_.r

            QT = B[11]
            GSs = [B[12], B[20]]
            QZs = [[B[16], B[17]], [B[18], B[19]]]
            PTs = [B[13], B[14], B[21]]
            sbank = [PS[2], PS[3], PS[4]]
            po, pd = PS[5], PS[6]
            RD, OA = X[8], X[9]
            slot_h = [None]
            nexp = [0]
            nblk = [0]

            def prep(qr, buf):
                if qr % 2 == 0:
                    slot_h[0] = next_w("o_q")
                slot = slot_h[0]
                r2 = qr % 2
                QZ, GS = QZs[buf], GSs[buf]
                proj_row(slot, r2 * 256, PS[0:2])
                qk_norm_row(PS[0:2], f"qg{p}", QT, QT[:, :], g == 1)
                for hh in range(2):
                    sl = slice(hh * 64, hh * 64 + 64)
                    so = slice((1 - hh) * 64, (1 - hh) * 64 + 64)
                    S.op("dve", lambda en, t_=QZ[hh], so_=so: en.memset(t_[so_, :], 0.0), [], [QZ[hh].r])
                    cp(QZ[hh][sl, :], QT[sl, :], [QT.r, QZ[hh].r], [QZ[hh].r])
                proj_row(slot, r2 * 256 + 128, PS[0:2])
                for tb in range(2):
                    tsl = slice(tb * 512, (tb + 1) * 512)
                    act(X[5][:, tsl], PS[tb][:, :], AF.Sigmoid, [PS[tb].r], [X[5].r])
                    tt(GS[:, tsl], PS[tb][:, :], X[5][:, tsl], ALU.mult, [PS[tb].r, X[5].r], [GS.r])

            def attn(qr, buf):
                kvh = qr // 2
                QZ, GS = QZs[buf], GSs[buf]
                qblocks = [(q, q * 256, 256) for q in range(4)] if g == 0 else [(None, 0, 512), (None, 512, 512)]
                nk = nkb if g == 1 else 2
                units = [(bi, seq, q0, qn, hh, kb) for bi, (seq, q0, qn) in enumerate(qblocks)
                         for hh in range(2) for kb in range(nk)]
                ids = {}

                def emit_score(u):
                    bi, seq, q0, qn, hh, kb = u
                    i = nexp[0]
                    nexp[0] += 1
                    ids[u] = i
                    sb = sbank[i % 3]
                    kap, kreg = kblock(kb, kvh, slice(0, 128), seq)
                    mm(sb[:, 0:qn], kap, QZ[hh][:, q0:q0 + qn], True, True, [kreg, QZ[hh].r], [sb.r])

                def emit_rest(u):
                    bi, seq, q0, qn, hh, kb = u
                    i = ids[u]
                    sb = sbank[i % 3]
                    pt_ = PTs[i % len(PTs)]
                    po = PS[5] if (nblk[0] + bi) % 2 == 0 else PS[7]
                    sl = slice(hh * 64, hh * 64 + 64)
                    act(pt_[:, 0:qn], sb[:, 0:qn], AF.Exp, [sb.r], [pt_.r], scale=0.125)
                    vap, vreg = vblock(kb, kvh, seq)
                    last = kb == nk - 1
                    mm(po[sl, 0:qn], vap, pt_[:, 0:qn], kb == 0, last, [vreg, pt_.r], [po.r], tp=(0, hh * 64))
                    mm(pd[sl, 0:qn], ONES[:, 0:64], pt_[:, 0:qn], kb == 0, last, [CB.r, pt_.r], [pd.r],
                       tp=(0, hh * 64), inc=True)
                    if last and hh == 1:
                        recip(RD[:, 0:qn], pd[:, 0:qn], [pd.r], [RD.r])
                        tt(OA[:, 0:qn], po[:, 0:qn], RD[:, 0:qn], ALU.mult, [po.r, RD.r], [OA.r])
                        tt(YT[:, qr, q0:q0 + qn], OA[:, 0:qn], GS[:, q0:q0 + qn], ALU.mult, [OA.r, GS.r], [YT.r])

                emit_score(units[0])
                if len(units) > 1:
                    emit_score(units[1])
                for k, u in enumerate(units):
                    if k + 2 < len(units):
                        emit_score(units[k + 2])
                    emit_rest(u)
                nblk[0] += len(qblocks)

            S.replay(S.record(lambda: prep(0, 0)))
            for qr in range(8):
                qa = S.record(lambda: attn(qr, qr % 2))
                qb = S.record(lambda: prep(qr + 1, (qr + 1) % 2)) if qr < 7 else []
                S.replay(qa, qb)

        mk_tb = [mk(f"TBF{i}", [128, 256], BF16) for i in range(4)]
        mk_xs = [mk(f"XSb{i}", [128, 256], BF16) for i in range(4)]
        TST1 = mk("TST1", [128, 256], F32)

        for g in range(2):
            xin = xp_d if g == 0 else xs_d
            yout = yp_d if g == 0 else ys_d
            for rnd in range(2):
                for i in range(4):
                    tt_ = rnd * 4 + i
                    S.dma("sp", X[i][:, :], xin[tt_ * 128:(tt_ + 1) * 128, :], X[i].r, True)
                for kc in range(KC):
                    pm = PS[kc % 2]
                    for i in range(4):
                        tr(pm[:, i * 128:(i + 1) * 128], X[i][:, kc * 128:(kc + 1) * 128], ident, [X[i].r, CF.r], [pm.r])
                    cp(XT[:, kc, rnd * 512:(rnd + 1) * 512], pm[:, :], [pm.r], [XT.r], e="act" if kc % 2 else "dve")
            try:
              for l in LAYERS:
                S.barrier()
                rms_and_modulate(g, l)
                if l % 2 == 0:
                    even_layer(g, l)
                else:
                    odd_layer(g, l)
                out_proj(g, l)
                if dbg:
                    S.dma("sp", dbg_d[l, g].rearrange("p (k t) -> p k t", k=KC), XT[:, :, :], XT.r, False)
            except _Stop:
                wi['i'] = len(plan)
                LAYERS = []
            S.barrier()
            sq = B[15]
            for tb in range(2):
                tsl = slice(tb * 512, (tb + 1) * 512)
                pm = PS[tb]
                for kc in range(KC):
                    act(sq[:, tsl], XT[:, kc, tsl], AF.Square, [XT.r], [sq.r])
                    mm(pm[:, :], ONES, sq[:, tsl], kc == 0, kc == KC - 1, [CB.r, sq.r], [pm.r], inc=True)
                rsqrt_from(RSTD, RSTD[:, tsl], pm[:, :], [pm.r, SMALL.r], 1.0, DEPSC)
            ts(GSC[:, :], par("finalg", 0, 8), 32.0, None, ALU.mult, None, [PAR.r], [GSC.r])
            for kc in range(KC):
                stt(XT[:, kc, :], XT[:, kc, :], GSC[:, kc:kc + 1], RSTD[:, :], ALU.mult, ALU.mult,
                    [XT.r, GSC.r, RSTD.r], [XT.r])
            for rnd in range(2):
                for i in range(4):
                    tt_ = rnd * 4 + i
                    for kc4 in range(2):
                        pm = PS[(i * 2 + kc4) % 2]
                        for k2 in range(4):
                            kc = kc4 * 4 + k2
                            tr(pm[:, k2 * 128:(k2 + 1) * 128], XT[:, kc, tt_ * 128:(tt_ + 1) * 128], ident,
                               [XT.r, CF.r], [pm.r])
                        cp(X[i][:, kc4 * 512:(kc4 + 1) * 512], pm[:, :], [pm.r], [X[i].r], e="act" if kc4 else "dve")
                    S.dma("sp", yout[tt_ * 128:(tt_ + 1) * 128, :], X[i][:, :], X[i].r, False)
            S.barrier()

        S.finish([t_.r for t_ in X] + [TST.r, XT.r] + [w_.r for w_ in WS])
        print("instructions emitted:", S.ninstr, {k: v for k, v in S.cnt.items()})
    return nc


def _consts():
    f32 = np.float32
    p = np.arange(128)
    i64 = p % 64
    ident = np.eye(128, dtype=f32)
    col = np.arange(512)
    j64 = col % 64
    idn8 = (i64[:, None] == j64[None, :]).astype(f32)
    t = np.arange(1024)
    inv = 10000.0 ** (-(np.arange(0, 32, 2).astype(np.float64)) / 32.0)
    within = i64 % 32
    f = within % 16
    half = i64 // 32
    pos = np.where(half[:, None] == 0, (t // 64)[None, :], (t % 64)[None, :]).astype(np.float64)
    ang = pos * inv[f][:, None]
    cos = np.cos(ang).astype(f32)
    sin = np.sin(ang).astype(f32)
    cf = np.concatenate([ident, idn8, cos, sin], axis=1).astype(f32)
    ones = np.ones((128, 128), f32)
    bd1 = ((p[:, None] // 64) == (p[None, :] // 64)).astype(f32)
    bdm = bd1 / 64.0
    rott = np.zeros((128, 128), f32)
    for m in range(128):
        if (m % 32) < 16:
            rott[m + 16, m] = -1.0
        else:
            rott[m - 16, m] = 1.0
    msl = (i64[:, None] > j64[None, :]).astype(f32)
    msu = (i64[:, None] < j64[None, :]).astype(f32)
    mil = (i64[:, None] >= j64[None, :]).astype(f32)
    miu = (i64[:, None] <= j64[None, :]).astype(f32)
    resf = np.broadcast_to((t % 64 != 0).astype(f32)[None, :], (128, 1024))
    resb = np.broadcast_to((t % 64 != 63).astype(f32)[None, :], (128, 1024))
    cb = np.concatenate([ones, bd1, bdm, rott, ident, msl, msu, mil, miu, resf, resb], axis=1).astype(f32)
    assert cf.shape[1] == NCF and cb.shape[1] == NCB
    return np.ascontiguousarray(cf), np.ascontiguousarray(cb)


def _pack_params(inp):
    f32 = np.float32
    P = np.zeros((128, NPAR), f32)

    def put(name, arr):
        o0, n = _off[name]
        assert arr.shape == (128, n), (name, arr.shape, n)
        P[:, o0:o0 + n] = arr

    col = lambda v, n: np.asarray(v, f32).reshape(n, 128).T
    for l in range(DEPTH):
        put(f"bada{l}", col(inp["b_ada"][l], 24))
        put(f"normg{l}", col(inp["norm_g"][l], 8))
    put("finalg", col(inp["final_g"], 8))
    for p in range(2):
        put(f"mu{p}", col(inp["mu_shift"][p], 14))
        put(f"w0{p}", np.asarray(inp["w0"][p], f32).reshape(2, 4, 128).transpose(2, 0, 1).reshape(128, 8))
        put(f"a0{p}", np.asarray(inp["a0"][p], f32).reshape(2, 4, 128).transpose(2, 0, 1).reshape(128, 8))
        put(f"kk{p}", col(inp["k_k"][p], 4))
        put(f"ka{p}", col(inp["k_a"][p], 4))
        put(f"rk{p}", col(np.asarray(inp["r_k"][p]).reshape(512), 4))
        put(f"lng{p}", col(inp["lnx_g"][p], 4))
        put(f"lnb{p}", col(inp["lnx_b"][p], 4))
        put(f"cw{p}", np.asarray(inp["conv_w"][p], f32).reshape(3, 4, 128).transpose(2, 0, 1).reshape(128, 12))
        put(f"cb{p}", col(inp["conv_b"][p], 4))
        put(f"qg{p}", np.tile(np.asarray(inp["q_norm_g"][p], f32), 2).reshape(128, 1))
        put(f"kg{p}", np.tile(np.asarray(inp["k_norm_g"][p], f32), 2).reshape(128, 1))
    return P


_DBG = False


def kernel(**inp):
    f32 = np.float32
    inp = {k: np.asarray(v) for k, v in inp.items()}
    nc = build_program(dbg=_DBG)
    cf, cb = _consts()
    params = _pack_params(inp)
    shared = {
        "params": params, "consts_f": cf, "consts_b": cb,
        "w_ada": np.ascontiguousarray(inp["w_ada"], f32),
        "w_in_e": np.ascontiguousarray(inp["w_in_e"], f32),
        "w_out_e": np.ascontiguousarray(inp["w_out_e"], f32),
        "w_in_o": np.ascontiguousarray(inp["w_in_o"], f32),
        "w_out_o": np.ascontiguousarray(inp["w_out_o"], f32),
        "lora_w2": np.ascontiguousarray(inp["lora_w2"], f32).reshape(2, 128, 512),
        "lora_a2": np.ascontiguousarray(inp["lora_a2"], f32).reshape(2, 128, 512),
    }
    in_maps = []
    for i in range(8):
        cv = np.zeros((128, 16), f32)
        cv[:, 0:8] = inp["c_ctx"].astype(f32).reshape(8, 128).T
        cv[:, 8:16] = inp["c"][i].astype(f32).reshape(8, 128).T
        m = dict(shared)
        m["x_prompt"] = np.ascontiguousarray(inp["x_prompt"][4 * i:4 * i + 4], f32).reshape(1024, D)
        m["x_sample"] = np.ascontiguousarray(inp["x_sample"][i], f32)
        m["cvec"] = cv
        m["state"] = np.ascontiguousarray(inp["state_rwkv"][i], f32)
        m["cache_k"] = np.ascontiguousarray(inp["cache_k"][i], f32)
        m["cache_v"] = np.ascontiguousarray(inp["cache_v"][i], f32)
        in_maps.append(m)
    res = run_bass_kernel_spmd(nc, in_maps, core_ids=list(range(8)))
    R = res.results
    y_prompt = np.concatenate([r["y_prompt"].reshape(4, 256, D) for r in R], axis=0)
    y_sample = np.stack([r["y_sample"] for r in R], axis=0)
    new_state = np.concatenate([r["new_state"] for r in R], axis=0)
    new_k = np.concatenate([r["new_k"] for r in R], axis=0)
    new_v = np.concatenate([r["new_v"] for r in R], axis=0)
    if _DBG:
        kernel.dbg = [r["dbg"] for r in R]
    return (y_prompt.astype(f32), y_sample.astype(f32), new_state.astype(f32), new_k.astype(f32), new_v.astype(f32))
```
